# Optimizing a Trainium2 kernel written in Bass

```python
import math
import jax, jax.numpy as jnp
from jax import lax
import numpy as np

D_MODEL = 2048
BATCH = 4
SEQ = 4096
DEPTH = 4

CHUNK = 64
Q_BLOCK = 128
D_MIX = D_MODEL
D_POOL = D_MIX // 2
D_DIFF = D_MIX - D_POOL
POOL_WINDOWS = (2, 4, 8, 16)
N_POOL_GROUPS = len(POOL_WINDOWS)
POOL_GROUP_DIM = D_POOL // N_POOL_GROUPS
DIFF_HEADS = 8
DIFF_V_DIM = D_DIFF // DIFF_HEADS
DIFF_QK_DIM = DIFF_V_DIM // 2
D_QK = DIFF_HEADS * 2 * DIFF_QK_DIM
D_IN = 2 * D_POOL + 2 * D_QK + 2 * D_DIFF
ROPE_THETA = 10000.0
EPS = 1e-6
NEG_INF = -1e30

kernel_name = "hybrid_pool_diffattn_sandwich_adaln"


def rms_norm(x, g):
    xf = x.astype(jnp.float32)
    y = xf * lax.rsqrt(jnp.mean(xf * xf, axis=-1, keepdims=True) + EPS)
    return (y * g.astype(jnp.float32)).astype(x.dtype)


def rope_tables(positions):
    inv_freq = 1.0 / (ROPE_THETA ** (jnp.arange(0, DIFF_QK_DIM, 2, dtype=jnp.float32) / DIFF_QK_DIM))
    ang = positions.astype(jnp.float32)[..., None] * inv_freq
    emb = jnp.concatenate([ang, ang], axis=-1)[:, :, None, None, :]
    return jnp.cos(emb), jnp.sin(emb)


def apply_rope(x, cos, sin):
    xf = x.astype(jnp.float32)
    half = DIFF_QK_DIM // 2
    rot = jnp.concatenate([-xf[..., half:], xf[..., :half]], axis=-1)
    return (xf * cos + rot * sin).astype(x.dtype)


def multiscale_pool(u, w_pool, pool_scale):
    B, S, _ = u.shape
    ug = u.reshape(B, S, N_POOL_GROUPS, POOL_GROUP_DIM).astype(jnp.float32)
    t1 = jnp.arange(1, S + 1, dtype=jnp.int32)
    outs = []
    for g, w in enumerate(POOL_WINDOWS):
        xg = ug[:, :, g]
        cs = jnp.cumsum(xg, axis=1)
        cs_lag = jnp.pad(cs, ((0, 0), (w, 0), (0, 0)))[:, :S]
        count = jnp.minimum(t1, w).astype(jnp.float32)[None, :, None]
        outs.append((cs - cs_lag) / count - xg)
    pooled = jnp.stack(outs, axis=2)
    mixed = jnp.einsum('bsgc,gcd->bsgd', pooled, w_pool.astype(jnp.float32))
    mixed = mixed * pool_scale.astype(jnp.float32).reshape(N_POOL_GROUPS, POOL_GROUP_DIM)
    return mixed.reshape(B, S, D_POOL).astype(u.dtype)


def diff_attention(q, k, v, lam):
    B, S = q.shape[0], q.shape[1]
    nb = S // Q_BLOCK
    scale = 1.0 / math.sqrt(DIFF_QK_DIM)
    q_blocks = jnp.moveaxis(q.reshape(B, nb, Q_BLOCK, DIFF_HEADS, 2, DIFF_QK_DIM), 1, 0)
    key_chunk = jnp.arange(S, dtype=jnp.int32) // CHUNK
    vf = v.astype(jnp.float32)
    kf = k.astype(jnp.float32)

    def block(args):
        qblk, bidx = args
        q_chunk = (bidx * Q_BLOCK + jnp.arange(Q_BLOCK, dtype=jnp.int32)) // CHUNK
        mask = key_chunk[None, :] <= q_chunk[:, None]
        s = jnp.einsum('bqhmd,bkhmd->bhmqk', qblk.astype(jnp.float32), kf) * scale
        s = jnp.where(mask[None, None, None], s, NEG_INF)
        p = jax.nn.softmax(s, axis=-1)
        a = p[:, :, 0] - lam * p[:, :, 1]
        return jnp.einsum('bhqk,bkhd->bqhd', a, vf)

    out = lax.map(block, (q_blocks, jnp.arange(nb, dtype=jnp.int32)))
    out = jnp.moveaxis(out, 0, 1).reshape(B, S, DIFF_HEADS, DIFF_V_DIM)
    return out.astype(v.dtype)


def hybrid_layer(x, c, cos, sin, layer_idx, w_ada, b_ada, g_pre, w_in, w_pool, pool_scale,
                 lq1, lk1, lq2, lk2, subln_g, w_out, g_post):
    B, S, _ = x.shape
    mod = (c @ w_ada + b_ada)[:, None, :]
    shift, scale, gate = jnp.split(mod, 3, axis=-1)
    h = rms_norm(x, g_pre) * (1 + scale) + shift

    z = h @ w_in
    splits = np.cumsum([D_POOL, D_POOL, D_QK, D_QK, D_DIFF]).tolist()
    u, g_pool, q, k, v, g_diff = jnp.split(z, splits, axis=-1)

    pool_out = multiscale_pool(u, w_pool, pool_scale) * jax.nn.silu(g_pool)

    lam_init = 0.8 - 0.6 * math.exp(-0.3 * layer_idx)
    lam = (jnp.exp(jnp.sum(lq1.astype(jnp.float32) * lk1.astype(jnp.float32)))
           - jnp.exp(jnp.sum(lq2.astype(jnp.float32) * lk2.astype(jnp.float32))) + lam_init)
    q = apply_rope(q.reshape(B, S, DIFF_HEADS, 2, DIFF_QK_DIM), cos, sin)
    k = apply_rope(k.reshape(B, S, DIFF_HEADS, 2, DIFF_QK_DIM), cos, sin)
    v = v.reshape(B, S, DIFF_HEADS, DIFF_V_DIM)
    att = diff_attention(q, k, v, lam)
    att = rms_norm(att, subln_g) * (1.0 - lam_init)
    diff_out = att.reshape(B, S, D_DIFF) * jax.nn.silu(g_diff)

    y = jnp.concatenate([pool_out, diff_out], axis=-1) @ w_out
    return x + (1 + gate) * rms_norm(y, g_post)


def setup_inputs(seed: int = 0) -> dict:
    key = jax.random.key(seed)
    ks = jax.random.split(key, 20)
    f32 = jnp.float32
    nrm = lambda k, shape, s: jax.random.normal(k, shape, f32) * s
    x = jax.random.normal(ks[0], (BATCH, SEQ, D_MODEL), f32)
    c = jax.random.normal(ks[1], (BATCH, D_MODEL), f32)
    offset = jax.random.randint(ks[2], (BATCH, 1), 0, 8192, dtype=jnp.int32)
    positions = offset + jnp.arange(SEQ, dtype=jnp.int32)[None, :]
    return {
        "x": x,
        "c": c,
        "positions": positions,
        "w_ada": nrm(ks[3], (DEPTH, D_MODEL, 3 * D_MODEL), 0.1 * D_MODEL ** -0.5),
        "b_ada": nrm(ks[4], (DEPTH, 3 * D_MODEL), 0.01),
        "g_pre": 1.0 + nrm(ks[5], (DEPTH, D_MODEL), 0.02),
        "w_in": nrm(ks[6], (DEPTH, D_MODEL, D_IN), D_MODEL ** -0.5),
        "w_pool": nrm(ks[7], (DEPTH, N_POOL_GROUPS, POOL_GROUP_DIM, POOL_GROUP_DIM), POOL_GROUP_DIM ** -0.5),
        "pool_scale": 1.0 + nrm(ks[8], (DEPTH, D_POOL), 0.02),
        "lambda_q1": nrm(ks[9], (DEPTH, DIFF_QK_DIM), 0.1),
        "lambda_k1": nrm(ks[10], (DEPTH, DIFF_QK_DIM), 0.1),
        "lambda_q2": nrm(ks[11], (DEPTH, DIFF_QK_DIM), 0.1),
        "lambda_k2": nrm(ks[12], (DEPTH, DIFF_QK_DIM), 0.1),
        "subln_g": 1.0 + nrm(ks[13], (DEPTH, DIFF_V_DIM), 0.02),
        "w_out": nrm(ks[14], (DEPTH, D_MIX, D_MODEL), D_MIX ** -0.5),
        "g_post": 1.0 + nrm(ks[15], (DEPTH, D_MODEL), 0.02),
    }


def reference(x, c, positions, w_ada, b_ada, g_pre, w_in, w_pool, pool_scale,
              lambda_q1, lambda_k1, lambda_q2, lambda_k2, subln_g, w_out, g_post):
    cos, sin = rope_tables(positions)
    for l in range(DEPTH):
        x = hybrid_layer(x, c, cos, sin, l, w_ada[l], b_ada[l], g_pre[l], w_in[l], w_pool[l],
                         pool_scale[l], lambda_q1[l], lambda_k1[l], lambda_q2[l], lambda_k2[l],
                         subln_g[l], w_out[l], g_post[l])
    return x
```

```python
import math
from contextlib import ExitStack

import numpy as np
import ml_dtypes

import concourse.bass as bass
import concourse.mybir as mybir
from concourse.bass_utils import run_bass_kernel_spmd

F32 = mybir.dt.float32
BF16 = mybir.dt.bfloat16
I32 = mybir.dt.int32
AF = mybir.ActivationFunctionType
ALU = mybir.AluOpType
AX = mybir.AxisListType

D = 2048
KC = 16
S = 4096
B = 4
DEPTH = 4
TT = 512
NS = 4
TO = NS * TT
DIN = 6144
EPS = 1e-6
NEG = -30000.0
TILES = ([0, 3, 4, 7], [1, 2, 5, 6])
WINS = (2, 4, 8, 16)
ENG = ("pe", "act", "dve", "pool", "sp")


class Sem:
    def __init__(self, h):
        self.h = h
        self.n = 0


class Tok:
    __slots__ = ("sem", "v")

    def __init__(self, sem, v):
        self.sem = sem
        self.v = v


_GLOBAL_SEMS = {}


def clear_sems(nc, sems):
    if not sems:
        return
    with nc.Block() as block:
        @block.gpsimd
        def _(e):
            for sm in sems:
                n = sm.h.num
                e.dma_reset(range(n, n + 1))
                e.sem_clear(sm.h)


class Phase:
    def __init__(self, nc, es, name):
        self.nc = nc
        self.es = es
        self.name = name
        self.q = {e: [] for e in ENG}
        self.waited = {e: {} for e in ENG}
        self.all_sems = []
        self.own = {e: self.sem("own_" + e) for e in ENG}
        self.dma_sems = []

    def sem(self, nm):
        sm = Sem(self.es.enter_context(self.nc.semaphore(f"{self.name}_{nm}")))
        self.all_sems.append(sm)
        return sm

    def dsem(self, nm):
        s = self.sem(nm)
        self.dma_sems.append(s)
        return s

    def op(self, eng, fn, waits=(), sig=None, inc=1):
        ws = []
        for t in waits:
            if t is None:
                continue
            cur = self.waited[eng].get(id(t.sem), 0)
            if t.v > cur:
                self.waited[eng][id(t.sem)] = t.v
                ws.append((t.sem.h, t.v))
        tok = None
        sh = None
        if sig is not None:
            sig.n += inc
            tok = Tok(sig, sig.n)
            sh = sig.h
        self.q[eng].append((ws, fn, sh, inc))
        return tok

    def c(self, eng, fn, waits=()):
        return self.op(eng, fn, waits, sig=self.own[eng], inc=1)

    def dma(self, eng, fn, sem, waits=()):
        return self.op(eng, fn, waits, sig=sem, inc=16)

    def run(self, extra_final_waits=()):
        fin = [Tok(s, s.n) for s in self.dma_sems if s.n > 0] + list(extra_final_waits)
        self.op("sp", None, fin)
        self.op("pool", None, fin)
        q = self.q
        pos = {e: 0 for e in ENG}
        val = {}
        prog = True
        while prog:
            prog = False
            for e in ENG:
                while pos[e] < len(q[e]):
                    ws, fn, sh, inc = q[e][pos[e]]
                    if all(val.get(id(h), _GLOBAL_SEMS.get(id(h), 0)) >= v for h, v in ws):
                        if sh is not None:
                            if id(sh) in _GLOBAL_SEMS:
                                _GLOBAL_SEMS[id(sh)] += inc
                            else:
                                val[id(sh)] = val.get(id(sh), 0) + inc
                        pos[e] += 1
                        prog = True
                    else:
                        break
        stuck = {e: (pos[e], len(q[e])) for e in ENG if pos[e] < len(q[e])}
        if stuck:
            raise RuntimeError(f"phase {self.name}: semaphore deadlock {stuck}")

        def replay(e, lst):
            for ws, fn, sh, inc in lst:
                for h, v in ws:
                    e.wait_ge(h, v)
                if fn is None:
                    continue
                ins = fn(e)
                if sh is not None:
                    ins.then_inc(sh, inc)

        clear_sems(self.nc, self.all_sems)

        with self.nc.Block() as block:
            @block.tensor
            def _(e):
                replay(e, q["pe"])

            @block.scalar
            def _(e):
                replay(e, q["act"])

            @block.vector
            def _(e):
                replay(e, q["dve"])

            @block.gpsimd
            def _(e):
                replay(e, q["pool"])

            @block.sync
            def _(e):
                replay(e, q["sp"])


def build_program(layers, first, last, dbg=False, stop=None, dup_B=0):
    nc = bass.Bass("TRN2", target_bir_lowering=False)
    es = ExitStack()

    def din(name, shape, dt):
        return nc.dram_tensor(name, shape, dt, kind="ExternalInput").ap()

    def dscr(name, shape, dt, out=False):
        if out:
            return nc.dram_tensor(name, shape, dt, kind="ExternalOutput").ap()
        return nc.dram_tensor(name, shape, dt).ap()

    NL = len(layers)
    NLP = NL
    assert list(layers) == list(range(NL))
    xT_in = din("xT", [D, TO], F32)
    cT_in = din("cT", [128, KC * B], F32)
    bsel_in = din("bsel", [128, B], F32)
    posr_in = din("posr", [128, TO], I32)
    invf_in = din("invf", [128, 1], F32)
    sgn_in = din("sgn", [128, 1], F32)
    perm_in = din("perm", [128, 128], BF16)
    ones_in = din("ones", [128, 128], BF16)
    ident_in = din("ident", [128, 128], BF16)
    khot_in = din("khot", [64, S], BF16)
    qmask_in = din("qmask", [64, TO], BF16)
    hw_in = din("hw", [128, NS * 8], F32)
    invc_in = din("invc16", [128, NS * 4 * 16], F32)
    wada_in = din("w_ada_sl", [NLP, D, 768], F32)
    bada_in = din("b_adaT", [128, NLP * 48], F32)
    gpre_in = din("g_preT", [128, NLP * KC], F32)
    gpost_in = din("g_postT", [128, NLP * KC], F32)
    pscale_in = din("pool_scaleT", [128, NLP * 8], F32)
    win_in = din("w_in", [NLP, D, DIN], F32)
    wout_in = din("w_out", [NLP, D, D], F32)
    wpool_in = din("w_pool", [NLP, 1024, 256], F32)
    lam_in = [din(n, [128, NLP * 64], F32) for n in ("lq1r", "lk1r", "lq2r", "lk2r")]
    subln_in = din("sublnr", [128, NLP * 128], F32)
    lamc_in = din("lamc", [128, NLP * 2], F32)
    oT = nc.dram_tensor("oT", [D, TO], F32, kind="ExternalOutput").ap()

    _win_bf = dscr("win_bf", [D, DIN], BF16)
    _wout_bf = [dscr(f"wout_bf{j}", [D, D], BF16) for j in range(2)]
    _wpool_bf = dscr("wpool_bf", [1024, 256], BF16)
    win_bf = [_win_bf for _ in layers]
    wout_bf = [_wout_bf[i % 2] for i in range(len(layers))]
    wpool_bf = [_wpool_bf for _ in layers]
    uT = dscr("uT", [1024, TO], F32, out=dbg)
    uh_loc = dscr("uh_loc", [1024, NS * 16], F32)
    uh_g = dscr("uh_g", [2 * 1024, NS * 16], F32)
    sgpT = dscr("sgpT", [1024, TO], BF16, out=dbg)
    qT = dscr("qT", [1024, TO], BF16, out=dbg)
    kT_loc = [dscr(f"kT_loc{j}", [512, TO], BF16) for j in range(2)]
    kT_g = [dscr(f"kT_g{j}", [2 * 512, TO], BF16) for j in range(2)]
    v_loc = [dscr(f"v_loc{j}", [TO // 2, 1024], BF16) for j in range(2)]
    v_g = [dscr(f"v_g{j}", [TO, 1024], BF16) for j in range(2)]
    gd = dscr("gd", [TO, 1024], BF16, out=dbg)
    mixT = dscr("mixT", [D, TO], BF16, out=dbg)
    mod_loc = [dscr(f"mod_loc{l}", [128, 24], F32) for l in range(NLP)]
    mod_g = [dscr(f"mod_g{l}", [8 * 128, 24], F32) for l in range(NLP)]
    if dbg:
        dbg_k = [dscr(f"dbg_k{j}", [2 * 512, TO], BF16, out=True) for j in range(2)]
        dbg_mod = dscr("dbg_mod", [128, NLP * 48], F32, out=True)
        dbg_tab = dscr("dbg_tab", [128, 2 * TO], F32, out=True)

    def sb(name, shape, dt):
        return es.enter_context(nc.sbuf_tensor("g_" + name, shape, dt))

    ones_bf = sb("ones_bf", [128, 128], BF16)
    perm_bf = sb("perm_bf", [128, 128], BF16)
    ident_bf = sb("ident_bf", [128, 128], BF16)
    cosT = sb("cosT", [128, TO], F32)
    sinT = sb("sinT", [128, TO], F32)
    modA = sb("modA", [128, NLP, 48], F32)
    A_all = sb("A_all", [128, NLP, KC], F32)
    G2_all = sb("G2_all", [128, NLP, KC], F32)
    gpre_sb = sb("gpre_sb", [128, NLP, KC], F32)
    gpost_sb = sb("gpost_sb", [128, NLP, KC], F32)
    pscale_sb = sb("pscale_sb", [128, NLP, 8], F32)
    neglam = sb("neglam", [128, NLP], F32)
    sublnG = sb("sublnG", [128, NLP, 512], F32)
    hw_sb = sb("hw_sb", [128, NS * 8], F32)
    invc_sb = sb("invc_sb", [128, NS, 4, 16], F32)
    neghalf = sb("neghalf", [128, 512], F32)

    wcast_tok = [None] * NL

    def cast_chunks(i):
        l = layers[i]
        out = []
        wi = win_in[l].rearrange("(kc p) n -> p kc n", p=128)
        wb_ = win_bf[i].rearrange("(kc p) n -> p kc n", p=128)
        for kc in range(KC):
            for t3 in range(3):
                out.append((wi[:, kc, t3 * 2048:(t3 + 1) * 2048], wb_[:, kc, t3 * 2048:(t3 + 1) * 2048]))
        wo_i = wout_in[l].rearrange("(kc p) n -> p kc n", p=128)
        wo_b = wout_bf[i].rearrange("(kc p) n -> p kc n", p=128)
        for kc in range(KC):
            out.append((wo_i[:, kc, :], wo_b[:, kc, :]))
        out.append((wpool_in[l].rearrange("(a p) d -> p a d", p=128), wpool_bf[i].rearrange("(a p) d -> p a d", p=128)))
        return out

    class Caster:
        def __init__(self, P, lsb, i, engines):
            self.P = P
            self.chunks = cast_chunks(i) if i < NL else []
            self.engines = engines
            self.k = 0
            if not self.chunks:
                return
            R = self.R = 3
            self.s32 = [lsb(f"cst32_{j}", [128, 2048], F32) for j in range(R)]
            self.s16 = [lsb(f"cst16_{j}", [128, 2048], BF16) for j in range(R)]
            self.sl = [P.dsem(f"cstl{j}") for j in range(R)]
            self.ss = [P.dsem(f"csts{j}") for j in range(R)]
            self.rd32 = [None] * R
            self.rd16 = [None] * R

        def step(self, n=1):
            P = self.P
            for _ in range(n):
                if self.k >= len(self.chunks):
                    return
                src, dst = self.chunks[self.k]
                j = self.k % self.R
                eng = self.engines[self.k % len(self.engines)]
                self.k += 1
                s32, s16 = self.s32[j], self.s16[j]
                is3 = len(src.shape) == 3
                o32 = s32[:].rearrange("p (a d) -> p a d", a=src.shape[1]) if is3 else s32[:]
                o16 = s16[:].rearrange("p (a d) -> p a d", a=src.shape[1]) if is3 else s16[:]
                t_l = P.dma("sp", lambda e, o32=o32, src=src: e.dma_start(out=o32, in_=src), self.sl[j], [self.rd32[j]])
                if eng == "act":
                    t_c = P.c(eng, lambda e, s32=s32, s16=s16: e.activation(out=s16[:], in_=s32[:], func=AF.Copy), [t_l, self.rd16[j]])
                else:
                    t_c = P.c(eng, lambda e, s32=s32, s16=s16: e.tensor_copy(out=s16[:], in_=s32[:]), [t_l, self.rd16[j]])
                self.rd32[j] = t_c
                self.rd16[j] = P.dma("sp", lambda e, o16=o16, dst=dst: e.dma_start(out=dst, in_=o16), self.ss[j], [t_c])

        def finish(self):
            self.step(len(self.chunks))

    def prologue():
        with ExitStack() as les:
            P = Phase(nc, les, "pro")

            def lsb(name, shape, dt):
                return les.enter_context(nc.sbuf_tensor(f"{P.name}_{name}", shape, dt))

            caster0 = Caster(P, lsb, 0, ("dve", "act", "pool"))

            ld = P.dsem("ld")
            posi = lsb("posi", [128, TO], I32)
            invf = lsb("invf", [128, 1], F32)
            sgn = lsb("sgn", [128, 1], F32)
            cT32 = lsb("cT32", [128, KC * B], F32)
            bsel = lsb("bsel", [128, B], F32)
            bada = lsb("bada", [128, NLP, 48], F32)
            lam_sb = [lsb(f"lam{i}", [128, NLP, 64], F32) for i in range(4)]
            subln_sb = lsb("subln_sb", [128, NLP, 128], F32)
            lamc_sb = lsb("lamc_sb", [128, NLP, 2], F32)
            loads = [
                (posi[:], posr_in), (invf[:], invf_in), (sgn[:], sgn_in), (cT32[:], cT_in),
                (bsel[:], bsel_in), (bada[:].rearrange("p l j -> p (l j)"), bada_in),
                (ones_bf[:], ones_in), (perm_bf[:], perm_in), (ident_bf[:], ident_in),
                (hw_sb[:], hw_in), (invc_sb[:].rearrange("p s g x -> p (s g x)"), invc_in),
                (gpre_sb[:].rearrange("p l j -> p (l j)"), gpre_in),
                (gpost_sb[:].rearrange("p l j -> p (l j)"), gpost_in),
                (pscale_sb[:].rearrange("p l j -> p (l j)"), pscale_in),
                (subln_sb[:].rearrange("p l j -> p (l j)"), subln_in),
                (lamc_sb[:].rearrange("p l j -> p (l j)"), lamc_in),
            ] + [(lam_sb[i][:].rearrange("p l j -> p (l j)"), lam_in[i]) for i in range(4)]
            for o_, i_ in loads:
                P.dma("sp", lambda e, o_=o_, i_=i_: e.dma_start(out=o_, in_=i_), ld)
            t_ld = Tok(ld, ld.n)

            posf = lsb("posf", [128, TO], F32)
            ang = lsb("ang", [128, TO], F32)
            kk_i = lsb("kk_i", [128, TO], I32)
            kk_f = lsb("kk_f", [128, TO], F32)
            red = lsb("red", [128, TO], F32)
            C1 = 6.28125
            C2 = 2.0 * math.pi - C1
            t = P.c("dve", lambda e: e.tensor_copy(out=posf[:], in_=posi[:]), [t_ld])
            t_ang = P.c("dve", lambda e: e.tensor_scalar(out=ang[:], in0=posf[:], scalar1=invf[:, 0:1],
                                                         scalar2=None, op0=ALU.mult), [t])
            t_prev_act = None
            for which, shift_, dst in ((0, 0.0, sinT), (1, math.pi / 2.0, cosT)):
                src = ang
                if which == 1:
                    t_ang2 = P.c("dve", lambda e, shift_=shift_: e.tensor_scalar(out=posf[:], in0=ang[:], scalar1=shift_,
                                                                  scalar2=None, op0=ALU.add), [t_ang, t_prev_act])
                    src = posf
                    t0 = t_ang2
                else:
                    t0 = t_ang
                t1 = P.c("dve", lambda e, src=src: e.tensor_scalar(out=kk_i[:], in0=src[:], scalar1=1.0 / (2.0 * math.pi),
                                                                   scalar2=None, op0=ALU.mult), [t0, t_prev_act])
                t2 = P.c("dve", lambda e: e.tensor_copy(out=kk_f[:], in_=kk_i[:]), [t1])
                t3 = P.c("dve", lambda e, src=src: e.scalar_tensor_tensor(out=red[:], in0=kk_f[:], scalar=-C1, in1=src[:],
                                                                          op0=ALU.mult, op1=ALU.add), [t2])
                t4 = P.c("dve", lambda e: e.scalar_tensor_tensor(out=red[:], in0=kk_f[:], scalar=-C2, in1=red[:],
                                                                 op0=ALU.mult, op1=ALU.add), [t3])
                t5 = P.c("dve", lambda e: e.tensor_scalar(out=red[:], in0=red[:], scalar1=-3.141592, scalar2=3.141592,
                                                          op0=ALU.max, op1=ALU.min), [t4])
                if which == 0:
                    t_prev_act = P.c("act", lambda e, dst=dst: e.activation(out=dst[:], in_=red[:], func=AF.Sin,
                                                                            scale=sgn[:, 0:1]), [t5])
                else:
                    t_prev_act = P.c("act", lambda e, dst=dst: e.activation(out=dst[:], in_=red[:], func=AF.Sin), [t5])
            t_tabs = t_prev_act

            lamt = lsb("lamt", [128, NLP, 64], F32)
            dots = lsb("dots", [128, 2, NLP], F32)
            exps = lsb("exps", [128, 2, NLP], F32)
            tl = None
            for j in range(2):
                ta = P.c("dve", lambda e, j=j: e.tensor_tensor(out=lamt[:], in0=lam_sb[2 * j][:], in1=lam_sb[2 * j + 1][:],
                                                               op=ALU.mult), [t_ld, tl])
                tl = P.c("dve", lambda e, j=j: e.tensor_reduce(out=dots[:, j, :], in_=lamt[:], axis=AX.X, op=ALU.add), [ta])
            te = P.c("act", lambda e: e.activation(out=exps[:], in_=dots[:], func=AF.Exp), [tl])
            tn = P.c("dve", lambda e: e.tensor_tensor(out=neglam[:], in0=exps[:, 1, :], in1=exps[:, 0, :], op=ALU.subtract), [te])
            for l in range(NLP):
                tn = P.c("dve", lambda e, l=l: e.tensor_scalar(out=neglam[:, l:l + 1], in0=neglam[:, l:l + 1],
                                                               scalar1=lamc_sb[:, l, 0:1], scalar2=None, op0=ALU.add), [tn, t_ld])
                for hh in range(4):
                    tn = P.c("dve", lambda e, l=l, hh=hh: e.tensor_scalar(
                        out=sublnG[:, l, hh * 128:(hh + 1) * 128], in0=subln_sb[:, l, :], scalar1=lamc_sb[:, l, 1:2],
                        scalar2=None, op0=ALU.mult), [tn, t_ld])
            P.c("pool", lambda e: e.memset(neghalf[:], -0.5))

            wada_sb = lsb("wada_sb", [128, KC, 768], F32)
            with nc.psum_tensor("modps", [128, 6 * B], F32) as modps:
                mod_sl = lsb("mod_sl", [128, NLP, 6, B], F32)
                wsem = P.dsem("wada")
                t_ev = None
                for l in range(NLP):
                    tw = P.dma("sp", lambda e, l=l: e.dma_start(
                        out=wada_sb[:], in_=wada_in[l].rearrange("(kc p) n -> p kc n", p=128)), wsem, [t_ev])
                    tm = None
                    for jc in range(6):
                        for kc in range(KC):
                            tm = P.op("pe", lambda e, jc=jc, kc=kc: e.matmul(
                                modps[:, jc * B:(jc + 1) * B], wada_sb[:, kc, jc * 128:(jc + 1) * 128],
                                cT32[:, kc * B:(kc + 1) * B],
                                start=(kc == 0), stop=(kc == KC - 1)),
                                [tw, t_ld, t_ev], sig=(P.own["pe"] if (jc == 5 and kc == KC - 1) else None))
                    t_ev = P.c("dve", lambda e, l=l: e.tensor_copy(
                        out=mod_sl[:, l, :, :].rearrange("p j b -> p (j b)"), in_=modps[:]), [tm])
                ms = P.dsem("modst")
                ccs = P.sem("cc")
                mod_all = lsb("mod_all", [128, 8, NLP, 6, B], F32)
                ml = P.dsem("modld")
                t_ml = None
                t_cc = None
                for l in range(NLP):
                    t_st = P.dma("sp", lambda e, l=l: e.dma_start(
                        out=mod_loc[l], in_=mod_sl[:, l, :, :].rearrange("p j b -> p (j b)")), ms, [t_ev])
                    t_cc = P.op("pool", lambda e, l=l: e.collective_compute(
                        "AllGather", ALU.bypass, replica_groups=[list(range(8))], ins=[mod_loc[l].opt()], outs=[mod_g[l].opt()]),
                        [t_st, t_cc], sig=ccs, inc=1)
                    P.op("pool", None, [t_cc])
                    t_ml = P.dma("sp", lambda e, l=l: e.dma_start(
                        out=mod_all[:, :, l, :, :].rearrange("p i j b -> p i (j b)"),
                        in_=mod_g[l].rearrange("(i p) c -> p i c", p=128)), ml, [t_cc])
                t_ml = Tok(ml, ml.n)
                tb_ = None
                for i8 in range(8):
                    for l in range(NLP):
                        dst = modA[:, l, i8 * 6:(i8 + 1) * 6]
                        tb_ = P.c("dve", lambda e, i8=i8, l=l, dst=dst: e.tensor_scalar(
                            out=dst, in0=mod_all[:, i8, l, :, 0], scalar1=bsel[:, 0:1], scalar2=None, op0=ALU.mult),
                            [t_ml, t_ld])
                        for b in range(1, B):
                            tb_ = P.c("dve", lambda e, i8=i8, l=l, dst=dst, b=b: e.scalar_tensor_tensor(
                                out=dst, in0=mod_all[:, i8, l, :, b], scalar=bsel[:, b:b + 1], in1=dst,
                                op0=ALU.mult, op1=ALU.add), [tb_])
                tb_ = P.c("dve", lambda e: e.tensor_tensor(out=modA[:], in0=modA[:], in1=bada[:], op=ALU.add), [tb_])
                tb_ = P.c("dve", lambda e: e.scalar_tensor_tensor(
                    out=A_all[:], in0=modA[:, :, 16:32], scalar=1.0, in1=gpre_sb[:], op0=ALU.add, op1=ALU.mult), [tb_])
                tb_ = P.c("dve", lambda e: e.scalar_tensor_tensor(
                    out=G2_all[:], in0=modA[:, :, 32:48], scalar=1.0, in1=gpost_sb[:], op0=ALU.add, op1=ALU.mult), [tb_])
                caster0.finish()
                fin = [tb_, tn, t_tabs]
                if dbg:
                    dsm = P.dsem("dbg")
                    P.dma("sp", lambda e: e.dma_start(out=dbg_mod, in_=modA[:].rearrange("p l j -> p (l j)")), dsm, [tb_])
                    P.dma("sp", lambda e: e.dma_start(out=dbg_tab[:, 0:TO], in_=cosT[:]), dsm, [t_tabs])
                    P.dma("sp", lambda e: e.dma_start(out=dbg_tab[:, TO:2 * TO], in_=sinT[:]), dsm, [t_tabs])
                P.op("dve", None, fin)
                P.run()

    def phase_A(i, l, x_src):
        with ExitStack() as les:
            P = Phase(nc, les, f"A{i}")

            def lsb(name, shape, dt):
                return les.enter_context(nc.sbuf_tensor(f"{P.name}_{name}", shape, dt))

            def lps(name, shape, dt):
                return les.enter_context(nc.psum_tensor(f"{P.name}_{name}", shape, dt))

            xt = lsb("xt", [128, KC, TT], F32)
            sqr = [lsb(f"sqr{j}", [128, TT], BF16) for j in range(2)]
            hT = [lsb(f"hT{j}", [128, KC, TT], BF16) for j in range(2)]
            wb = [lsb(f"wb{j}", [128, KC, 512], BF16) for j in range(3)]
            rpre = lsb("rpre", [128, TT], F32)
            rstd = lsb("rstd", [128, TT], F32)
            xr = [lsb(f"xr{j}", [128, TT], F32) for j in range(2)]
            NR = 4
            tf = [lsb(f"tf{j}", [128, TT], F32) for j in range(NR)]
            tb = [lsb(f"tb{j}", [128, TT], BF16) for j in range(NR)]
            qb = [lsb(f"qb{j}", [128, TT], BF16) for j in range(2)]
            r1 = [lsb(f"r1{j}", [128, TT], F32) for j in range(2)]
            r2 = [lsb(f"r2{j}", [128, TT], F32) for j in range(2)]
            acc = [lps(f"acc{j}", [128, 512], F32) for j in range(4)]
            ssb = lps("ssb", [128, 512], F32)
            pq = [lps(f"pq{j}", [128, 512], F32) for j in range(2)]

            s_xt = P.dsem("xt")
            s_wb = [P.dsem(f"wb{j}") for j in range(3)]
            s_tf = [P.dsem(f"tfo{j}") for j in range(NR)]
            s_th = [P.dsem(f"tho{j}") for j in range(NR)]
            s_tb = [P.dsem(f"tbo{j}") for j in range(NR)]

            st = dict(th_rd=[None] * NR,
                xt_rd=[], sqr_rd=[None, None], ssb_rd=None, rstd_rd=None, hT_rd=[None, None],
                xr_rd=[None, None], wb_rd=[None, None, None], acc_rd=[None] * 4,
                tf_rd=[None] * NR, tb_rd=[None] * NR, qb_rd=[None, None], pq_rd=[None, None],
                r1_rd=[None, None], r2_rd=[None, None], tfi=0, tbi=0, qi=0, ai=0,
            )
            win_v = win_bf[i].rearrange("(kc p) n -> p kc n", p=128)
            x_v = x_src.rearrange("(kc p) t -> p kc t", p=128)
            wloads = {}
            t_hT = {}
            pending_pe = []

            def issue_wload(gi):
                if gi >= NS * 12 or gi in wloads:
                    return
                g = gi % 12
                j = gi % 3
                wloads[gi] = P.dma("sp", lambda e, g=g, j=j: e.dma_start(
                    out=wb[j][:], in_=win_v[:, :, g * 512:(g + 1) * 512]), s_wb[j],
                    [st["wb_rd"][j], wcast_tok[i]])

            def emit_norm(s):
                c0 = s * TT
                hb = s % 2
                t_x = P.dma("sp", lambda e, c0=c0: e.dma_start(out=xt[:], in_=x_v[:, :, c0:c0 + TT]), s_xt, st["xt_rd"])
                st["xt_rd"] = []
                t_ss = None
                for kc in range(KC):
                    j = kc % 2
                    t_sq = P.c("act", lambda e, kc=kc, j=j: e.activation(out=sqr[j][:], in_=xt[:, kc, :], func=AF.Square),
                               [t_x, st["sqr_rd"][j]])
                    t_ss = P.op("pe", lambda e, kc=kc, j=j: e.matmul(ssb[:], ones_bf[:], sqr[j][:], start=(kc == 0), stop=(kc == KC - 1)),
                                [t_sq, st["ssb_rd"] if kc == 0 else None], sig=P.own["pe"])
                    st["sqr_rd"][j] = t_ss
                t_rp = P.c("dve", lambda e: e.tensor_scalar(out=rpre[:], in0=ssb[:], scalar1=1.0 / D, scalar2=EPS,
                                                            op0=ALU.mult, op1=ALU.add), [t_ss, st["rstd_rd"]])
                st["ssb_rd"] = t_rp
                t_rs = P.c("pool", lambda e: e.tensor_tensor(out=rstd[:], in0=rpre[:], in1=neghalf[:], op=ALU.pow),
                           [t_rp, st["rstd_rd"]])
                t_h = None
                for kc in range(KC):
                    j = kc % 2
                    t_xr = P.c("dve", lambda e, kc=kc, j=j: e.tensor_tensor(out=xr[j][:], in0=xt[:, kc, :], in1=rstd[:], op=ALU.mult),
                               [t_rs, t_x, st["xr_rd"][j]])
                    t_h = P.c("act", lambda e, kc=kc, j=j, hb=hb: e.activation(
                        out=hT[hb][:, kc, :], in_=xr[j][:], func=AF.Identity,
                        scale=A_all[:, l, kc:kc + 1], bias=modA[:, l, kc:kc + 1]),
                        [t_xr, st["hT_rd"][hb]])
                    st["xr_rd"][j] = t_h
                    if kc == KC - 1:
                        st["xt_rd"] = [t_xr, t_ss]
                        st["rstd_rd"] = t_xr
                t_hT[s] = t_h

            issue_wload(0)
            issue_wload(1)
            emit_norm(0)
            for s in range(NS):
                c0 = s * TT
                hb = s % 2
                t_h = t_hT[s]
                t_last_mm = None
                for g in range(12):
                    gi = s * 12 + g
                    issue_wload(gi)
                    issue_wload(gi + 1)
                    issue_wload(gi + 2)
                    j3 = gi % 3
                    t_w = wloads[gi]
                    t_mm_last_group = None
                    for sub in range(4):
                        a = st["ai"] % 4
                        st["ai"] += 1
                        t_mm = None
                        for kc in range(KC):
                            if g < 8:
                                fn = lambda e, a=a, j3=j3, kc=kc, sub=sub, hb=hb: e.matmul(
                                    acc[a][:], wb[j3][:, kc, sub * 128:(sub + 1) * 128], hT[hb][:, kc, :],
                                    start=(kc == 0), stop=(kc == KC - 1))
                            else:
                                fn = lambda e, a=a, j3=j3, kc=kc, sub=sub, hb=hb: e.matmul(
                                    acc[a][:], hT[hb][:, kc, sub * 128:(sub + 1) * 128], wb[j3][:, kc, :],
                                    start=(kc == 0), stop=(kc == KC - 1))
                            t_mm = P.op("pe", fn, [t_w, t_h, st["acc_rd"][a]],
                                        sig=(P.own["pe"] if kc == KC - 1 else None))
                        t_mm_last_group = t_mm
                        t_last_mm = t_mm
                        for fnp in pending_pe:
                            fnp()
                        pending_pe.clear()
                        if g < 2:
                            c = g * 4 + sub
                            k = st["tfi"] % NR
                            st["tfi"] += 1
                            t_e = P.c("act", lambda e, a=a, k=k: e.activation(out=tf[k][:], in_=acc[a][:], func=AF.Copy),
                                      [t_mm, st["tf_rd"][k], st["th_rd"][k]])
                            st["acc_rd"][a] = t_e
                            t_d1 = P.dma("pool", lambda e, k=k, c=c, c0=c0: e.dma_start(
                                out=uT[c * 128:(c + 1) * 128, c0:c0 + TT], in_=tf[k][:]), s_tf[k], [t_e])
                            t_d2 = P.dma("sp", lambda e, k=k, c=c, s=s: e.dma_start(
                                out=uh_loc[c * 128:(c + 1) * 128, s * 16:(s + 1) * 16], in_=tf[k][:, TT - 16:TT]), s_th[k], [t_e])
                            st["tf_rd"][k] = t_d1
                            st["th_rd"][k] = t_d2
                        elif g < 4:
                            c = (g - 2) * 4 + sub
                            k = st["tbi"] % NR
                            st["tbi"] += 1
                            t_e = P.c("act", lambda e, a=a, k=k: e.activation(out=tb[k][:], in_=acc[a][:], func=AF.Silu),
                                      [t_mm, st["tb_rd"][k]])
                            st["acc_rd"][a] = t_e
                            st["tb_rd"][k] = P.dma("pool", lambda e, k=k, c=c, c0=c0: e.dma_start(
                                out=sgpT[c * 128:(c + 1) * 128, c0:c0 + TT], in_=tb[k][:]), s_tb[k], [t_e])
                        elif g < 8:
                            c = (g - 4) * 4 + sub
                            jq = st["qi"] % 2
                            st["qi"] += 1
                            t_e = P.c("act", lambda e, a=a, jq=jq: e.activation(out=qb[jq][:], in_=acc[a][:], func=AF.Copy),
                                      [t_mm, st["qb_rd"][jq]])
                            st["acc_rd"][a] = t_e
                            k = st["tbi"] % NR
                            st["tbi"] += 1

                            def rope_tail(jq=jq, k=k, c=c, c0=c0, t_e=t_e):
                                t_p = P.op("pe", lambda e: e.matmul(pq[jq][:], perm_bf[:], qb[jq][:], start=True, stop=True),
                                           [t_e, st["pq_rd"][jq]], sig=P.own["pe"])
                                t_1 = P.c("pool", lambda e: e.tensor_tensor(
                                    out=r1[jq][:], in0=qb[jq][:], in1=cosT[:, c0:c0 + TT], op=ALU.mult), [t_e, st["r1_rd"][jq]])
                                t_2 = P.c("dve", lambda e: e.tensor_tensor(
                                    out=r2[jq][:], in0=pq[jq][:], in1=sinT[:, c0:c0 + TT], op=ALU.mult), [t_p, st["r2_rd"][jq]])
                                st["pq_rd"][jq] = t_2
                                t_3 = P.c("pool", lambda e: e.tensor_tensor(
                                    out=tb[k][:], in0=r1[jq][:], in1=r2[jq][:], op=ALU.add), [t_1, t_2, st["tb_rd"][k]])
                                st["qb_rd"][jq] = t_3
                                st["r1_rd"][jq] = t_3
                                st["r2_rd"][jq] = t_3
                                cc = c % 8
                                if c < 8:
                                    dst, rr = qT, cc * 128
                                else:
                                    dst, rr = kT_loc[cc // 4], (cc % 4) * 128
                                st["tb_rd"][k] = P.dma("pool", lambda e: e.dma_start(
                                    out=dst[rr:rr + 128, c0:c0 + TT], in_=tb[k][:]), s_tb[k], [t_3])
                            pending_pe.append(rope_tail)
                        elif g < 10:
                            k = st["tbi"] % NR
                            st["tbi"] += 1
                            t_e = P.c("act", lambda e, a=a, k=k: e.activation(out=tb[k][:], in_=acc[a][:], func=AF.Copy),
                                      [t_mm, st["tb_rd"][k]])
                            st["acc_rd"][a] = t_e
                            f0 = (g - 8) * 512
                            vr = (s % 2) * TT + sub * 128
                            st["tb_rd"][k] = P.dma("pool", lambda e, k=k, vr=vr, s=s, f0=f0: e.dma_start(
                                out=v_loc[s // 2][vr:vr + 128, f0:f0 + 512], in_=tb[k][:]), s_tb[k], [t_e])
                        else:
                            kf = st["tfi"] % NR
                            st["tfi"] += 1
                            t_e = P.c("act", lambda e, a=a, kf=kf: e.activation(out=tf[kf][:], in_=acc[a][:], func=AF.Silu),
                                      [t_mm, st["tf_rd"][kf], st["th_rd"][kf]])
                            st["acc_rd"][a] = t_e
                            k = st["tbi"] % NR
                            st["tbi"] += 1
                            t_m = P.c("dve", lambda e, kf=kf, k=k: e.tensor_tensor(
                                out=tb[k][:], in0=tf[kf][:], in1=sublnG[:, l, :], op=ALU.mult), [t_e, st["tb_rd"][k]])
                            st["tf_rd"][kf] = t_m
                            f0 = (g - 10) * 512
                            st["tb_rd"][k] = P.dma("pool", lambda e, k=k, c0=c0, sub=sub, f0=f0: e.dma_start(
                                out=gd[c0 + sub * 128:c0 + (sub + 1) * 128, f0:f0 + 512], in_=tb[k][:]), s_tb[k], [t_m])
                    st["wb_rd"][j3] = t_mm_last_group
                    if g == 5 and s + 1 < NS:
                        emit_norm(s + 1)
                st["hT_rd"][hb] = t_last_mm
            for fnp in pending_pe:
                fnp()
            pending_pe.clear()
            P.run()

    def phase_A2(i, l):
        with ExitStack() as les:
            P = Phase(nc, les, f"P{i}")

            def lsb(name, shape, dt):
                return les.enter_context(nc.sbuf_tensor(f"{P.name}_{name}", shape, dt))

            W = TT + 16
            ub = lsb("ub", [128, 8, W], F32)
            uhs = lsb("uhs", [128, 2, 8, NS * 16], F32)
            Ta = lsb("Ta", [128, W], F32)
            Tb = lsb("Tb", [128, W], F32)
            t16 = lsb("t16", [128, 16], F32)
            pooled = lsb("pooled", [128, 8, TT], BF16)
            sgp = lsb("sgp", [128, 8, TT], BF16)
            wp = lsb("wp", [128, 8, 256], BF16)
            mo = [lsb(f"mo{j}", [128, TT], BF16) for j in range(2)]
            pacc = [les.enter_context(nc.psum_tensor(f"{P.name}_pacc{j}", [128, 512], F32)) for j in range(2)]
            s_ld = P.dsem("ld")
            s_u = P.dsem("u")
            s_g = P.dsem("g")
            s_mo = [P.dsem(f"mo{j}") for j in range(2)]
            ccs = P.sem("cc")
            groups = [[0, 1], [2, 3], [4, 5], [6, 7]]
            t_cc = []
            for src, dst in ((uh_loc, uh_g), (kT_loc[0], kT_g[0]), (kT_loc[1], kT_g[1]), (v_loc[0], v_g[0]), (v_loc[1], v_g[1])):
                t_cc.append(P.op("pool", lambda e, src=src, dst=dst: e.collective_compute(
                    "AllGather", ALU.bypass, replica_groups=groups, ins=[src.opt()], outs=[dst.opt()]),
                    [t_cc[-1]] if t_cc else [], sig=ccs, inc=1))
                P.op("pool", None, [t_cc[-1]])
            t_w = P.dma("sp", lambda e: e.dma_start(out=wp[:], in_=wpool_bf[i].rearrange("(a p) d -> p a d", p=128)), s_ld, [wcast_tok[i]])
            t_h = P.dma("sp", lambda e: e.dma_start(
                out=uhs[:].rearrange("p r c x -> p (r c) x"),
                in_=uh_g.rearrange("(rc p) x -> p rc x", p=128)), s_ld, [t_cc[0]])
            t_ldc = Tok(s_ld, s_ld.n)
            ub_rd = []
            sgp_rd = None
            pooled_rd = None
            pacc_rd = [None, None]
            mo_rd = [None, None]
            tprev = None
            mi = 0
            for s in range(NS):
                c0 = s * TT
                t_u = P.dma("sp", lambda e, c0=c0: e.dma_start(
                    out=ub[:, :, 16:W], in_=uT.rearrange("(c p) t -> p c t", p=128)[:, :, c0:c0 + TT]), s_u, ub_rd)
                t_sg = P.dma("sp", lambda e, c0=c0: e.dma_start(
                    out=sgp[:], in_=sgpT.rearrange("(c p) t -> p c t", p=128)[:, :, c0:c0 + TT]), s_g, [sgp_rd])
                th = None
                first_ = True
                for r_ in range(2):
                    for s_ in range(NS):
                        idx = s * 8 + r_ * 4 + s_
                        src = uhs[:, r_, :, s_ * 16:(s_ + 1) * 16]
                        if first_:
                            th = P.c("dve", lambda e, src=src, idx=idx: e.tensor_scalar(
                                out=ub[:, :, 0:16], in0=src, scalar1=hw_sb[:, idx:idx + 1], scalar2=None, op0=ALU.mult),
                                [t_ldc] + ub_rd)
                            first_ = False
                        else:
                            th = P.c("dve", lambda e, src=src, idx=idx: e.scalar_tensor_tensor(
                                out=ub[:, :, 0:16], in0=src, scalar=hw_sb[:, idx:idx + 1], in1=ub[:, :, 0:16],
                                op0=ALU.mult, op1=ALU.add), [th])
                ub_rd = []
                tp = None
                for c in range(8):
                    g = c // 2
                    w = WINS[g]
                    u = ub[:, c, :]
                    cur, off = u, 0
                    bufs = [Ta, Tb]
                    bi = 0
                    sh = 1
                    tt_ = None
                    while sh < w:
                        o = bufs[bi]
                        lo = 2 * sh - 1
                        tt_ = P.c("dve", lambda e, o=o, cur=cur, lo=lo, sh=sh: e.tensor_tensor(
                            out=o[:, lo:W], in0=cur[:, lo:W], in1=cur[:, lo - sh:W - sh], op=ALU.add),
                            [t_u, th, tt_, tp, pooled_rd if c == 0 else None])
                        cur = o
                        bi ^= 1
                        sh *= 2
                    tp = P.c("dve", lambda e, c=c, cur=cur, w=w: e.scalar_tensor_tensor(
                        out=pooled[:, c, :], in0=cur[:, 16:W], scalar=1.0 / w, in1=ub[:, c, 16:W],
                        op0=ALU.mult, op1=ALU.subtract), [tt_])
                    tq = P.c("dve", lambda e, cur=cur, s=s, g=g: e.tensor_tensor(
                        out=t16[:], in0=cur[:, 16:32], in1=invc_sb[:, s, g, :], op=ALU.mult), [tp])
                    tp = P.c("dve", lambda e, c=c: e.tensor_tensor(
                        out=pooled[:, c, 0:16], in0=t16[:], in1=ub[:, c, 16:32], op=ALU.subtract), [tq])
                ub_rd = [tp]
                t_last_pm = None
                for c in range(8):
                    g, dc = c // 2, c % 2
                    a = mi % 2
                    tm = None
                    for cc in range(2):
                        tm = P.op("pe", lambda e, a=a, g=g, dc=dc, cc=cc: e.matmul(
                            pacc[a][:], wp[:, g * 2 + cc, dc * 128:(dc + 1) * 128], pooled[:, g * 2 + cc, :],
                            start=(cc == 0), stop=(cc == 1)), [tp, t_ldc, pacc_rd[a]],
                            sig=(P.own["pe"] if cc == 1 else None))
                    t_last_pm = tm
                    te = P.c("dve", lambda e, a=a, c=c: e.scalar_tensor_tensor(
                        out=mo[a][:], in0=pacc[a][:], scalar=pscale_sb[:, l, c:c + 1], in1=sgp[:, c, :],
                        op0=ALU.mult, op1=ALU.mult), [tm, t_sg, mo_rd[a]])
                    pacc_rd[a] = te
                    mo_rd[a] = P.dma("sp", lambda e, a=a, c=c, c0=c0: e.dma_start(
                        out=mixT[c * 128:(c + 1) * 128, c0:c0 + TT], in_=mo[a][:]), s_mo[a], [te])
                    sgp_rd = te
                    mi += 1
                pooled_rd = t_last_pm
            if dbg:
                ds_ = P.dsem("dbgk")
                for j in range(2):
                    P.dma("sp", lambda e, j=j: e.dma_start(out=dbg_k[j], in_=kT_g[j]), ds_, [t_cc[-1]])
            P.run(extra_final_waits=[t_cc[-1]])

    def phase_B(i, l):
        with ExitStack() as les:
            P = Phase(nc, les, f"B{i}_{nc.next_id()}")

            def lsb(name, shape, dt):
                return les.enter_context(nc.sbuf_tensor(f"{P.name}_{name}", shape, dt))

            def lps(name, shape, dt):
                return les.enter_context(nc.psum_tensor(f"{P.name}_{name}", shape, dt))

            kA = [[lsb(f"kA{b_}{m}", [128, S], BF16) for m in range(2)] for b_ in range(2)]
            vA = [lsb(f"vA{b_}", [128, 32, 129], BF16) for b_ in range(2)]
            qA = [[lsb(f"qA{s}{m}", [128, TT], BF16) for m in range(2)] for s in range(NS)]
            gA = [lsb(f"gA{j}", [128, 4, 128], BF16) for j in range(2)]
            pT = [lsb(f"pT{j}", [128, 2, TT], BF16) for j in range(2)]
            accs = lsb("accs", [128, 8, 129], F32)
            rinv = lsb("rinv", [128, 8], F32)
            o_sb = lsb("o_sb", [128, 4, 128], F32)
            t_sb = lsb("t_sb", [128, 128], F32)
            junk = lsb("junk", [128, 128], F32)
            ssq = lsb("ssq", [128, 4], F32)
            rs1 = lsb("rs1", [128, 4], F32)
            rs2 = lsb("rs2", [128, 4], F32)
            dt_ = lsb("dt_", [128, 4, 128], BF16)
            mixd = [lsb(f"mixd{j}", [128, TT], BF16) for j in range(2)]
            sc = [lps(f"sc{j}", [128, 2, 512], F32) for j in range(2)]
            accp = [lps(f"accp{j}", [128, 512], F32) for j in range(3)]
            trp = lps("trp", [128, 512], BF16)

            def acc_ap(m, ts):
                idx = m * 4 + ts
                return accp[idx // 3][:, (idx % 3) * 129:(idx % 3) * 129 + 129]

            s_c = P.dsem("const")
            s_k = [P.dsem(f"k{b_}") for b_ in range(2)]
            s_q = [P.dsem(f"q{s}") for s in range(NS)]
            s_g = [P.dsem(f"g{j}") for j in range(2)]
            s_o = [P.dsem(f"o{j}") for j in range(2)]
            for b_ in range(2):
                for m in range(2):
                    P.dma("sp", lambda e, b_=b_, m=m: e.dma_start(out=kA[b_][m][64:128, :], in_=khot_in), s_c)
            for s in range(NS):
                for m in range(2):
                    P.dma("sp", lambda e, s=s, m=m: e.dma_start(out=qA[s][m][64:128, :], in_=qmask_in[:, s * TT:(s + 1) * TT]), s_c)
            t_const = Tok(s_c, s_c.n)
            t_ones = None
            for b_ in range(2):
                t_ones = P.c("pool", lambda e, b_=b_: e.memset(vA[b_][:, :, 128:129], 1.0))

            kv_rd = [None, None]
            q_rd = [None] * NS
            g_rd = [None, None]
            sc_rd = [None, None]
            pT_rd = [None, None]
            stB = dict(acc_rd=None, accs_rd=None, trp_rd=None, dt_rd=None)
            mixd_rd = [None, None]
            kg_v = [kT_g[j].rearrange("(r n) c -> n r c", r=2) for j in range(2)]
            vg_v = [v_g[j].rearrange("(r b p) f -> r p b f", r=2, p=128) for j in range(2)]

            def load_kv(h):
                b_ = h % 2
                for m in range(2):
                    r0 = ((h % 4) * 2 + m) * 64
                    P.dma("sp", lambda e, b_=b_, m=m, r0=r0, h=h: e.dma_start(
                        out=kA[b_][m][0:64, :].rearrange("p (r c) -> p r c", r=2), in_=kg_v[h // 4][r0:r0 + 64, :, :]),
                        s_k[b_], [kv_rd[b_]])
                for r_ in range(2):
                    for hf in range(2):
                        b0 = r_ * 16 + hf * 8
                        P.dma("sp", lambda e, b_=b_, h=h, r_=r_, hf=hf, b0=b0: e.dma_start(
                            out=vA[b_][:, b0:b0 + 8, 0:128], in_=vg_v[hf][r_, :, :, h * 128:(h + 1) * 128]),
                            s_k[b_], [kv_rd[b_]])
                return Tok(s_k[b_], s_k[b_].n)

            iters = []
            for h in range(8):
                for s in range(NS):
                    nblk = 4 * (s + 1)
                    blocks = [r_ * 16 + k_ for r_ in range(2) for k_ in range(nblk)]
                    for bi_, kb in enumerate(blocks):
                        iters.append((h, s, kb, bi_ == 0, bi_ == len(blocks) - 1))
            N = len(iters)
            t_kv = {}
            t_qg = {}
            t_s_tok = {}
            pending = []
            gi_ = [0]

            def emit_loads(h, s):
                if s == 0:
                    if h == 0:
                        t_kv[0] = load_kv(0)
                    if h + 1 < 8:
                        t_kv[h + 1] = None
                c0 = s * TT
                for m in range(2):
                    r0 = (h * 2 + m) * 64
                    P.dma("sp", lambda e, s=s, m=m, r0=r0, c0=c0: e.dma_start(
                        out=qA[s][m][0:64, :], in_=qT[r0:r0 + 64, c0:c0 + TT]), s_q[s], [q_rd[s]])
                t_q = Tok(s_q[s], s_q[s].n)
                gj = gi_[0] % 2
                gi_[0] += 1
                t_g = P.dma("sp", lambda e, gj=gj, c0=c0, h=h: e.dma_start(
                    out=gA[gj][:], in_=gd[c0:c0 + TT, h * 128:(h + 1) * 128].rearrange("(t p) f -> p t f", p=128)),
                    s_g[gj], [g_rd[gj]])
                t_qg[(h, s)] = (t_q, t_g, gj)
                if s == 1 and h + 1 < 8:
                    t_kv[h + 1] = load_kv(h + 1)

            def emit_qk(n):
                h, s, kb, first_kb, last_kb = iters[n]
                if first_kb:
                    emit_loads(h, s)
                b_ = h % 2
                j = n % 2
                t_q = t_qg[(h, s)][0]
                t_s = None
                for m in range(2):
                    t_s = P.op("pe", lambda e, j=j, m=m, b_=b_, kb=kb, s=s: e.matmul(
                        sc[j][:, m, :], kA[b_][m][:, kb * 128:(kb + 1) * 128], qA[s][m][:, :], start=True, stop=True),
                        [t_kv[h], t_q, t_const, sc_rd[j]], sig=(P.own["pe"] if m == 1 else None))
                t_s_tok[n] = t_s
                if last_kb:
                    q_rd[s] = t_s

            def emit_rest(n):
                h, s, kb, first_kb, last_kb = iters[n]
                b_ = h % 2
                j = n % 2
                c0 = s * TT
                t_e = P.c("act", lambda e, j=j: e.activation(out=pT[j][:], in_=sc[j][:], func=AF.Exp, scale=0.125),
                          [t_s_tok[n], pT_rd[j]])
                sc_rd[j] = t_e
                t_pv = None
                for m in range(2):
                    for ts in range(4):
                        idx = m * 4 + ts
                        st_flag = first_kb and (idx % 3 == 0)
                        t_pv = P.op("pe", lambda e, j=j, m=m, ts=ts, b_=b_, kb=kb, st_flag=st_flag, last_kb=last_kb: e.matmul(
                            acc_ap(m, ts), pT[j][:, m, ts * 128:(ts + 1) * 128], vA[b_][:, kb, :],
                            start=st_flag, stop=last_kb, skip_group_check=True),
                            [t_e, t_ones, stB["acc_rd"] if first_kb else None],
                            sig=(P.own["pe"] if idx == 7 else None))
                pT_rd[j] = t_pv
                if not last_kb:
                    return
                if s == NS - 1:
                    kv_rd[b_] = t_pv
                t_q, t_g, gj = t_qg[(h, s)]
                tcp = None
                for bk in range(3):
                    nn = 3 if bk < 2 else 2
                    tcp = P.c("dve", lambda e, bk=bk, nn=nn: e.tensor_copy(
                        out=accs[:, bk * 3:bk * 3 + nn, :].rearrange("p a x -> p (a x)"), in_=accp[bk][:, 0:nn * 129]),
                        [t_pv, stB["accs_rd"]])
                stB["acc_rd"] = tcp
                t1 = P.c("dve", lambda e: e.reciprocal(out=rinv[:], in_=accs[:, :, 128]), [tcp])
                t1 = P.c("dve", lambda e: e.tensor_scalar(out=rinv[:, 4:8], in0=rinv[:, 4:8], scalar1=neglam[:, l:l + 1],
                                                          scalar2=None, op0=ALU.mult), [t1])
                tl_ = t1
                for ts in range(4):
                    ta_ = P.c("dve", lambda e, ts=ts: e.tensor_scalar(out=t_sb[:], in0=accs[:, 4 + ts, 0:128],
                                                                      scalar1=rinv[:, 4 + ts:5 + ts], scalar2=None, op0=ALU.mult),
                              [tl_, stB["dt_rd"] if ts == 0 else None])
                    tb2 = P.c("dve", lambda e, ts=ts: e.scalar_tensor_tensor(
                        out=o_sb[:, ts, :], in0=accs[:, ts, 0:128], scalar=rinv[:, ts:ts + 1], in1=t_sb[:],
                        op0=ALU.mult, op1=ALU.add), [ta_])
                    tl_ = P.c("dve", lambda e, ts=ts: e.scalar_tensor_tensor(
                        out=junk[:], in0=o_sb[:, ts, :], scalar=1.0, in1=o_sb[:, ts, :],
                        op0=ALU.mult, op1=ALU.mult, accum_out=ssq[:, ts:ts + 1]), [tb2])
                stB["accs_rd"] = tl_
                t2 = P.c("dve", lambda e: e.tensor_scalar(out=rs1[:], in0=ssq[:], scalar1=1.0 / 128.0, scalar2=EPS,
                                                          op0=ALU.mult, op1=ALU.add), [tl_])
                t3 = P.c("pool", lambda e: e.tensor_tensor(out=rs2[:], in0=rs1[:], in1=neghalf[:, 0:4], op=ALU.pow), [t2])
                td = None
                for ts in range(4):
                    td = P.c("dve", lambda e, ts=ts, gj=gj: e.scalar_tensor_tensor(
                        out=dt_[:, ts, :], in0=o_sb[:, ts, :], scalar=rs2[:, ts:ts + 1], in1=gA[gj][:, ts, :],
                        op0=ALU.mult, op1=ALU.mult), [t3, t_g, stB["dt_rd"]])
                g_rd[gj] = td
                mj = (h * NS + s) % 2

                def tail(td=td, mj=mj, h=h, c0=c0):
                    ttr = None
                    for ts in range(4):
                        ttr = P.op("pe", lambda e, ts=ts: e.transpose(trp[:, ts * 128:(ts + 1) * 128], dt_[:, ts, :], ident_bf[:]),
                                   [td, stB["trp_rd"]], sig=(P.own["pe"] if ts == 3 else None))
                    stB["dt_rd"] = ttr
                    tev = P.c("dve", lambda e: e.tensor_copy(out=mixd[mj][:], in_=trp[:]), [ttr, mixd_rd[mj]])
                    stB["trp_rd"] = tev
                    mixd_rd[mj] = P.dma("pool", lambda e: e.dma_start(
                        out=mixT[1024 + h * 128:1024 + (h + 1) * 128, c0:c0 + TT], in_=mixd[mj][:]), s_o[mj], [tev])
                pending.append((n + 4, tail))

            caster = Caster(P, lsb, i + 1, ("pool", "dve"))
            emit_qk(0)
            for n in range(N):
                if n + 1 < N:
                    emit_qk(n + 1)
                emit_rest(n)
                if n % 9 == 4:
                    caster.step()
                while pending and pending[0][0] <= n:
                    pending.pop(0)[1]()
            while pending:
                pending.pop(0)[1]()
            caster.finish()
            P.run()

    def phase_C(i, l, x_src, x_dst):
        with ExitStack() as les:
            P = Phase(nc, les, f"C{i}")

            def lsb(name, shape, dt):
                return les.enter_context(nc.sbuf_tensor(f"{P.name}_{name}", shape, dt))

            def lps(name, shape, dt):
                return les.enter_context(nc.psum_tensor(f"{P.name}_{name}", shape, dt))

            wo = lsb("wo", [128, KC, D], BF16)
            mx = [lsb(f"mx{j}", [128, KC, TT], BF16) for j in range(2)]
            yT = lsb("yT", [128, KC, TT], F32)
            ysq = [lsb(f"ysq{j}", [128, TT], BF16) for j in range(2)]
            xt = lsb("xt", [128, KC, TT], F32)
            rpre = lsb("rpre", [128, TT], F32)
            rstd = lsb("rstd", [128, TT], F32)
            tmp = [lsb(f"tmp{j}", [128, TT], F32) for j in range(2)]
            NXO = 4
            xo = [lsb(f"xo{j}", [128, TT], F32) for j in range(NXO)]
            acc = [lps(f"acc{j}", [128, 512], F32) for j in range(3)]
            ssb = lps("ssb", [128, 512], F32)
            s_w = P.dsem("w")
            s_m = [P.dsem(f"m{j}") for j in range(2)]
            s_x = P.dsem("x")
            s_o = [P.dsem(f"o{j}") for j in range(NXO)]
            t_w = P.dma("sp", lambda e: e.dma_start(out=wo[:], in_=wout_bf[i].rearrange("(kc p) n -> p kc n", p=128)),
                        s_w, [wcast_tok[i]])
            mix_v = mixT.rearrange("(kc p) t -> p kc t", p=128)
            x_v = x_src.rearrange("(kc p) t -> p kc t", p=128)
            xd_v = x_dst.rearrange("(kc p) t -> p kc t", p=128)
            mx_rd = [None, None]
            yT_rd = None
            ysq_rd = [None, None]
            acc_rd = [None] * 3
            ssb_rd = None
            xt_rd = None
            rstd_rd = None
            tmp_rd = [None, None]
            xo_rd = [None] * NXO
            ai = 0
            oi = 0
            t_mload = {}
            pend_c = []
            t_ss_box = [None]
            ssb_rd_box = [None]

            def load_m(s):
                if s >= NS or s in t_mload:
                    return
                j = s % 2
                t_mload[s] = P.dma("sp", lambda e, j=j, s=s: e.dma_start(out=mx[j][:], in_=mix_v[:, :, s * TT:(s + 1) * TT]),
                                   s_m[j], [mx_rd[j]])

            load_m(0)
            for s in range(NS):
                c0 = s * TT
                j = s % 2
                load_m(s + 1)
                t_m = t_mload[s]
                t_x = P.dma("sp", lambda e, c0=c0: e.dma_start(out=xt[:], in_=x_v[:, :, c0:c0 + TT]), s_x, [xt_rd])
                t_ss = None
                t_mm = None
                for oc in range(KC):
                    a = ai % 3
                    ai += 1
                    for kc in range(KC):
                        t_mm = P.op("pe", lambda e, a=a, kc=kc, oc=oc, j=j: e.matmul(
                            acc[a][:], wo[:, kc, oc * 128:(oc + 1) * 128], mx[j][:, kc, :], start=(kc == 0), stop=(kc == KC - 1)),
                            [t_w, t_m, acc_rd[a]], sig=(P.own["pe"] if kc == KC - 1 else None))
                    t_e = P.c("act", lambda e, a=a, oc=oc: e.activation(out=yT[:, oc, :], in_=acc[a][:], func=AF.Copy),
                              [t_mm, yT_rd if oc == 0 else None])
                    jj = oc % 2
                    t_q = P.c("act", lambda e, a=a, jj=jj: e.activation(out=ysq[jj][:], in_=acc[a][:], func=AF.Square),
                              [t_mm, ysq_rd[jj]])
                    acc_rd[a] = t_q
                    def ss_tail(jj=jj, oc=oc, t_q=t_q, s=s):
                        t = P.op("pe", lambda e: e.matmul(ssb[:], ones_bf[:], ysq[jj][:], start=(oc == 0), stop=(oc == KC - 1)),
                                 [t_q, ssb_rd_box[0] if oc == 0 else None], sig=P.own["pe"])
                        ysq_rd[jj] = t
                        t_ss_box[0] = t
                    if pend_c:
                        pend_c.pop(0)()
                    pend_c.append(ss_tail)
                while pend_c:
                    pend_c.pop(0)()
                t_ss = t_ss_box[0]
                mx_rd[j] = t_mm
                t_rp = P.c("dve", lambda e: e.tensor_scalar(out=rpre[:], in0=ssb[:], scalar1=1.0 / D, scalar2=EPS,
                                                            op0=ALU.mult, op1=ALU.add), [t_ss, rstd_rd])
                ssb_rd_box[0] = t_rp
                t_rs = P.c("pool", lambda e: e.tensor_tensor(out=rstd[:], in0=rpre[:], in1=neghalf[:], op=ALU.pow), [t_rp, rstd_rd])
                t_o = None
                for oc in range(KC):
                    jj = oi % 2
                    jx = oi % NXO
                    oi += 1
                    t_a = P.c("dve", lambda e, oc=oc, jj=jj: e.tensor_tensor(out=tmp[jj][:], in0=yT[:, oc, :], in1=rstd[:], op=ALU.mult),
                              [t_rs, t_e, tmp_rd[jj]])
                    t_b = P.c("dve", lambda e, oc=oc, jj=jj, jx=jx: e.scalar_tensor_tensor(
                        out=xo[jx][:], in0=tmp[jj][:], scalar=G2_all[:, l, oc:oc + 1], in1=xt[:, oc, :],
                        op0=ALU.mult, op1=ALU.add), [t_a, t_x, xo_rd[jx]])
                    tmp_rd[jj] = t_b
                    xo_rd[jx] = P.dma("pool" if oi % 2 else "sp", lambda e, oc=oc, jx=jx, c0=c0: e.dma_start(
                        out=xd_v[:, oc, c0:c0 + TT], in_=xo[jx][:]), s_o[jx], [t_b])
                    t_o = t_b
                yT_rd = t_o
                xt_rd = t_o
                rstd_rd = t_o
            P.run()

    prologue()
    for i, l in enumerate(layers):
        if stop == "pro":
            break
        x_src = xT_in if i == 0 else oT
        x_dst = oT
        phase_A(i, l, x_src)
        if stop == "A":
            break
        phase_A2(i, l)
        if stop == "P":
            break
        phase_B(i, l)
        for _ in range(dup_B):
            phase_B(i, l)
        if stop == "B":
            break
        phase_C(i, l, x_src, x_dst)
    es.close()
    return nc


def _bf(a):
    return np.asarray(a, dtype=np.float32).astype(ml_dtypes.bfloat16)


def _tok_idx(r):
    return np.concatenate([np.arange(t * TT, (t + 1) * TT) for t in TILES[r]])


def _static_tables():
    half = 32
    i = np.arange(128) % 64
    jf = (i % 32).astype(np.float32)
    inv = (np.float32(1.0) / (np.float32(10000.0) ** (np.arange(0, 64, 2, dtype=np.float32) / np.float32(64)))).astype(np.float32)
    invf = inv[(i % 32)].reshape(128, 1).astype(np.float32)
    sgn = np.where(i < half, -1.0, 1.0).astype(np.float32).reshape(128, 1)
    perm = np.zeros((128, 128), np.float32)
    for m in range(128):
        base = (m // 64) * 64
        im = m % 64
        partner = base + ((im + 32) % 64)
        perm[partner, m] = 1.0
    kchunk = np.concatenate([_tok_idx(0), _tok_idx(1)]) // 64
    khot = (np.arange(64)[:, None] == kchunk[None, :]).astype(np.float32)
    return invf, sgn, perm, khot


def _per_rank_tables(r):
    tok = _tok_idx(r)
    qchunk = tok // 64
    qmask = np.where(np.arange(64)[:, None] > qchunk[None, :], NEG, 0.0).astype(np.float32)
    hw = np.zeros((NS, 2, NS), np.float32)
    for s in range(NS):
        t = TILES[r][s]
        if t == 0:
            continue
        for r_ in range(2):
            if (t - 1) in TILES[r_]:
                hw[s, r_, TILES[r_].index(t - 1)] = 1.0
    invc = np.zeros((NS, 4, 16), np.float32)
    for s in range(NS):
        t0 = TILES[r][s] * TT
        for g, w in enumerate(WINS):
            cnt = np.minimum(t0 + np.arange(16) + 1, w).astype(np.float32)
            invc[s, g] = np.float32(1.0) / cnt
    return qmask, hw.reshape(-1), invc.reshape(-1)


def _rep(v):
    return np.ascontiguousarray(np.broadcast_to(np.asarray(v, np.float32).reshape(1, -1), (128, np.asarray(v).size)))


def _colT(a, nchunk):
    a = np.asarray(a, np.float32)
    L = a.shape[0]
    return np.ascontiguousarray(a.reshape(L, nchunk, 128).transpose(2, 0, 1).reshape(128, L * nchunk))


def _prepare(x, c, positions, w_ada, b_ada, g_pre, w_in, w_pool, pool_scale,
             lambda_q1, lambda_k1, lambda_q2, lambda_k2, subln_g, w_out, g_post):
    invf, sgn, perm, khot = _static_tables()
    c = np.asarray(c, np.float32)
    shared = dict(
        cT=np.ascontiguousarray(c.reshape(B, KC, 128).transpose(2, 1, 0).reshape(128, KC * B)),
        invf=invf, sgn=sgn, perm=_bf(perm), ones=_bf(np.ones((128, 128))), ident=_bf(np.eye(128)),
        khot=_bf(khot),
    )
    per_layer = dict(
        b_ada=np.asarray(b_ada, np.float32), g_pre=np.asarray(g_pre, np.float32),
        g_post=np.asarray(g_post, np.float32), pool_scale=np.asarray(pool_scale, np.float32),
        w_in=np.asarray(w_in, np.float32), w_out=np.asarray(w_out, np.float32),
        w_pool=np.asarray(w_pool, np.float32).reshape(DEPTH, 1024, 256),
        lq1=np.asarray(lambda_q1, np.float32), lk1=np.asarray(lambda_k1, np.float32),
        lq2=np.asarray(lambda_q2, np.float32), lk2=np.asarray(lambda_k2, np.float32),
        subln=np.asarray(subln_g, np.float32), w_ada=np.asarray(w_ada, np.float32),
    )
    per_core = []
    xT0 = []
    x = np.asarray(x, np.float32)
    positions = np.asarray(positions)
    for core in range(8):
        b, r = core // 2, core % 2
        tok = _tok_idx(r)
        qmask, hw, invc = _per_rank_tables(r)
        bsel = np.zeros((128, B), np.float32)
        bsel[:, b] = 1.0
        per_core.append(dict(
            bsel=bsel,
            posr=np.ascontiguousarray(np.broadcast_to(positions[b, tok].astype(np.int32)[None, :], (128, TO))),
            qmask=_bf(qmask), hw=_rep(hw), invc16=_rep(invc),
        ))
        xT0.append(np.ascontiguousarray(x[b, tok, :].T))
    return shared, per_layer, per_core, xT0


def _layer_maps(shared, per_layer, per_core, xT, layers):
    L = list(layers)
    pl = per_layer
    lamc = np.array([[-(0.8 - 0.6 * math.exp(-0.3 * l)), 1.0 - (0.8 - 0.6 * math.exp(-0.3 * l))] for l in L], np.float32)
    lay = dict(
        b_adaT=_colT(pl["b_ada"][L], 48), g_preT=_colT(pl["g_pre"][L], KC), g_postT=_colT(pl["g_post"][L], KC),
        pool_scaleT=_colT(pl["pool_scale"][L], 8),
        w_in=np.ascontiguousarray(pl["w_in"][L]), w_out=np.ascontiguousarray(pl["w_out"][L]),
        w_pool=np.ascontiguousarray(pl["w_pool"][L]),
        lq1r=_rep(pl["lq1"][L]), lk1r=_rep(pl["lk1"][L]), lq2r=_rep(pl["lq2"][L]), lk2r=_rep(pl["lk2"][L]),
        sublnr=_rep(pl["subln"][L]), lamc=_rep(lamc),
    )
    maps = []
    for core in range(8):
        m = dict(shared)
        m.update(lay)
        m.update(per_core[core])
        m["w_ada_sl"] = np.ascontiguousarray(pl["w_ada"][L][:, :, core * 768:(core + 1) * 768])
        m["xT"] = xT[core]
        maps.append(m)
    return maps


_PROG_CACHE = {}


def _get_prog(nl, dbg=False):
    key = (nl, dbg)
    if key not in _PROG_CACHE:
        _PROG_CACHE[key] = build_program(list(range(nl)), True, True, dbg=dbg)
    return _PROG_CACHE[key]


LAYERS_PER_LAUNCH = 4


def kernel(x, c, positions, w_ada, b_ada, g_pre, w_in, w_pool, pool_scale,
           lambda_q1, lambda_k1, lambda_q2, lambda_k2, subln_g, w_out, g_post):
    shared, per_layer, per_core, xT = _prepare(x, c, positions, w_ada, b_ada, g_pre, w_in, w_pool, pool_scale,
                                               lambda_q1, lambda_k1, lambda_q2, lambda_k2, subln_g, w_out, g_post)
    nl = LAYERS_PER_LAUNCH
    nc = _get_prog(nl)
    for l0 in range(0, DEPTH, nl):
        maps = _layer_maps(shared, per_layer, per_core, xT, range(l0, l0 + nl))
        res = run_bass_kernel_spmd(nc, maps, core_ids=list(range(8)))
        xT = [np.ascontiguousarray(res.results[core]["oT"]) for core in range(8)]
    out = np.empty((B, S, D), np.float32)
    for core in range(8):
        b, r = core // 2, core % 2
        out[b, _tok_idx(r), :] = xT[core].T
    return out
```

```python
import math
from contextlib import ExitStack

import numpy as np
import ml_dtypes

import concourse.bass as bass
import concourse.mybir as mybir
from concourse.bass_utils import run_bass_kernel_spmd

F32 = mybir.dt.float32
BF16 = mybir.dt.bfloat16
I32 = mybir.dt.int32
AF = mybir.ActivationFunctionType
ALU = mybir.AluOpType
AX = mybir.AxisListType

D = 2048
KC = 16
S = 4096
B = 4
DEPTH = 4
TT = 512
NS = 4
TO = NS * TT
DIN = 6144
EPS = 1e-6
NEG = -30000.0
TILES = ([0, 3, 4, 7], [1, 2, 5, 6])
WINS = (2, 4, 8, 16)
ENG = ("pe", "act", "dve", "pool", "sp")


class Sem:
    def __init__(self, h):
        self.h = h
        self.n = 0


class Tok:
    __slots__ = ("sem", "v")

    def __init__(self, sem, v):
        self.sem = sem
        self.v = v


_GLOBAL_SEMS = {}


def clear_sems(nc, sems):
    if not sems:
        return
    with nc.Block() as block:
        @block.gpsimd
        def _(e):
            for sm in sems:
                n = sm.h.num
                e.dma_reset(range(n, n + 1))
                e.sem_clear(sm.h)


class Phase:
    def __init__(self, nc, es, name):
        self.nc = nc
        self.es = es
        self.name = name
        self.q = {e: [] for e in ENG}
        self.waited = {e: {} for e in ENG}
        self.all_sems = []
        self.own = {e: self.sem("own_" + e) for e in ENG}
        self.dma_sems = []

    def sem(self, nm):
        sm = Sem(self.es.enter_context(self.nc.semaphore(f"{self.name}_{nm}")))
        self.all_sems.append(sm)
        return sm

    def dsem(self, nm):
        s = self.sem(nm)
        self.dma_sems.append(s)
        return s

    def op(self, eng, fn, waits=(), sig=None, inc=1):
        ws = []
        for t in waits:
            if t is None:
                continue
            cur = self.waited[eng].get(id(t.sem), 0)
            if t.v > cur:
                self.waited[eng][id(t.sem)] = t.v
                ws.append((t.sem.h, t.v))
        tok = None
        sh = None
        if sig is not None:
            sig.n += inc
            tok = Tok(sig, sig.n)
            sh = sig.h
        self.q[eng].append((ws, fn, sh, inc))
        return tok

    def c(self, eng, fn, waits=()):
        return self.op(eng, fn, waits, sig=self.own[eng], inc=1)

    def dma(self, eng, fn, sem, waits=()):
        return self.op(eng, fn, waits, sig=sem, inc=16)

    def run(self, extra_final_waits=()):
        fin = [Tok(s, s.n) for s in self.dma_sems if s.n > 0] + list(extra_final_waits)
        self.op("sp", None, fin)
        self.op("pool", None, fin)
        q = self.q
        pos = {e: 0 for e in ENG}
        val = {}
        prog = True
        while prog:
            prog = False
            for e in ENG:
                while pos[e] < len(q[e]):
                    ws, fn, sh, inc = q[e][pos[e]]
                    if all(val.get(id(h), _GLOBAL_SEMS.get(id(h), 0)) >= v for h, v in ws):
                        if sh is not None:
                            if id(sh) in _GLOBAL_SEMS:
                                _GLOBAL_SEMS[id(sh)] += inc
                            else:
                                val[id(sh)] = val.get(id(sh), 0) + inc
                        pos[e] += 1
                        prog = True
                    else:
                        break
        stuck = {e: (pos[e], len(q[e])) for e in ENG if pos[e] < len(q[e])}
        if stuck:
            raise RuntimeError(f"phase {self.name}: semaphore deadlock {stuck}")

        def replay(e, lst):
            for ws, fn, sh, inc in lst:
                for h, v in ws:
                    e.wait_ge(h, v)
                if fn is None:
                    continue
                ins = fn(e)
                if sh is not None:
                    ins.then_inc(sh, inc)

        clear_sems(self.nc, self.all_sems)

        with self.nc.Block() as block:
            @block.tensor
            def _(e):
                replay(e, q["pe"])

            @block.scalar
            def _(e):
                replay(e, q["act"])

            @block.vector
            def _(e):
                replay(e, q["dve"])

            @block.gpsimd
            def _(e):
                replay(e, q["pool"])

            @block.sync
            def _(e):
                replay(e, q["sp"])


def build_program(layers, first, last, dbg=False, stop=None, dup_B=0):
    nc = bass.Bass("TRN2", target_bir_lowering=False)
    es = ExitStack()

    def din(name, shape, dt):
        return nc.dram_tensor(name, shape, dt, kind="ExternalInput").ap()

    def dscr(name, shape, dt, out=False):
        if out:
            return nc.dram_tensor(name, shape, dt, kind="ExternalOutput").ap()
        return nc.dram_tensor(name, shape, dt).ap()

    NL = len(layers)
    NLP = NL
    assert list(layers) == list(range(NL))
    xT_in = din("xT", [D, TO], F32)
    cT_in = din("cT", [128, KC * B], F32)
    bsel_in = din("bsel", [128, B], F32)
    posr_in = din("posr", [128, TO], I32)
    invf_in = din("invf", [128, 1], F32)
    sgn_in = din("sgn", [128, 1], F32)
    perm_in = din("perm", [128, 128], BF16)
    ones_in = din("ones", [128, 128], BF16)
    ident_in = din("ident", [128, 128], BF16)
    khot_in = din("khot", [64, S], BF16)
    qmask_in = din("qmask", [64, TO], BF16)
    hw_in = din("hw", [128, NS * 8], F32)
    invc_in = din("invc16", [128, NS * 4 * 16], F32)
    wada_in = din("w_ada_sl", [NLP, D, 768], F32)
    bada_in = din("b_adaT", [128, NLP * 48], F32)
    gpre_in = din("g_preT", [128, NLP * KC], F32)
    gpost_in = din("g_postT", [128, NLP * KC], F32)
    pscale_in = din("pool_scaleT", [128, NLP * 8], F32)
    win_in = din("w_in", [NLP, D, DIN], F32)
    wout_in = din("w_out", [NLP, D, D], F32)
    wpool_in = din("w_pool", [NLP, 1024, 256], F32)
    lam_in = [din(n, [128, NLP * 64], F32) for n in ("lq1r", "lk1r", "lq2r", "lk2r")]
    subln_in = din("sublnr", [128, NLP * 128], F32)
    lamc_in = din("lamc", [128, NLP * 2], F32)
    oT = nc.dram_tensor("oT", [D, TO], F32, kind="ExternalOutput").ap()

    _win_bf = dscr("win_bf", [D, DIN], BF16)
    _wout_bf = [dscr(f"wout_bf{j}", [D, D], BF16) for j in range(2)]
    _wpool_bf = dscr("wpool_bf", [1024, 256], BF16)
    win_bf = [_win_bf for _ in layers]
    wout_bf = [_wout_bf[i % 2] for i in range(len(layers))]
    wpool_bf = [_wpool_bf for _ in layers]
    uT = dscr("uT", [1024, TO], F32, out=dbg)
    uh_loc = dscr("uh_loc", [1024, NS * 16], F32)
    uh_g = dscr("uh_g", [2 * 1024, NS * 16], F32)
    sgpT = dscr("sgpT", [1024, TO], BF16, out=dbg)
    qT = dscr("qT", [1024, TO], BF16, out=dbg)
    kT_loc = [dscr(f"kT_loc{j}", [512, TO], BF16) for j in range(2)]
    kT_g = [dscr(f"kT_g{j}", [2 * 512, TO], BF16) for j in range(2)]
    v_loc = [dscr(f"v_loc{j}", [TO // 2, 1024], BF16) for j in range(2)]
    v_g = [dscr(f"v_g{j}", [TO, 1024], BF16) for j in range(2)]
    gd = dscr("gd", [TO, 1024], BF16, out=dbg)
    mixT = dscr("mixT", [D, TO], BF16, out=dbg)
    mod_loc = [dscr(f"mod_loc{l}", [128, 24], F32) for l in range(NLP)]
    mod_g = [dscr(f"mod_g{l}", [8 * 128, 24], F32) for l in range(NLP)]
    if dbg:
        dbg_k = [dscr(f"dbg_k{j}", [2 * 512, TO], BF16, out=True) for j in range(2)]
        dbg_mod = dscr("dbg_mod", [128, NLP * 48], F32, out=True)
        dbg_tab = dscr("dbg_tab", [128, 2 * TO], F32, out=True)

    def sb(name, shape, dt):
        return es.enter_context(nc.sbuf_tensor("g_" + name, shape, dt))

    ones_bf = sb("ones_bf", [128, 128], BF16)
    perm_bf = sb("perm_bf", [128, 128], BF16)
    ident_bf = sb("ident_bf", [128, 128], BF16)
    cosT = sb("cosT", [128, TO], F32)
    sinT = sb("sinT", [128, TO], F32)
    modA = sb("modA", [128, NLP, 48], F32)
    A_all = sb("A_all", [128, NLP, KC], F32)
    G2_all = sb("G2_all", [128, NLP, KC], F32)
    gpre_sb = sb("gpre_sb", [128, NLP, KC], F32)
    gpost_sb = sb("gpost_sb", [128, NLP, KC], F32)
    pscale_sb = sb("pscale_sb", [128, NLP, 8], F32)
    neglam = sb("neglam", [128, NLP], F32)
    sublnG = sb("sublnG", [128, NLP, 512], F32)
    hw_sb = sb("hw_sb", [128, NS * 8], F32)
    invc_sb = sb("invc_sb", [128, NS, 4, 16], F32)
    neghalf = sb("neghalf", [128, 512], F32)

    wcast_tok = [None] * NL

    def cast_chunks(i):
        l = layers[i]
        out = []
        wi = win_in[l].rearrange("(kc p) n -> p kc n", p=128)
        wb_ = win_bf[i].rearrange("(kc p) n -> p kc n", p=128)
        for kc in range(KC):
            for t3 in range(3):
                out.append((wi[:, kc, t3 * 2048:(t3 + 1) * 2048], wb_[:, kc, t3 * 2048:(t3 + 1) * 2048]))
        wo_i = wout_in[l].rearrange("(kc p) n -> p kc n", p=128)
        wo_b = wout_bf[i].rearrange("(kc p) n -> p kc n", p=128)
        for kc in range(KC):
            out.append((wo_i[:, kc, :], wo_b[:, kc, :]))
        out.append((wpool_in[l].rearrange("(a p) d -> p a d", p=128), wpool_bf[i].rearrange("(a p) d -> p a d", p=128)))
        return out

    class Caster:
        def __init__(self, P, lsb, i, engines):
            self.P = P
            self.chunks = cast_chunks(i) if i < NL else []
            self.engines = engines
            self.k = 0
            if not self.chunks:
                return
            R = self.R = 3
            self.s32 = [lsb(f"cst32_{j}", [128, 2048], F32) for j in range(R)]
            self.s16 = [lsb(f"cst16_{j}", [128, 2048], BF16) for j in range(R)]
            self.sl = [P.dsem(f"cstl{j}") for j in range(R)]
            self.ss = [P.dsem(f"csts{j}") for j in range(R)]
            self.rd32 = [None] * R
            self.rd16 = [None] * R

        def step(self, n=1):
            P = self.P
            for _ in range(n):
                if self.k >= len(self.chunks):
                    return
                src, dst = self.chunks[self.k]
                j = self.k % self.R
                eng = self.engines[self.k % len(self.engines)]
                self.k += 1
                s32, s16 = self.s32[j], self.s16[j]
                is3 = len(src.shape) == 3
                o32 = s32[:].rearrange("p (a d) -> p a d", a=src.shape[1]) if is3 else s32[:]
                o16 = s16[:].rearrange("p (a d) -> p a d", a=src.shape[1]) if is3 else s16[:]
                t_l = P.dma("sp", lambda e, o32=o32, src=src: e.dma_start(out=o32, in_=src), self.sl[j], [self.rd32[j]])
                if eng == "act":
                    t_c = P.c(eng, lambda e, s32=s32, s16=s16: e.activation(out=s16[:], in_=s32[:], func=AF.Copy), [t_l, self.rd16[j]])
                else:
                    t_c = P.c(eng, lambda e, s32=s32, s16=s16: e.tensor_copy(out=s16[:], in_=s32[:]), [t_l, self.rd16[j]])
                self.rd32[j] = t_c
                self.rd16[j] = P.dma("sp", lambda e, o16=o16, dst=dst: e.dma_start(out=dst, in_=o16), self.ss[j], [t_c])

        def finish(self):
            self.step(len(self.chunks))

    def prologue():
        with ExitStack() as les:
            P = Phase(nc, les, "pro")

            def lsb(name, shape, dt):
                return les.enter_context(nc.sbuf_tensor(f"{P.name}_{name}", shape, dt))

            caster0 = Caster(P, lsb, 0, ("dve", "act", "pool"))

            ld = P.dsem("ld")
            posi = lsb("posi", [128, TO], I32)
            invf = lsb("invf", [128, 1], F32)
            sgn = lsb("sgn", [128, 1], F32)
            cT32 = lsb("cT32", [128, KC * B], F32)
            bsel = lsb("bsel", [128, B], F32)
            bada = lsb("bada", [128, NLP, 48], F32)
            lam_sb = [lsb(f"lam{i}", [128, NLP, 64], F32) for i in range(4)]
            subln_sb = lsb("subln_sb", [128, NLP, 128], F32)
            lamc_sb = lsb("lamc_sb", [128, NLP, 2], F32)
            loads = [
                (posi[:], posr_in), (invf[:], invf_in), (sgn[:], sgn_in), (cT32[:], cT_in),
                (bsel[:], bsel_in), (bada[:].rearrange("p l j -> p (l j)"), bada_in),
                (ones_bf[:], ones_in), (perm_bf[:], perm_in), (ident_bf[:], ident_in),
                (hw_sb[:], hw_in), (invc_sb[:].rearrange("p s g x -> p (s g x)"), invc_in),
                (gpre_sb[:].rearrange("p l j -> p (l j)"), gpre_in),
                (gpost_sb[:].rearrange("p l j -> p (l j)"), gpost_in),
                (pscale_sb[:].rearrange("p l j -> p (l j)"), pscale_in),
                (subln_sb[:].rearrange("p l j -> p (l j)"), subln_in),
                (lamc_sb[:].rearrange("p l j -> p (l j)"), lamc_in),
            ] + [(lam_sb[i][:].rearrange("p l j -> p (l j)"), lam_in[i]) for i in range(4)]
            for o_, i_ in loads:
                P.dma("sp", lambda e, o_=o_, i_=i_: e.dma_start(out=o_, in_=i_), ld)
            t_ld = Tok(ld, ld.n)

            posf = lsb("posf", [128, TO], F32)
            ang = lsb("ang", [128, TO], F32)
            kk_i = lsb("kk_i", [128, TO], I32)
            kk_f = lsb("kk_f", [128, TO], F32)
            red = lsb("red", [128, TO], F32)
            C1 = 6.28125
            C2 = 2.0 * math.pi - C1
            t = P.c("dve", lambda e: e.tensor_copy(out=posf[:], in_=posi[:]), [t_ld])
            t_ang = P.c("dve", lambda e: e.tensor_scalar(out=ang[:], in0=posf[:], scalar1=invf[:, 0:1],
                                                         scalar2=None, op0=ALU.mult), [t])
            t_prev_act = None
            for which, shift_, dst in ((0, 0.0, sinT), (1, math.pi / 2.0, cosT)):
                src = ang
                if which == 1:
                    t_ang2 = P.c("dve", lambda e, shift_=shift_: e.tensor_scalar(out=posf[:], in0=ang[:], scalar1=shift_,
                                                                  scalar2=None, op0=ALU.add), [t_ang, t_prev_act])
                    src = posf
                    t0 = t_ang2
                else:
                    t0 = t_ang
                t1 = P.c("dve", lambda e, src=src: e.tensor_scalar(out=kk_i[:], in0=src[:], scalar1=1.0 / (2.0 * math.pi),
                                                                   scalar2=None, op0=ALU.mult), [t0, t_prev_act])
                t2 = P.c("dve", lambda e: e.tensor_copy(out=kk_f[:], in_=kk_i[:]), [t1])
                t3 = P.c("dve", lambda e, src=src: e.scalar_tensor_tensor(out=red[:], in0=kk_f[:], scalar=-C1, in1=src[:],
                                                                          op0=ALU.mult, op1=ALU.add), [t2])
                t4 = P.c("dve", lambda e: e.scalar_tensor_tensor(out=red[:], in0=kk_f[:], scalar=-C2, in1=red[:],
                                                                 op0=ALU.mult, op1=ALU.add), [t3])
                t5 = P.c("dve", lambda e: e.tensor_scalar(out=red[:], in0=red[:], scalar1=-3.141592, scalar2=3.141592,
                                                          op0=ALU.max, op1=ALU.min), [t4])
                if which == 0:
                    t_prev_act = P.c("act", lambda e, dst=dst: e.activation(out=dst[:], in_=red[:], func=AF.Sin,
                                                                            scale=sgn[:, 0:1]), [t5])
                else:
                    t_prev_act = P.c("act", lambda e, dst=dst: e.activation(out=dst[:], in_=red[:], func=AF.Sin), [t5])
            t_tabs = t_prev_act

            lamt = lsb("lamt", [128, NLP, 64], F32)
            dots = lsb("dots", [128, 2, NLP], F32)
            exps = lsb("exps", [128, 2, NLP], F32)
            tl = None
            for j in range(2):
                ta = P.c("dve", lambda e, j=j: e.tensor_tensor(out=lamt[:], in0=lam_sb[2 * j][:], in1=lam_sb[2 * j + 1][:],
                                                               op=ALU.mult), [t_ld, tl])
                tl = P.c("dve", lambda e, j=j: e.tensor_reduce(out=dots[:, j, :], in_=lamt[:], axis=AX.X, op=ALU.add), [ta])
            te = P.c("act", lambda e: e.activation(out=exps[:], in_=dots[:], func=AF.Exp), [tl])
            tn = P.c("dve", lambda e: e.tensor_tensor(out=neglam[:], in0=exps[:, 1, :], in1=exps[:, 0, :], op=ALU.subtract), [te])
            for l in range(NLP):
                tn = P.c("dve", lambda e, l=l: e.tensor_scalar(out=neglam[:, l:l + 1], in0=neglam[:, l:l + 1],
                                                               scalar1=lamc_sb[:, l, 0:1], scalar2=None, op0=ALU.add), [tn, t_ld])
                for hh in range(4):
                    tn = P.c("dve", lambda e, l=l, hh=hh: e.tensor_scalar(
                        out=sublnG[:, l, hh * 128:(hh + 1) * 128], in0=subln_sb[:, l, :], scalar1=lamc_sb[:, l, 1:2],
                        scalar2=None, op0=ALU.mult), [tn, t_ld])
            P.c("pool", lambda e: e.memset(neghalf[:], -0.5))

            wada_sb = lsb("wada_sb", [128, KC, 768], F32)
            with nc.psum_tensor("modps", [128, 6 * B], F32) as modps:
                mod_sl = lsb("mod_sl", [128, NLP, 6, B], F32)
                wsem = P.dsem("wada")
                t_ev = None
                for l in range(NLP):
                    tw = P.dma("sp", lambda e, l=l: e.dma_start(
                        out=wada_sb[:], in_=wada_in[l].rearrange("(kc p) n -> p kc n", p=128)), wsem, [t_ev])
                    tm = None
                    for jc in range(6):
                        for kc in range(KC):
                            tm = P.op("pe", lambda e, jc=jc, kc=kc: e.matmul(
                                modps[:, jc * B:(jc + 1) * B], wada_sb[:, kc, jc * 128:(jc + 1) * 128],
                                cT32[:, kc * B:(kc + 1) * B],
                                start=(kc == 0), stop=(kc == KC - 1)),
                                [tw, t_ld, t_ev], sig=(P.own["pe"] if (jc == 5 and kc == KC - 1) else None))
                    t_ev = P.c("dve", lambda e, l=l: e.tensor_copy(
                        out=mod_sl[:, l, :, :].rearrange("p j b -> p (j b)"), in_=modps[:]), [tm])
                ms = P.dsem("modst")
                ccs = P.sem("cc")
                mod_all = lsb("mod_all", [128, 8, NLP, 6, B], F32)
                ml = P.dsem("modld")
                t_ml = None
                t_cc = None
                for l in range(NLP):
                    t_st = P.dma("sp", lambda e, l=l: e.dma_start(
                        out=mod_loc[l], in_=mod_sl[:, l, :, :].rearrange("p j b -> p (j b)")), ms, [t_ev])
                    t_cc = P.op("pool", lambda e, l=l: e.collective_compute(
                        "AllGather", ALU.bypass, replica_groups=[list(range(8))], ins=[mod_loc[l].opt()], outs=[mod_g[l].opt()]),
                        [t_st, t_cc], sig=ccs, inc=1)
                    P.op("pool", None, [t_cc])
                    t_ml = P.dma("sp", lambda e, l=l: e.dma_start(
                        out=mod_all[:, :, l, :, :].rearrange("p i j b -> p i (j b)"),
                        in_=mod_g[l].rearrange("(i p) c -> p i c", p=128)), ml, [t_cc])
                t_ml = Tok(ml, ml.n)
                tb_ = None
                for i8 in range(8):
                    for l in range(NLP):
                        dst = modA[:, l, i8 * 6:(i8 + 1) * 6]
                        tb_ = P.c("dve", lambda e, i8=i8, l=l, dst=dst: e.tensor_scalar(
                            out=dst, in0=mod_all[:, i8, l, :, 0], scalar1=bsel[:, 0:1], scalar2=None, op0=ALU.mult),
                            [t_ml, t_ld])
                        for b in range(1, B):
                            tb_ = P.c("dve", lambda e, i8=i8, l=l, dst=dst, b=b: e.scalar_tensor_tensor(
                                out=dst, in0=mod_all[:, i8, l, :, b], scalar=bsel[:, b:b + 1], in1=dst,
                                op0=ALU.mult, op1=ALU.add), [tb_])
                tb_ = P.c("dve", lambda e: e.tensor_tensor(out=modA[:], in0=modA[:], in1=bada[:], op=ALU.add), [tb_])
                tb_ = P.c("dve", lambda e: e.scalar_tensor_tensor(
                    out=A_all[:], in0=modA[:, :, 16:32], scalar=1.0, in1=gpre_sb[:], op0=ALU.add, op1=ALU.mult), [tb_])
                tb_ = P.c("dve", lambda e: e.scalar_tensor_tensor(
                    out=G2_all[:], in0=modA[:, :, 32:48], scalar=1.0, in1=gpost_sb[:], op0=ALU.add, op1=ALU.mult), [tb_])
                caster0.finish()
                fin = [tb_, tn, t_tabs]
                if dbg:
                    dsm = P.dsem("dbg")
                    P.dma("sp", lambda e: e.dma_start(out=dbg_mod, in_=modA[:].rearrange("p l j -> p (l j)")), dsm, [tb_])
                    P.dma("sp", lambda e: e.dma_start(out=dbg_tab[:, 0:TO], in_=cosT[:]), dsm, [t_tabs])
                    P.dma("sp", lambda e: e.dma_start(out=dbg_tab[:, TO:2 * TO], in_=sinT[:]), dsm, [t_tabs])
                P.op("dve", None, fin)
                P.run()

    def phase_A(i, l, x_src):
        with ExitStack() as les:
            P = Phase(nc, les, f"A{i}")

            def lsb(name, shape, dt):
                return les.enter_context(nc.sbuf_tensor(f"{P.name}_{name}", shape, dt))

            def lps(name, shape, dt):
                return les.enter_context(nc.psum_tensor(f"{P.name}_{name}", shape, dt))

            xt = lsb("xt", [128, KC, TT], F32)
            sqr = [lsb(f"sqr{j}", [128, TT], BF16) for j in range(2)]
            hT = [lsb(f"hT{j}", [128, KC, TT], BF16) for j in range(2)]
            wb = [lsb(f"wb{j}", [128, KC, 512], BF16) for j in range(3)]
            rpre = lsb("rpre", [128, TT], F32)
            rstd = lsb("rstd", [128, TT], F32)
            xr = [lsb(f"xr{j}", [128, TT], F32) for j in range(2)]
            NR = 4
            tf = [lsb(f"tf{j}", [128, TT], F32) for j in range(NR)]
            tb = [lsb(f"tb{j}", [128, TT], BF16) for j in range(NR)]
            qb = [lsb(f"qb{j}", [128, TT], BF16) for j in range(2)]
            r1 = [lsb(f"r1{j}", [128, TT], F32) for j in range(2)]
            r2 = [lsb(f"r2{j}", [128, TT], F32) for j in range(2)]
            acc = [lps(f"acc{j}", [128, 512], F32) for j in range(4)]
            ssb = lps("ssb", [128, 512], F32)
            pq = [lps(f"pq{j}", [128, 512], F32) for j in range(2)]

            s_xt = P.dsem("xt")
            s_wb = [P.dsem(f"wb{j}") for j in range(3)]
            s_tf = [P.dsem(f"tfo{j}") for j in range(NR)]
            s_th = [P.dsem(f"tho{j}") for j in range(NR)]
            s_tb = [P.dsem(f"tbo{j}") for j in range(NR)]

            st = dict(th_rd=[None] * NR,
                xt_rd=[], sqr_rd=[None, None], ssb_rd=None, rstd_rd=None, hT_rd=[None, None],
                xr_rd=[None, None], wb_rd=[None, None, None], acc_rd=[None] * 4,
                tf_rd=[None] * NR, tb_rd=[None] * NR, qb_rd=[None, None], pq_rd=[None, None],
                r1_rd=[None, None], r2_rd=[None, None], tfi=0, tbi=0, qi=0, ai=0,
            )
            win_v = win_bf[i].rearrange("(kc p) n -> p kc n", p=128)
            x_v = x_src.rearrange("(kc p) t -> p kc t", p=128)
            wloads = {}
            t_hT = {}
            pending_pe = []

            def issue_wload(gi):
                if gi >= NS * 12 or gi in wloads:
                    return
                g = gi % 12
                j = gi % 3
                wloads[gi] = P.dma("sp", lambda e, g=g, j=j: e.dma_start(
                    out=wb[j][:], in_=win_v[:, :, g * 512:(g + 1) * 512]), s_wb[j],
                    [st["wb_rd"][j], wcast_tok[i]])

            def emit_norm(s):
                c0 = s * TT
                hb = s % 2
                t_x = P.dma("sp", lambda e, c0=c0: e.dma_start(out=xt[:], in_=x_v[:, :, c0:c0 + TT]), s_xt, st["xt_rd"])
                st["xt_rd"] = []
                t_ss = None
                for kc in range(KC):
                    j = kc % 2
                    t_sq = P.c("act", lambda e, kc=kc, j=j: e.activation(out=sqr[j][:], in_=xt[:, kc, :], func=AF.Square),
                               [t_x, st["sqr_rd"][j]])
                    t_ss = P.op("pe", lambda e, kc=kc, j=j: e.matmul(ssb[:], ones_bf[:], sqr[j][:], start=(kc == 0), stop=(kc == KC - 1)),
                                [t_sq, st["ssb_rd"] if kc == 0 else None], sig=P.own["pe"])
                    st["sqr_rd"][j] = t_ss
                t_rp = P.c("dve", lambda e: e.tensor_scalar(out=rpre[:], in0=ssb[:], scalar1=1.0 / D, scalar2=EPS,
                                                            op0=ALU.mult, op1=ALU.add), [t_ss, st["rstd_rd"]])
                st["ssb_rd"] = t_rp
                t_ln = P.c("act", lambda e: e.activation(out=rpre[:], in_=rpre[:], func=AF.Ln), [t_rp])
                t_rs = P.c("act", lambda e: e.activation(out=rstd[:], in_=rpre[:], func=AF.Exp, scale=-0.5),
                           [t_ln, st["rstd_rd"]])
                st["ssb_rd"] = t_rs
                t_h = None
                for kc in range(KC):
                    j = kc % 2
                    t_xr = P.c("dve", lambda e, kc=kc, j=j: e.tensor_tensor(out=xr[j][:], in0=xt[:, kc, :], in1=rstd[:], op=ALU.mult),
                               [t_rs, t_x, st["xr_rd"][j]])
                    t_h = P.c("act", lambda e, kc=kc, j=j, hb=hb: e.activation(
                        out=hT[hb][:, kc, :], in_=xr[j][:], func=AF.Identity,
                        scale=A_all[:, l, kc:kc + 1], bias=modA[:, l, kc:kc + 1]),
                        [t_xr, st["hT_rd"][hb]])
                    st["xr_rd"][j] = t_h
                    if kc == KC - 1:
                        st["xt_rd"] = [t_xr, t_ss]
                        st["rstd_rd"] = t_xr
                t_hT[s] = t_h

            issue_wload(0)
            issue_wload(1)
            emit_norm(0)
            for s in range(NS):
                c0 = s * TT
                hb = s % 2
                t_h = t_hT[s]
                t_last_mm = None
                for g in range(12):
                    gi = s * 12 + g
                    issue_wload(gi)
                    issue_wload(gi + 1)
                    issue_wload(gi + 2)
                    j3 = gi % 3
                    t_w = wloads[gi]
                    t_mm_last_group = None
                    for sub in range(4):
                        a = st["ai"] % 4
                        st["ai"] += 1
                        t_mm = None
                        for kc in range(KC):
                            if g < 8:
                                fn = lambda e, a=a, j3=j3, kc=kc, sub=sub, hb=hb: e.matmul(
                                    acc[a][:], wb[j3][:, kc, sub * 128:(sub + 1) * 128], hT[hb][:, kc, :],
                                    start=(kc == 0), stop=(kc == KC - 1))
                            else:
                                fn = lambda e, a=a, j3=j3, kc=kc, sub=sub, hb=hb: e.matmul(
                                    acc[a][:], hT[hb][:, kc, sub * 128:(sub + 1) * 128], wb[j3][:, kc, :],
                                    start=(kc == 0), stop=(kc == KC - 1))
                            t_mm = P.op("pe", fn, [t_w, t_h, st["acc_rd"][a]],
                                        sig=(P.own["pe"] if kc == KC - 1 else None))
                        t_mm_last_group = t_mm
                        t_last_mm = t_mm
                        for fnp in pending_pe:
                            fnp()
                        pending_pe.clear()
                        if g < 2:
                            c = g * 4 + sub
                            k = st["tfi"] % NR
                            st["tfi"] += 1
                            t_e = P.c("act", lambda e, a=a, k=k: e.activation(out=tf[k][:], in_=acc[a][:], func=AF.Copy),
                                      [t_mm, st["tf_rd"][k], st["th_rd"][k]])
                            st["acc_rd"][a] = t_e
                            t_d1 = P.dma("pool", lambda e, k=k, c=c, c0=c0: e.dma_start(
                                out=uT[c * 128:(c + 1) * 128, c0:c0 + TT], in_=tf[k][:]), s_tf[k], [t_e])
                            t_d2 = P.dma("sp", lambda e, k=k, c=c, s=s: e.dma_start(
                                out=uh_loc[c * 128:(c + 1) * 128, s * 16:(s + 1) * 16], in_=tf[k][:, TT - 16:TT]), s_th[k], [t_e])
                            st["tf_rd"][k] = t_d1
                            st["th_rd"][k] = t_d2
                        elif g < 4:
                            c = (g - 2) * 4 + sub
                            k = st["tbi"] % NR
                            st["tbi"] += 1
                            t_e = P.c("act", lambda e, a=a, k=k: e.activation(out=tb[k][:], in_=acc[a][:], func=AF.Silu),
                                      [t_mm, st["tb_rd"][k]])
                            st["acc_rd"][a] = t_e
                            st["tb_rd"][k] = P.dma("pool", lambda e, k=k, c=c, c0=c0: e.dma_start(
                                out=sgpT[c * 128:(c + 1) * 128, c0:c0 + TT], in_=tb[k][:]), s_tb[k], [t_e])
                        elif g < 8:
                            c = (g - 4) * 4 + sub
                            jq = st["qi"] % 2
                            st["qi"] += 1
                            t_e = P.c("act", lambda e, a=a, jq=jq: e.activation(out=qb[jq][:], in_=acc[a][:], func=AF.Copy),
                                      [t_mm, st["qb_rd"][jq]])
                            st["acc_rd"][a] = t_e
                            k = st["tbi"] % NR
                            st["tbi"] += 1

                            def rope_tail(jq=jq, k=k, c=c, c0=c0, t_e=t_e):
                                t_p = P.op("pe", lambda e: e.matmul(pq[jq][:], perm_bf[:], qb[jq][:], start=True, stop=True),
                                           [t_e, st["pq_rd"][jq]], sig=P.own["pe"])
                                t_1 = P.c("pool", lambda e: e.tensor_tensor(
                                    out=r1[jq][:], in0=qb[jq][:], in1=cosT[:, c0:c0 + TT], op=ALU.mult), [t_e, st["r1_rd"][jq]])
                                t_2 = P.c("dve", lambda e: e.tensor_tensor(
                                    out=r2[jq][:], in0=pq[jq][:], in1=sinT[:, c0:c0 + TT], op=ALU.mult), [t_p, st["r2_rd"][jq]])
                                st["pq_rd"][jq] = t_2
                                t_3 = P.c("pool", lambda e: e.tensor_tensor(
                                    out=tb[k][:], in0=r1[jq][:], in1=r2[jq][:], op=ALU.add), [t_1, t_2, st["tb_rd"][k]])
                                st["qb_rd"][jq] = t_3
                                st["r1_rd"][jq] = t_3
                                st["r2_rd"][jq] = t_3
                                cc = c % 8
                                if c < 8:
                                    dst, rr = qT, cc * 128
                                else:
                                    dst, rr = kT_loc[cc // 4], (cc % 4) * 128
                                st["tb_rd"][k] = P.dma("pool", lambda e: e.dma_start(
                                    out=dst[rr:rr + 128, c0:c0 + TT], in_=tb[k][:]), s_tb[k], [t_3])
                            pending_pe.append(rope_tail)
                        elif g < 10:
                            k = st["tbi"] % NR
                            st["tbi"] += 1
                            t_e = P.c("act", lambda e, a=a, k=k: e.activation(out=tb[k][:], in_=acc[a][:], func=AF.Copy),
                                      [t_mm, st["tb_rd"][k]])
                            st["acc_rd"][a] = t_e
                            f0 = (g - 8) * 512
                            vr = (s % 2) * TT + sub * 128
                            st["tb_rd"][k] = P.dma("pool", lambda e, k=k, vr=vr, s=s, f0=f0: e.dma_start(
                                out=v_loc[s // 2][vr:vr + 128, f0:f0 + 512], in_=tb[k][:]), s_tb[k], [t_e])
                        else:
                            kf = st["tfi"] % NR
                            st["tfi"] += 1
                            t_e = P.c("act", lambda e, a=a, kf=kf: e.activation(out=tf[kf][:], in_=acc[a][:], func=AF.Silu),
                                      [t_mm, st["tf_rd"][kf], st["th_rd"][kf]])
                            st["acc_rd"][a] = t_e
                            k = st["tbi"] % NR
                            st["tbi"] += 1
                            t_m = P.c("dve", lambda e, kf=kf, k=k: e.tensor_tensor(
                                out=tb[k][:], in0=tf[kf][:], in1=sublnG[:, l, :], op=ALU.mult), [t_e, st["tb_rd"][k]])
                            st["tf_rd"][kf] = t_m
                            f0 = (g - 10) * 512
                            st["tb_rd"][k] = P.dma("pool", lambda e, k=k, c0=c0, sub=sub, f0=f0: e.dma_start(
                                out=gd[c0 + sub * 128:c0 + (sub + 1) * 128, f0:f0 + 512], in_=tb[k][:]), s_tb[k], [t_m])
                    st["wb_rd"][j3] = t_mm_last_group
                    if g == 5 and s + 1 < NS:
                        emit_norm(s + 1)
                st["hT_rd"][hb] = t_last_mm
            for fnp in pending_pe:
                fnp()
            pending_pe.clear()
            P.run()

    def phase_A2(i, l):
        with ExitStack() as les:
            P = Phase(nc, les, f"P{i}")

            def lsb(name, shape, dt):
                return les.enter_context(nc.sbuf_tensor(f"{P.name}_{name}", shape, dt))

            W = TT + 16
            ub = lsb("ub", [128, 8, W], F32)
            uhs = lsb("uhs", [128, 2, 8, NS * 16], F32)
            Ta = lsb("Ta", [128, W], F32)
            Tb = lsb("Tb", [128, W], F32)
            t16 = lsb("t16", [128, 16], F32)
            pooled = lsb("pooled", [128, 8, TT], BF16)
            sgp = lsb("sgp", [128, 8, TT], BF16)
            wp = lsb("wp", [128, 8, 256], BF16)
            mo = [lsb(f"mo{j}", [128, TT], BF16) for j in range(2)]
            pacc = [les.enter_context(nc.psum_tensor(f"{P.name}_pacc{j}", [128, 512], F32)) for j in range(2)]
            s_ld = P.dsem("ld")
            s_u = P.dsem("u")
            s_g = P.dsem("g")
            s_mo = [P.dsem(f"mo{j}") for j in range(2)]
            ccs = P.sem("cc")
            groups = [[0, 1], [2, 3], [4, 5], [6, 7]]
            t_cc = []
            for src, dst in ((uh_loc, uh_g), (kT_loc[0], kT_g[0]), (kT_loc[1], kT_g[1]), (v_loc[0], v_g[0]), (v_loc[1], v_g[1])):
                t_cc.append(P.op("pool", lambda e, src=src, dst=dst: e.collective_compute(
                    "AllGather", ALU.bypass, replica_groups=groups, ins=[src.opt()], outs=[dst.opt()]),
                    [t_cc[-1]] if t_cc else [], sig=ccs, inc=1))
                P.op("pool", None, [t_cc[-1]])
            t_w = P.dma("sp", lambda e: e.dma_start(out=wp[:], in_=wpool_bf[i].rearrange("(a p) d -> p a d", p=128)), s_ld, [wcast_tok[i]])
            t_h = P.dma("sp", lambda e: e.dma_start(
                out=uhs[:].rearrange("p r c x -> p (r c) x"),
                in_=uh_g.rearrange("(rc p) x -> p rc x", p=128)), s_ld, [t_cc[0]])
            t_ldc = Tok(s_ld, s_ld.n)
            ub_rd = []
            sgp_rd = None
            pooled_rd = None
            pacc_rd = [None, None]
            mo_rd = [None, None]
            tprev = None
            mi = 0
            for s in range(NS):
                c0 = s * TT
                t_u = P.dma("sp", lambda e, c0=c0: e.dma_start(
                    out=ub[:, :, 16:W], in_=uT.rearrange("(c p) t -> p c t", p=128)[:, :, c0:c0 + TT]), s_u, ub_rd)
                t_sg = P.dma("sp", lambda e, c0=c0: e.dma_start(
                    out=sgp[:], in_=sgpT.rearrange("(c p) t -> p c t", p=128)[:, :, c0:c0 + TT]), s_g, [sgp_rd])
                th = None
                first_ = True
                for r_ in range(2):
                    for s_ in range(NS):
                        idx = s * 8 + r_ * 4 + s_
                        src = uhs[:, r_, :, s_ * 16:(s_ + 1) * 16]
                        if first_:
                            th = P.c("dve", lambda e, src=src, idx=idx: e.tensor_scalar(
                                out=ub[:, :, 0:16], in0=src, scalar1=hw_sb[:, idx:idx + 1], scalar2=None, op0=ALU.mult),
                                [t_ldc] + ub_rd)
                            first_ = False
                        else:
                            th = P.c("dve", lambda e, src=src, idx=idx: e.scalar_tensor_tensor(
                                out=ub[:, :, 0:16], in0=src, scalar=hw_sb[:, idx:idx + 1], in1=ub[:, :, 0:16],
                                op0=ALU.mult, op1=ALU.add), [th])
                ub_rd = []
                tp = None
                for c in range(8):
                    g = c // 2
                    w = WINS[g]
                    u = ub[:, c, :]
                    cur, off = u, 0
                    bufs = [Ta, Tb]
                    bi = 0
                    sh = 1
                    tt_ = None
                    while sh < w:
                        o = bufs[bi]
                        lo = 2 * sh - 1
                        tt_ = P.c("dve", lambda e, o=o, cur=cur, lo=lo, sh=sh: e.tensor_tensor(
                            out=o[:, lo:W], in0=cur[:, lo:W], in1=cur[:, lo - sh:W - sh], op=ALU.add),
                            [t_u, th, tt_, tp, pooled_rd if c == 0 else None])
                        cur = o
                        bi ^= 1
                        sh *= 2
                    tp = P.c("dve", lambda e, c=c, cur=cur, w=w: e.scalar_tensor_tensor(
                        out=pooled[:, c, :], in0=cur[:, 16:W], scalar=1.0 / w, in1=ub[:, c, 16:W],
                        op0=ALU.mult, op1=ALU.subtract), [tt_])
                    tq = P.c("dve", lambda e, cur=cur, s=s, g=g: e.tensor_tensor(
                        out=t16[:], in0=cur[:, 16:32], in1=invc_sb[:, s, g, :], op=ALU.mult), [tp])
                    tp = P.c("dve", lambda e, c=c: e.tensor_tensor(
                        out=pooled[:, c, 0:16], in0=t16[:], in1=ub[:, c, 16:32], op=ALU.subtract), [tq])
                ub_rd = [tp]
                t_last_pm = None
                for c in range(8):
                    g, dc = c // 2, c % 2
                    a = mi % 2
                    tm = None
                    for cc in range(2):
                        tm = P.op("pe", lambda e, a=a, g=g, dc=dc, cc=cc: e.matmul(
                            pacc[a][:], wp[:, g * 2 + cc, dc * 128:(dc + 1) * 128], pooled[:, g * 2 + cc, :],
                            start=(cc == 0), stop=(cc == 1)), [tp, t_ldc, pacc_rd[a]],
                            sig=(P.own["pe"] if cc == 1 else None))
                    t_last_pm = tm
                    te = P.c("dve", lambda e, a=a, c=c: e.scalar_tensor_tensor(
                        out=mo[a][:], in0=pacc[a][:], scalar=pscale_sb[:, l, c:c + 1], in1=sgp[:, c, :],
                        op0=ALU.mult, op1=ALU.mult), [tm, t_sg, mo_rd[a]])
                    pacc_rd[a] = te
                    mo_rd[a] = P.dma("sp", lambda e, a=a, c=c, c0=c0: e.dma_start(
                        out=mixT[c * 128:(c + 1) * 128, c0:c0 + TT], in_=mo[a][:]), s_mo[a], [te])
                    sgp_rd = te
                    mi += 1
                pooled_rd = t_last_pm
            if dbg:
                ds_ = P.dsem("dbgk")
                for j in range(2):
                    P.dma("sp", lambda e, j=j: e.dma_start(out=dbg_k[j], in_=kT_g[j]), ds_, [t_cc[-1]])
            P.run(extra_final_waits=[t_cc[-1]])

    def phase_B(i, l):
        with ExitStack() as les:
            P = Phase(nc, les, f"B{i}_{nc.next_id()}")

            def lsb(name, shape, dt):
                return les.enter_context(nc.sbuf_tensor(f"{P.name}_{name}", shape, dt))

            def lps(name, shape, dt):
                return les.enter_context(nc.psum_tensor(f"{P.name}_{name}", shape, dt))

            kA = [[lsb(f"kA{b_}{m}", [128, S], BF16) for m in range(2)] for b_ in range(2)]
            vA = [lsb(f"vA{b_}", [128, 32, 129], BF16) for b_ in range(2)]
            qA = [[lsb(f"qA{s}{m}", [128, TT], BF16) for m in range(2)] for s in range(NS)]
            gA = [lsb(f"gA{j}", [128, 4, 128], BF16) for j in range(2)]
            pT = [lsb(f"pT{j}", [128, 2, TT], BF16) for j in range(2)]
            accs = lsb("accs", [128, 8, 129], F32)
            rinv = lsb("rinv", [128, 8], F32)
            o_sb = lsb("o_sb", [128, 4, 128], F32)
            t_sb = lsb("t_sb", [128, 128], F32)
            junk = lsb("junk", [128, 128], F32)
            ssq = lsb("ssq", [128, 4], F32)
            rs1 = lsb("rs1", [128, 4], F32)
            rs2 = lsb("rs2", [128, 4], F32)
            dt_ = lsb("dt_", [128, 4, 128], BF16)
            mixd = [lsb(f"mixd{j}", [128, TT], BF16) for j in range(2)]
            sc = [lps(f"sc{j}", [128, 2, 512], F32) for j in range(2)]
            accp = [lps(f"accp{j}", [128, 512], F32) for j in range(3)]
            trp = lps("trp", [128, 512], BF16)

            def acc_ap(m, ts):
                idx = m * 4 + ts
                return accp[idx // 3][:, (idx % 3) * 129:(idx % 3) * 129 + 129]

            s_c = P.dsem("const")
            s_k = [P.dsem(f"k{b_}") for b_ in range(2)]
            s_q = [P.dsem(f"q{s}") for s in range(NS)]
            s_g = [P.dsem(f"g{j}") for j in range(2)]
            s_o = [P.dsem(f"o{j}") for j in range(2)]
            for b_ in range(2):
                for m in range(2):
                    P.dma("sp", lambda e, b_=b_, m=m: e.dma_start(out=kA[b_][m][64:128, :], in_=khot_in), s_c)
            for s in range(NS):
                for m in range(2):
                    P.dma("sp", lambda e, s=s, m=m: e.dma_start(out=qA[s][m][64:128, :], in_=qmask_in[:, s * TT:(s + 1) * TT]), s_c)
            t_const = Tok(s_c, s_c.n)
            t_ones = None
            for b_ in range(2):
                t_ones = P.c("pool", lambda e, b_=b_: e.memset(vA[b_][:, :, 128:129], 1.0))

            kv_rd = [None, None]
            q_rd = [None] * NS
            g_rd = [None, None]
            sc_rd = [None, None]
            pT_rd = [None, None]
            stB = dict(acc_rd=None, accs_rd=None, trp_rd=None, dt_rd=None)
            mixd_rd = [None, None]
            kg_v = [kT_g[j].rearrange("(r n) c -> n r c", r=2) for j in range(2)]
            vg_v = [v_g[j].rearrange("(r b p) f -> r p b f", r=2, p=128) for j in range(2)]

            def load_kv(h):
                b_ = h % 2
                for m in range(2):
                    r0 = ((h % 4) * 2 + m) * 64
                    P.dma("sp", lambda e, b_=b_, m=m, r0=r0, h=h: e.dma_start(
                        out=kA[b_][m][0:64, :].rearrange("p (r c) -> p r c", r=2), in_=kg_v[h // 4][r0:r0 + 64, :, :]),
                        s_k[b_], [kv_rd[b_]])
                for r_ in range(2):
                    for hf in range(2):
                        b0 = r_ * 16 + hf * 8
                        P.dma("sp", lambda e, b_=b_, h=h, r_=r_, hf=hf, b0=b0: e.dma_start(
                            out=vA[b_][:, b0:b0 + 8, 0:128], in_=vg_v[hf][r_, :, :, h * 128:(h + 1) * 128]),
                            s_k[b_], [kv_rd[b_]])
                return Tok(s_k[b_], s_k[b_].n)

            iters = []
            for h in range(8):
                for s in range(NS):
                    nblk = 4 * (s + 1)
                    blocks = [r_ * 16 + k_ for r_ in range(2) for k_ in range(nblk)]
                    for bi_, kb in enumerate(blocks):
                        iters.append((h, s, kb, bi_ == 0, bi_ == len(blocks) - 1))
            N = len(iters)
            t_kv = {}
            t_qg = {}
            t_s_tok = {}
            pending = []
            gi_ = [0]

            def emit_loads(h, s):
                if s == 0:
                    if h == 0:
                        t_kv[0] = load_kv(0)
                    if h + 1 < 8:
                        t_kv[h + 1] = None
                c0 = s * TT
                for m in range(2):
                    r0 = (h * 2 + m) * 64
                    P.dma("sp", lambda e, s=s, m=m, r0=r0, c0=c0: e.dma_start(
                        out=qA[s][m][0:64, :], in_=qT[r0:r0 + 64, c0:c0 + TT]), s_q[s], [q_rd[s]])
                t_q = Tok(s_q[s], s_q[s].n)
                gj = gi_[0] % 2
                gi_[0] += 1
                t_g = P.dma("sp", lambda e, gj=gj, c0=c0, h=h: e.dma_start(
                    out=gA[gj][:], in_=gd[c0:c0 + TT, h * 128:(h + 1) * 128].rearrange("(t p) f -> p t f", p=128)),
                    s_g[gj], [g_rd[gj]])
                t_qg[(h, s)] = (t_q, t_g, gj)
                if s == 1 and h + 1 < 8:
                    t_kv[h + 1] = load_kv(h + 1)

            def emit_qk(n):
                h, s, kb, first_kb, last_kb = iters[n]
                if first_kb:
                    emit_loads(h, s)
                b_ = h % 2
                j = n % 2
                t_q = t_qg[(h, s)][0]
                t_s = None
                for m in range(2):
                    t_s = P.op("pe", lambda e, j=j, m=m, b_=b_, kb=kb, s=s: e.matmul(
                        sc[j][:, m, :], kA[b_][m][:, kb * 128:(kb + 1) * 128], qA[s][m][:, :], start=True, stop=True),
                        [t_kv[h], t_q, t_const, sc_rd[j]], sig=(P.own["pe"] if m == 1 else None))
                t_s_tok[n] = t_s
                if last_kb:
                    q_rd[s] = t_s

            def emit_rest(n):
                h, s, kb, first_kb, last_kb = iters[n]
                b_ = h % 2
                j = n % 2
                c0 = s * TT
                t_e = P.c("act", lambda e, j=j: e.activation(out=pT[j][:], in_=sc[j][:], func=AF.Exp, scale=0.125),
                          [t_s_tok[n], pT_rd[j]])
                sc_rd[j] = t_e
                t_pv = None
                for m in range(2):
                    for ts in range(4):
                        idx = m * 4 + ts
                        st_flag = first_kb and (idx % 3 == 0)
                        t_pv = P.op("pe", lambda e, j=j, m=m, ts=ts, b_=b_, kb=kb, st_flag=st_flag, last_kb=last_kb: e.matmul(
                            acc_ap(m, ts), pT[j][:, m, ts * 128:(ts + 1) * 128], vA[b_][:, kb, :],
                            start=st_flag, stop=last_kb, skip_group_check=True),
                            [t_e, t_ones, stB["acc_rd"] if first_kb else None],
                            sig=(P.own["pe"] if idx == 7 else None))
                pT_rd[j] = t_pv
                if not last_kb:
                    return
                if s == NS - 1:
                    kv_rd[b_] = t_pv
                t_q, t_g, gj = t_qg[(h, s)]
                tcp = None
                for bk in range(3):
                    nn = 3 if bk < 2 else 2
                    tcp = P.c("dve", lambda e, bk=bk, nn=nn: e.tensor_copy(
                        out=accs[:, bk * 3:bk * 3 + nn, :].rearrange("p a x -> p (a x)"), in_=accp[bk][:, 0:nn * 129]),
                        [t_pv, stB["accs_rd"]])
                stB["acc_rd"] = tcp
                t1 = P.c("dve", lambda e: e.reciprocal(out=rinv[:], in_=accs[:, :, 128]), [tcp])
                t1 = P.c("dve", lambda e: e.tensor_scalar(out=rinv[:, 4:8], in0=rinv[:, 4:8], scalar1=neglam[:, l:l + 1],
                                                          scalar2=None, op0=ALU.mult), [t1])
                tl_ = t1
                for ts in range(4):
                    ta_ = P.c("dve", lambda e, ts=ts: e.tensor_scalar(out=t_sb[:], in0=accs[:, 4 + ts, 0:128],
                                                                      scalar1=rinv[:, 4 + ts:5 + ts], scalar2=None, op0=ALU.mult),
                              [tl_, stB["dt_rd"] if ts == 0 else None])
                    tb2 = P.c("dve", lambda e, ts=ts: e.scalar_tensor_tensor(
                        out=o_sb[:, ts, :], in0=accs[:, ts, 0:128], scalar=rinv[:, ts:ts + 1], in1=t_sb[:],
                        op0=ALU.mult, op1=ALU.add), [ta_])
                    tl_ = P.c("dve", lambda e, ts=ts: e.scalar_tensor_tensor(
                        out=junk[:], in0=o_sb[:, ts, :], scalar=1.0, in1=o_sb[:, ts, :],
                        op0=ALU.mult, op1=ALU.mult, accum_out=ssq[:, ts:ts + 1]), [tb2])
                stB["accs_rd"] = tl_
                t2 = P.c("dve", lambda e: e.tensor_scalar(out=rs1[:], in0=ssq[:], scalar1=1.0 / 128.0, scalar2=EPS,
                                                          op0=ALU.mult, op1=ALU.add), [tl_])
                t3 = P.c("pool", lambda e: e.tensor_tensor(out=rs2[:], in0=rs1[:], in1=neghalf[:, 0:4], op=ALU.pow), [t2])
                td = None
                for ts in range(4):
                    td = P.c("dve", lambda e, ts=ts, gj=gj: e.scalar_tensor_tensor(
                        out=dt_[:, ts, :], in0=o_sb[:, ts, :], scalar=rs2[:, ts:ts + 1], in1=gA[gj][:, ts, :],
                        op0=ALU.mult, op1=ALU.mult), [t3, t_g, stB["dt_rd"]])
                g_rd[gj] = td
                mj = (h * NS + s) % 2

                def tail(td=td, mj=mj, h=h, c0=c0):
                    ttr = None
                    for ts in range(4):
                        ttr = P.op("pe", lambda e, ts=ts: e.transpose(trp[:, ts * 128:(ts + 1) * 128], dt_[:, ts, :], ident_bf[:]),
                                   [td, stB["trp_rd"]], sig=(P.own["pe"] if ts == 3 else None))
                    stB["dt_rd"] = ttr
                    tev = P.c("dve", lambda e: e.tensor_copy(out=mixd[mj][:], in_=trp[:]), [ttr, mixd_rd[mj]])
                    stB["trp_rd"] = tev
                    mixd_rd[mj] = P.dma("pool", lambda e: e.dma_start(
                        out=mixT[1024 + h * 128:1024 + (h + 1) * 128, c0:c0 + TT], in_=mixd[mj][:]), s_o[mj], [tev])
                pending.append((n + 4, tail))

            caster = Caster(P, lsb, i + 1, ("pool", "dve"))
            emit_qk(0)
            for n in range(N):
                if n + 1 < N:
                    emit_qk(n + 1)
                emit_rest(n)
                if n % 9 == 4:
                    caster.step()
                while pending and pending[0][0] <= n:
                    pending.pop(0)[1]()
            while pending:
                pending.pop(0)[1]()
            caster.finish()
            P.run()

    def phase_C(i, l, x_src, x_dst):
        with ExitStack() as les:
            P = Phase(nc, les, f"C{i}")

            def lsb(name, shape, dt):
                return les.enter_context(nc.sbuf_tensor(f"{P.name}_{name}", shape, dt))

            def lps(name, shape, dt):
                return les.enter_context(nc.psum_tensor(f"{P.name}_{name}", shape, dt))

            wo = lsb("wo", [128, KC, D], BF16)
            mx = [lsb(f"mx{j}", [128, KC, TT], BF16) for j in range(2)]
            yT = lsb("yT", [128, KC, TT], F32)
            ysq = [lsb(f"ysq{j}", [128, TT], BF16) for j in range(2)]
            xt = lsb("xt", [128, KC, TT], F32)
            rpre = lsb("rpre", [128, TT], F32)
            rstd = lsb("rstd", [128, TT], F32)
            tmp = [lsb(f"tmp{j}", [128, TT], F32) for j in range(2)]
            NXO = 4
            xo = [lsb(f"xo{j}", [128, TT], F32) for j in range(NXO)]
            acc = [lps(f"acc{j}", [128, 512], F32) for j in range(3)]
            ssb = lps("ssb", [128, 512], F32)
            s_w = P.dsem("w")
            s_m = [P.dsem(f"m{j}") for j in range(2)]
            s_x = P.dsem("x")
            s_o = [P.dsem(f"o{j}") for j in range(NXO)]
            t_w = P.dma("sp", lambda e: e.dma_start(out=wo[:], in_=wout_bf[i].rearrange("(kc p) n -> p kc n", p=128)),
                        s_w, [wcast_tok[i]])
            mix_v = mixT.rearrange("(kc p) t -> p kc t", p=128)
            x_v = x_src.rearrange("(kc p) t -> p kc t", p=128)
            xd_v = x_dst.rearrange("(kc p) t -> p kc t", p=128)
            mx_rd = [None, None]
            yT_rd = None
            ysq_rd = [None, None]
            acc_rd = [None] * 3
            ssb_rd = None
            xt_rd = None
            rstd_rd = None
            tmp_rd = [None, None]
            xo_rd = [None] * NXO
            ai = 0
            oi = 0
            t_mload = {}
            pend_c = []
            t_ss_box = [None]
            ssb_rd_box = [None]

            def load_m(s):
                if s >= NS or s in t_mload:
                    return
                j = s % 2
                t_mload[s] = P.dma("sp", lambda e, j=j, s=s: e.dma_start(out=mx[j][:], in_=mix_v[:, :, s * TT:(s + 1) * TT]),
                                   s_m[j], [mx_rd[j]])

            load_m(0)
            for s in range(NS):
                c0 = s * TT
                j = s % 2
                load_m(s + 1)
                t_m = t_mload[s]
                t_x = P.dma("sp", lambda e, c0=c0: e.dma_start(out=xt[:], in_=x_v[:, :, c0:c0 + TT]), s_x, [xt_rd])
                t_ss = None
                t_mm = None
                for oc in range(KC):
                    a = ai % 3
                    ai += 1
                    for kc in range(KC):
                        t_mm = P.op("pe", lambda e, a=a, kc=kc, oc=oc, j=j: e.matmul(
                            acc[a][:], wo[:, kc, oc * 128:(oc + 1) * 128], mx[j][:, kc, :], start=(kc == 0), stop=(kc == KC - 1)),
                            [t_w, t_m, acc_rd[a]], sig=(P.own["pe"] if kc == KC - 1 else None))
                    t_e = P.c("act", lambda e, a=a, oc=oc: e.activation(out=yT[:, oc, :], in_=acc[a][:], func=AF.Copy),
                              [t_mm, yT_rd if oc == 0 else None])
                    jj = oc % 2
                    t_q = P.c("act", lambda e, a=a, jj=jj: e.activation(out=ysq[jj][:], in_=acc[a][:], func=AF.Square),
                              [t_mm, ysq_rd[jj]])
                    acc_rd[a] = t_q
                    def ss_tail(jj=jj, oc=oc, t_q=t_q, s=s):
                        t = P.op("pe", lambda e: e.matmul(ssb[:], ones_bf[:], ysq[jj][:], start=(oc == 0), stop=(oc == KC - 1)),
                                 [t_q, ssb_rd_box[0] if oc == 0 else None], sig=P.own["pe"])
                        ysq_rd[jj] = t
                        t_ss_box[0] = t
                    if pend_c:
                        pend_c.pop(0)()
                    pend_c.append(ss_tail)
                while pend_c:
                    pend_c.pop(0)()
                t_ss = t_ss_box[0]
                mx_rd[j] = t_mm
                t_rp = P.c("dve", lambda e: e.tensor_scalar(out=rpre[:], in0=ssb[:], scalar1=1.0 / D, scalar2=EPS,
                                                            op0=ALU.mult, op1=ALU.add), [t_ss, rstd_rd])
                ssb_rd_box[0] = t_rp
                t_ln = P.c("act", lambda e: e.activation(out=rpre[:], in_=rpre[:], func=AF.Ln), [t_rp])
                t_rs = P.c("act", lambda e: e.activation(out=rstd[:], in_=rpre[:], func=AF.Exp, scale=-0.5), [t_ln, rstd_rd])
                t_o = None
                for oc in range(KC):
                    jj = oi % 2
                    jx = oi % NXO
                    oi += 1
                    t_a = P.c("dve", lambda e, oc=oc, jj=jj: e.tensor_tensor(out=tmp[jj][:], in0=yT[:, oc, :], in1=rstd[:], op=ALU.mult),
                              [t_rs, t_e, tmp_rd[jj]])
                    t_b = P.c("dve", lambda e, oc=oc, jj=jj, jx=jx: e.scalar_tensor_tensor(
                        out=xo[jx][:], in0=tmp[jj][:], scalar=G2_all[:, l, oc:oc + 1], in1=xt[:, oc, :],
                        op0=ALU.mult, op1=ALU.add), [t_a, t_x, xo_rd[jx]])
                    tmp_rd[jj] = t_b
                    xo_rd[jx] = P.dma("pool" if oi % 2 else "sp", lambda e, oc=oc, jx=jx, c0=c0: e.dma_start(
                        out=xd_v[:, oc, c0:c0 + TT], in_=xo[jx][:]), s_o[jx], [t_b])
                    t_o = t_b
                yT_rd = t_o
                xt_rd = t_o
                rstd_rd = t_o
            P.run()

    prologue()
    for i, l in enumerate(layers):
        if stop == "pro":
            break
        x_src = xT_in if i == 0 else oT
        x_dst = oT
        phase_A(i, l, x_src)
        if stop == "A":
            break
        phase_A2(i, l)
        if stop == "P":
            break
        phase_B(i, l)
        for _ in range(dup_B):
            phase_B(i, l)
        if stop == "B":
            break
        phase_C(i, l, x_src, x_dst)
    es.close()
    return nc


def _bf(a):
    return np.asarray(a, dtype=np.float32).astype(ml_dtypes.bfloat16)


def _tok_idx(r):
    return np.concatenate([np.arange(t * TT, (t + 1) * TT) for t in TILES[r]])


def _static_tables():
    half = 32
    i = np.arange(128) % 64
    jf = (i % 32).astype(np.float32)
    inv = (np.float32(1.0) / (np.float32(10000.0) ** (np.arange(0, 64, 2, dtype=np.float32) / np.float32(64)))).astype(np.float32)
    invf = inv[(i % 32)].reshape(128, 1).astype(np.float32)
    sgn = np.where(i < half, -1.0, 1.0).astype(np.float32).reshape(128, 1)
    perm = np.zeros((128, 128), np.float32)
    for m in range(128):
        base = (m // 64) * 64
        im = m % 64
        partner = base + ((im + 32) % 64)
        perm[partner, m] = 1.0
    kchunk = np.concatenate([_tok_idx(0), _tok_idx(1)]) // 64
    khot = (np.arange(64)[:, None] == kchunk[None, :]).astype(np.float32)
    return invf, sgn, perm, khot


def _per_rank_tables(r):
    tok = _tok_idx(r)
    qchunk = tok // 64
    qmask = np.where(np.arange(64)[:, None] > qchunk[None, :], NEG, 0.0).astype(np.float32)
    hw = np.zeros((NS, 2, NS), np.float32)
    for s in range(NS):
        t = TILES[r][s]
        if t == 0:
            continue
        for r_ in range(2):
            if (t - 1) in TILES[r_]:
                hw[s, r_, TILES[r_].index(t - 1)] = 1.0
    invc = np.zeros((NS, 4, 16), np.float32)
    for s in range(NS):
        t0 = TILES[r][s] * TT
        for g, w in enumerate(WINS):
            cnt = np.minimum(t0 + np.arange(16) + 1, w).astype(np.float32)
            invc[s, g] = np.float32(1.0) / cnt
    return qmask, hw.reshape(-1), invc.reshape(-1)


def _rep(v):
    return np.ascontiguousarray(np.broadcast_to(np.asarray(v, np.float32).reshape(1, -1), (128, np.asarray(v).size)))


def _colT(a, nchunk):
    a = np.asarray(a, np.float32)
    L = a.shape[0]
    return np.ascontiguousarray(a.reshape(L, nchunk, 128).transpose(2, 0, 1).reshape(128, L * nchunk))


def _prepare(x, c, positions, w_ada, b_ada, g_pre, w_in, w_pool, pool_scale,
             lambda_q1, lambda_k1, lambda_q2, lambda_k2, subln_g, w_out, g_post):
    invf, sgn, perm, khot = _static_tables()
    c = np.asarray(c, np.float32)
    shared = dict(
        cT=np.ascontiguousarray(c.reshape(B, KC, 128).transpose(2, 1, 0).reshape(128, KC * B)),
        invf=invf, sgn=sgn, perm=_bf(perm), ones=_bf(np.ones((128, 128))), ident=_bf(np.eye(128)),
        khot=_bf(khot),
    )
    per_layer = dict(
        b_ada=np.asarray(b_ada, np.float32), g_pre=np.asarray(g_pre, np.float32),
        g_post=np.asarray(g_post, np.float32), pool_scale=np.asarray(pool_scale, np.float32),
        w_in=np.asarray(w_in, np.float32), w_out=np.asarray(w_out, np.float32),
        w_pool=np.asarray(w_pool, np.float32).reshape(DEPTH, 1024, 256),
        lq1=np.asarray(lambda_q1, np.float32), lk1=np.asarray(lambda_k1, np.float32),
        lq2=np.asarray(lambda_q2, np.float32), lk2=np.asarray(lambda_k2, np.float32),
        subln=np.asarray(subln_g, np.float32), w_ada=np.asarray(w_ada, np.float32),
    )
    per_core = []
    xT0 = []
    x = np.asarray(x, np.float32)
    positions = np.asarray(positions)
    for core in range(8):
        b, r = core // 2, core % 2
        tok = _tok_idx(r)
        qmask, hw, invc = _per_rank_tables(r)
        bsel = np.zeros((128, B), np.float32)
        bsel[:, b] = 1.0
        per_core.append(dict(
            bsel=bsel,
            posr=np.ascontiguousarray(np.broadcast_to(positions[b, tok].astype(np.int32)[None, :], (128, TO))),
            qmask=_bf(qmask), hw=_rep(hw), invc16=_rep(invc),
        ))
        xT0.append(np.ascontiguousarray(x[b, tok, :].T))
    return shared, per_layer, per_core, xT0


def _layer_maps(shared, per_layer, per_core, xT, layers):
    L = list(layers)
    pl = per_layer
    lamc = np.array([[-(0.8 - 0.6 * math.exp(-0.3 * l)), 1.0 - (0.8 - 0.6 * math.exp(-0.3 * l))] for l in L], np.float32)
    lay = dict(
        b_adaT=_colT(pl["b_ada"][L], 48), g_preT=_colT(pl["g_pre"][L], KC), g_postT=_colT(pl["g_post"][L], KC),
        pool_scaleT=_colT(pl["pool_scale"][L], 8),
        w_in=np.ascontiguousarray(pl["w_in"][L]), w_out=np.ascontiguousarray(pl["w_out"][L]),
        w_pool=np.ascontiguousarray(pl["w_pool"][L]),
        lq1r=_rep(pl["lq1"][L]), lk1r=_rep(pl["lk1"][L]), lq2r=_rep(pl["lq2"][L]), lk2r=_rep(pl["lk2"][L]),
        sublnr=_rep(pl["subln"][L]), lamc=_rep(lamc),
    )
    maps = []
    for core in range(8):
        m = dict(shared)
        m.update(lay)
        m.update(per_core[core])
        m["w_ada_sl"] = np.ascontiguousarray(pl["w_ada"][L][:, :, core * 768:(core + 1) * 768])
        m["xT"] = xT[core]
        maps.append(m)
    return maps


_PROG_CACHE = {}


def _get_prog(nl, dbg=False):
    key = (nl, dbg)
    if key not in _PROG_CACHE:
        _PROG_CACHE[key] = build_program(list(range(nl)), True, True, dbg=dbg)
    return _PROG_CACHE[key]


LAYERS_PER_LAUNCH = 4


def kernel(x, c, positions, w_ada, b_ada, g_pre, w_in, w_pool, pool_scale,
           lambda_q1, lambda_k1, lambda_q2, lambda_k2, subln_g, w_out, g_post):
    shared, per_layer, per_core, xT = _prepare(x, c, positions, w_ada, b_ada, g_pre, w_in, w_pool, pool_scale,
                                               lambda_q1, lambda_k1, lambda_q2, lambda_k2, subln_g, w_out, g_post)
    nl = LAYERS_PER_LAUNCH
    nc = _get_prog(nl)
    for l0 in range(0, DEPTH, nl):
        maps = _layer_maps(shared, per_layer, per_core, xT, range(l0, l0 + nl))
        res = run_bass_kernel_spmd(nc, maps, core_ids=list(range(8)))
        xT = [np.ascontiguousarray(res.results[core]["oT"]) for core in range(8)]
    out = np.empty((B, S, D), np.float32)
    for core in range(8):
        b, r = core // 2, core % 2
        out[b, _tok_idx(r), :] = xT[core].T
    return out
```

```python
import math
from contextlib import ExitStack

import numpy as np
import ml_dtypes

import concourse.bass as bass
import concourse.mybir as mybir
from concourse.bass_utils import run_bass_kernel_spmd

F32 = mybir.dt.float32
BF16 = mybir.dt.bfloat16
I32 = mybir.dt.int32
AF = mybir.ActivationFunctionType
ALU = mybir.AluOpType
AX = mybir.AxisListType

D = 2048
KC = 16
S = 4096
B = 4
DEPTH = 4
TT = 512
NS = 4
TO = NS * TT
DIN = 6144
EPS = 1e-6
NEG = -30000.0
TILES = ([0, 3, 4, 7], [1, 2, 5, 6])
WINS = (2, 4, 8, 16)
ENG = ("pe", "act", "dve", "pool", "sp")


class Sem:
    def __init__(self, h):
        self.h = h
        self.n = 0


class Tok:
    __slots__ = ("sem", "v")

    def __init__(self, sem, v):
        self.sem = sem
        self.v = v


_GLOBAL_SEMS = {}


def clear_sems(nc, sems):
    if not sems:
        return
    with nc.Block() as block:
        @block.gpsimd
        def _(e):
            for sm in sems:
                n = sm.h.num
                e.dma_reset(range(n, n + 1))
                e.sem_clear(sm.h)


class Phase:
    def __init__(self, nc, es, name):
        self.nc = nc
        self.es = es
        self.name = name
        self.q = {e: [] for e in ENG}
        self.waited = {e: {} for e in ENG}
        self.all_sems = []
        self.own = {e: self.sem("own_" + e) for e in ENG}
        self.dma_sems = []

    def sem(self, nm):
        sm = Sem(self.es.enter_context(self.nc.semaphore(f"{self.name}_{nm}")))
        self.all_sems.append(sm)
        return sm

    def dsem(self, nm):
        s = self.sem(nm)
        self.dma_sems.append(s)
        return s

    def op(self, eng, fn, waits=(), sig=None, inc=1):
        ws = []
        for t in waits:
            if t is None:
                continue
            cur = self.waited[eng].get(id(t.sem), 0)
            if t.v > cur:
                self.waited[eng][id(t.sem)] = t.v
                ws.append((t.sem.h, t.v))
        tok = None
        sh = None
        if sig is not None:
            sig.n += inc
            tok = Tok(sig, sig.n)
            sh = sig.h
        self.q[eng].append((ws, fn, sh, inc))
        return tok

    def c(self, eng, fn, waits=()):
        return self.op(eng, fn, waits, sig=self.own[eng], inc=1)

    def dma(self, eng, fn, sem, waits=()):
        return self.op(eng, fn, waits, sig=sem, inc=16)

    def run(self, extra_final_waits=()):
        fin = [Tok(s, s.n) for s in self.dma_sems if s.n > 0] + list(extra_final_waits)
        self.op("sp", None, fin)
        self.op("pool", None, fin)
        q = self.q
        pos = {e: 0 for e in ENG}
        val = {}
        prog = True
        while prog:
            prog = False
            for e in ENG:
                while pos[e] < len(q[e]):
                    ws, fn, sh, inc = q[e][pos[e]]
                    if all(val.get(id(h), _GLOBAL_SEMS.get(id(h), 0)) >= v for h, v in ws):
                        if sh is not None:
                            if id(sh) in _GLOBAL_SEMS:
                                _GLOBAL_SEMS[id(sh)] += inc
                            else:
                                val[id(sh)] = val.get(id(sh), 0) + inc
                        pos[e] += 1
                        prog = True
                    else:
                        break
        stuck = {e: (pos[e], len(q[e])) for e in ENG if pos[e] < len(q[e])}
        if stuck:
            raise RuntimeError(f"phase {self.name}: semaphore deadlock {stuck}")

        def replay(e, lst):
            for ws, fn, sh, inc in lst:
                for h, v in ws:
                    e.wait_ge(h, v)
                if fn is None:
                    continue
                ins = fn(e)
                if sh is not None:
                    ins.then_inc(sh, inc)

        clear_sems(self.nc, self.all_sems)

        with self.nc.Block() as block:
            @block.tensor
            def _(e):
                replay(e, q["pe"])

            @block.scalar
            def _(e):
                replay(e, q["act"])

            @block.vector
            def _(e):
                replay(e, q["dve"])

            @block.gpsimd
            def _(e):
                replay(e, q["pool"])

            @block.sync
            def _(e):
                replay(e, q["sp"])


def build_program(layers, first, last, dbg=False, stop=None, dup_B=0):
    nc = bass.Bass("TRN2", target_bir_lowering=False)
    es = ExitStack()

    def din(name, shape, dt):
        return nc.dram_tensor(name, shape, dt, kind="ExternalInput").ap()

    def dscr(name, shape, dt, out=False):
        if out:
            return nc.dram_tensor(name, shape, dt, kind="ExternalOutput").ap()
        return nc.dram_tensor(name, shape, dt).ap()

    NL = len(layers)
    NLP = NL
    assert list(layers) == list(range(NL))
    xT_in = din("xT", [D, TO], F32)
    cT_in = din("cT", [128, KC * B], F32)
    bsel_in = din("bsel", [128, B], F32)
    posr_in = din("posr", [128, TO], I32)
    invf_in = din("invf", [128, 1], F32)
    sgn_in = din("sgn", [128, 1], F32)
    perm_in = din("perm", [128, 128], BF16)
    ones_in = din("ones", [128, 128], BF16)
    ident_in = din("ident", [128, 128], BF16)
    khot_in = din("khot", [64, S], BF16)
    qmask_in = din("qmask", [64, TO], BF16)
    hw_in = din("hw", [128, NS * 8], F32)
    invc_in = din("invc16", [128, NS * 4 * 16], F32)
    wada_in = din("w_ada_sl", [NLP, D, 768], F32)
    bada_in = din("b_adaT", [128, NLP * 48], F32)
    gpre_in = din("g_preT", [128, NLP * KC], F32)
    gpost_in = din("g_postT", [128, NLP * KC], F32)
    pscale_in = din("pool_scaleT", [128, NLP * 8], F32)
    win_in = din("w_in", [NLP, D, DIN], F32)
    wout_in = din("w_out", [NLP, D, D], F32)
    wpool_in = din("w_pool", [NLP, 1024, 256], F32)
    lam_in = [din(n, [128, NLP * 64], F32) for n in ("lq1r", "lk1r", "lq2r", "lk2r")]
    subln_in = din("sublnr", [128, NLP * 128], F32)
    lamc_in = din("lamc", [128, NLP * 2], F32)
    oT = nc.dram_tensor("oT", [D, TO], F32, kind="ExternalOutput").ap()

    _win_bf = dscr("win_bf", [D, DIN], BF16)
    _wout_bf = [dscr(f"wout_bf{j}", [D, D], BF16) for j in range(2)]
    _wpool_bf = dscr("wpool_bf", [1024, 256], BF16)
    win_bf = [_win_bf for _ in layers]
    wout_bf = [_wout_bf[i % 2] for i in range(len(layers))]
    wpool_bf = [_wpool_bf for _ in layers]
    uT = dscr("uT", [1024, TO], F32, out=dbg)
    uh_loc = dscr("uh_loc", [1024, NS * 16], F32)
    uh_g = dscr("uh_g", [2 * 1024, NS * 16], F32)
    sgpT = dscr("sgpT", [1024, TO], BF16, out=dbg)
    qT = dscr("qT", [1024, TO], BF16, out=dbg)
    kT_loc = [dscr(f"kT_loc{j}", [512, TO], BF16) for j in range(2)]
    kT_g = [dscr(f"kT_g{j}", [2 * 512, TO], BF16) for j in range(2)]
    v_loc = [dscr(f"v_loc{j}", [TO // 2, 1024], BF16) for j in range(2)]
    v_g = [dscr(f"v_g{j}", [TO, 1024], BF16) for j in range(2)]
    gd = dscr("gd", [TO, 1024], BF16, out=dbg)
    mixT = dscr("mixT", [D, TO], BF16, out=dbg)
    mod_loc = [dscr(f"mod_loc{l}", [128, 24], F32) for l in range(NLP)]
    mod_g = [dscr(f"mod_g{l}", [8 * 128, 24], F32) for l in range(NLP)]
    if dbg:
        dbg_k = [dscr(f"dbg_k{j}", [2 * 512, TO], BF16, out=True) for j in range(2)]
        dbg_mod = dscr("dbg_mod", [128, NLP * 48], F32, out=True)
        dbg_tab = dscr("dbg_tab", [128, 2 * TO], F32, out=True)

    def sb(name, shape, dt):
        return es.enter_context(nc.sbuf_tensor("g_" + name, shape, dt))

    ones_bf = sb("ones_bf", [128, 128], BF16)
    perm_bf = sb("perm_bf", [128, 128], BF16)
    ident_bf = sb("ident_bf", [128, 128], BF16)
    cosT = sb("cosT", [128, TO], F32)
    sinT = sb("sinT", [128, TO], F32)
    modA = sb("modA", [128, NLP, 48], F32)
    A_all = sb("A_all", [128, NLP, KC], F32)
    G2_all = sb("G2_all", [128, NLP, KC], F32)
    gpre_sb = sb("gpre_sb", [128, NLP, KC], F32)
    gpost_sb = sb("gpost_sb", [128, NLP, KC], F32)
    pscale_sb = sb("pscale_sb", [128, NLP, 8], F32)
    neglam = sb("neglam", [128, NLP], F32)
    sublnG = sb("sublnG", [128, NLP, 512], F32)
    hw_sb = sb("hw_sb", [128, NS * 8], F32)
    invc_sb = sb("invc_sb", [128, NS, 4, 16], F32)
    neghalf = sb("neghalf", [128, 512], F32)

    wcast_tok = [None] * NL

    def cast_chunks(i, part):
        if i >= NL:
            return []
        l = layers[i]
        out = []
        if part == "out":
            wo_i = wout_in[l].rearrange("(kc p) n -> p kc n", p=128)
            wo_b = wout_bf[i].rearrange("(kc p) n -> p kc n", p=128)
            for kc in range(KC):
                out.append((wo_i[:, kc, :], wo_b[:, kc, :]))
            return out
        wi = win_in[l].rearrange("(kc p) n -> p kc n", p=128)
        wb_ = win_bf[i].rearrange("(kc p) n -> p kc n", p=128)
        for kc in range(KC):
            for t3 in range(3):
                out.append((wi[:, kc, t3 * 2048:(t3 + 1) * 2048], wb_[:, kc, t3 * 2048:(t3 + 1) * 2048]))
        out.append((wpool_in[l].rearrange("(a p) d -> p a d", p=128), wpool_bf[i].rearrange("(a p) d -> p a d", p=128)))
        return out

    class Caster:
        def __init__(self, P, lsb, chunks, engines):
            self.P = P
            self.chunks = chunks
            self.engines = engines
            self.k = 0
            if not self.chunks:
                return
            R = self.R = 3
            self.s32 = [lsb(f"cst32_{j}", [128, 2048], F32) for j in range(R)]
            self.s16 = [lsb(f"cst16_{j}", [128, 2048], BF16) for j in range(R)]
            self.sl = [P.dsem(f"cstl{j}") for j in range(R)]
            self.ss = [P.dsem(f"csts{j}") for j in range(R)]
            self.rd32 = [None] * R
            self.rd16 = [None] * R

        def step(self, n=1):
            P = self.P
            for _ in range(n):
                if self.k >= len(self.chunks):
                    return
                src, dst = self.chunks[self.k]
                j = self.k % self.R
                eng = self.engines[self.k % len(self.engines)]
                self.k += 1
                s32, s16 = self.s32[j], self.s16[j]
                is3 = len(src.shape) == 3
                o32 = s32[:].rearrange("p (a d) -> p a d", a=src.shape[1]) if is3 else s32[:]
                o16 = s16[:].rearrange("p (a d) -> p a d", a=src.shape[1]) if is3 else s16[:]
                t_l = P.dma("sp", lambda e, o32=o32, src=src: e.dma_start(out=o32, in_=src), self.sl[j], [self.rd32[j]])
                if eng == "act":
                    t_c = P.c(eng, lambda e, s32=s32, s16=s16: e.activation(out=s16[:], in_=s32[:], func=AF.Copy), [t_l, self.rd16[j]])
                else:
                    t_c = P.c(eng, lambda e, s32=s32, s16=s16: e.tensor_copy(out=s16[:], in_=s32[:]), [t_l, self.rd16[j]])
                self.rd32[j] = t_c
                self.rd16[j] = P.dma("sp", lambda e, o16=o16, dst=dst: e.dma_start(out=dst, in_=o16), self.ss[j], [t_c])

        def finish(self):
            self.step(len(self.chunks))

    def prologue():
        with ExitStack() as les:
            P = Phase(nc, les, "pro")

            def lsb(name, shape, dt):
                return les.enter_context(nc.sbuf_tensor(f"{P.name}_{name}", shape, dt))

            caster0 = Caster(P, lsb, cast_chunks(0, "in"), ("dve", "act", "pool"))

            ld = P.dsem("ld")
            posi = lsb("posi", [128, TO], I32)
            invf = lsb("invf", [128, 1], F32)
            sgn = lsb("sgn", [128, 1], F32)
            cT32 = lsb("cT32", [128, KC * B], F32)
            bsel = lsb("bsel", [128, B], F32)
            bada = lsb("bada", [128, NLP, 48], F32)
            lam_sb = [lsb(f"lam{i}", [128, NLP, 64], F32) for i in range(4)]
            subln_sb = lsb("subln_sb", [128, NLP, 128], F32)
            lamc_sb = lsb("lamc_sb", [128, NLP, 2], F32)
            loads = [
                (posi[:], posr_in), (invf[:], invf_in), (sgn[:], sgn_in), (cT32[:], cT_in),
                (bsel[:], bsel_in), (bada[:].rearrange("p l j -> p (l j)"), bada_in),
                (ones_bf[:], ones_in), (perm_bf[:], perm_in), (ident_bf[:], ident_in),
                (hw_sb[:], hw_in), (invc_sb[:].rearrange("p s g x -> p (s g x)"), invc_in),
                (gpre_sb[:].rearrange("p l j -> p (l j)"), gpre_in),
                (gpost_sb[:].rearrange("p l j -> p (l j)"), gpost_in),
                (pscale_sb[:].rearrange("p l j -> p (l j)"), pscale_in),
                (subln_sb[:].rearrange("p l j -> p (l j)"), subln_in),
                (lamc_sb[:].rearrange("p l j -> p (l j)"), lamc_in),
            ] + [(lam_sb[i][:].rearrange("p l j -> p (l j)"), lam_in[i]) for i in range(4)]
            for o_, i_ in loads:
                P.dma("sp", lambda e, o_=o_, i_=i_: e.dma_start(out=o_, in_=i_), ld)
            t_ld = Tok(ld, ld.n)

            posf = lsb("posf", [128, TO], F32)
            ang = lsb("ang", [128, TO], F32)
            kk_i = lsb("kk_i", [128, TO], I32)
            kk_f = lsb("kk_f", [128, TO], F32)
            red = lsb("red", [128, TO], F32)
            C1 = 6.28125
            C2 = 2.0 * math.pi - C1
            t = P.c("dve", lambda e: e.tensor_copy(out=posf[:], in_=posi[:]), [t_ld])
            t_ang = P.c("dve", lambda e: e.tensor_scalar(out=ang[:], in0=posf[:], scalar1=invf[:, 0:1],
                                                         scalar2=None, op0=ALU.mult), [t])
            t_prev_act = None
            for which, shift_, dst in ((0, 0.0, sinT), (1, math.pi / 2.0, cosT)):
                src = ang
                if which == 1:
                    t_ang2 = P.c("dve", lambda e, shift_=shift_: e.tensor_scalar(out=posf[:], in0=ang[:], scalar1=shift_,
                                                                  scalar2=None, op0=ALU.add), [t_ang, t_prev_act])
                    src = posf
                    t0 = t_ang2
                else:
                    t0 = t_ang
                t1 = P.c("dve", lambda e, src=src: e.tensor_scalar(out=kk_i[:], in0=src[:], scalar1=1.0 / (2.0 * math.pi),
                                                                   scalar2=None, op0=ALU.mult), [t0, t_prev_act])
                t2 = P.c("dve", lambda e: e.tensor_copy(out=kk_f[:], in_=kk_i[:]), [t1])
                t3 = P.c("dve", lambda e, src=src: e.scalar_tensor_tensor(out=red[:], in0=kk_f[:], scalar=-C1, in1=src[:],
                                                                          op0=ALU.mult, op1=ALU.add), [t2])
                t4 = P.c("dve", lambda e: e.scalar_tensor_tensor(out=red[:], in0=kk_f[:], scalar=-C2, in1=red[:],
                                                                 op0=ALU.mult, op1=ALU.add), [t3])
                t5 = P.c("dve", lambda e: e.tensor_scalar(out=red[:], in0=red[:], scalar1=-3.141592, scalar2=3.141592,
                                                          op0=ALU.max, op1=ALU.min), [t4])
                if which == 0:
                    t_prev_act = P.c("act", lambda e, dst=dst: e.activation(out=dst[:], in_=red[:], func=AF.Sin,
                                                                            scale=sgn[:, 0:1]), [t5])
                else:
                    t_prev_act = P.c("act", lambda e, dst=dst: e.activation(out=dst[:], in_=red[:], func=AF.Sin), [t5])
            t_tabs = t_prev_act

            lamt = lsb("lamt", [128, NLP, 64], F32)
            dots = lsb("dots", [128, 2, NLP], F32)
            exps = lsb("exps", [128, 2, NLP], F32)
            tl = None
            for j in range(2):
                ta = P.c("dve", lambda e, j=j: e.tensor_tensor(out=lamt[:], in0=lam_sb[2 * j][:], in1=lam_sb[2 * j + 1][:],
                                                               op=ALU.mult), [t_ld, tl])
                tl = P.c("dve", lambda e, j=j: e.tensor_reduce(out=dots[:, j, :], in_=lamt[:], axis=AX.X, op=ALU.add), [ta])
            te = P.c("act", lambda e: e.activation(out=exps[:], in_=dots[:], func=AF.Exp), [tl])
            tn = P.c("dve", lambda e: e.tensor_tensor(out=neglam[:], in0=exps[:, 1, :], in1=exps[:, 0, :], op=ALU.subtract), [te])
            for l in range(NLP):
                tn = P.c("dve", lambda e, l=l: e.tensor_scalar(out=neglam[:, l:l + 1], in0=neglam[:, l:l + 1],
                                                               scalar1=lamc_sb[:, l, 0:1], scalar2=None, op0=ALU.add), [tn, t_ld])
                for hh in range(4):
                    tn = P.c("dve", lambda e, l=l, hh=hh: e.tensor_scalar(
                        out=sublnG[:, l, hh * 128:(hh + 1) * 128], in0=subln_sb[:, l, :], scalar1=lamc_sb[:, l, 1:2],
                        scalar2=None, op0=ALU.mult), [tn, t_ld])
            P.c("pool", lambda e: e.memset(neghalf[:], -0.5))

            wada_sb = lsb("wada_sb", [128, KC, 768], F32)
            with nc.psum_tensor("modps", [128, 6 * B], F32) as modps:
                mod_sl = lsb("mod_sl", [128, NLP, 6, B], F32)
                wsem = P.dsem("wada")
                t_ev = None
                for l in range(NLP):
                    tw = P.dma("sp", lambda e, l=l: e.dma_start(
                        out=wada_sb[:], in_=wada_in[l].rearrange("(kc p) n -> p kc n", p=128)), wsem, [t_ev])
                    tm = None
                    for jc in range(6):
                        for kc in range(KC):
                            tm = P.op("pe", lambda e, jc=jc, kc=kc: e.matmul(
                                modps[:, jc * B:(jc + 1) * B], wada_sb[:, kc, jc * 128:(jc + 1) * 128],
                                cT32[:, kc * B:(kc + 1) * B],
                                start=(kc == 0), stop=(kc == KC - 1)),
                                [tw, t_ld, t_ev], sig=(P.own["pe"] if (jc == 5 and kc == KC - 1) else None))
                    t_ev = P.c("dve", lambda e, l=l: e.tensor_copy(
                        out=mod_sl[:, l, :, :].rearrange("p j b -> p (j b)"), in_=modps[:]), [tm])
                ms = P.dsem("modst")
                ccs = P.sem("cc")
                mod_all = lsb("mod_all", [128, 8, NLP, 6, B], F32)
                ml = P.dsem("modld")
                t_ml = None
                t_cc = None
                for l in range(NLP):
                    t_st = P.dma("sp", lambda e, l=l: e.dma_start(
                        out=mod_loc[l], in_=mod_sl[:, l, :, :].rearrange("p j b -> p (j b)")), ms, [t_ev])
                    t_cc = P.op("pool", lambda e, l=l: e.collective_compute(
                        "AllGather", ALU.bypass, replica_groups=[list(range(8))], ins=[mod_loc[l].opt()], outs=[mod_g[l].opt()]),
                        [t_st, t_cc], sig=ccs, inc=1)
                    P.op("pool", None, [t_cc])
                    t_ml = P.dma("sp", lambda e, l=l: e.dma_start(
                        out=mod_all[:, :, l, :, :].rearrange("p i j b -> p i (j b)"),
                        in_=mod_g[l].rearrange("(i p) c -> p i c", p=128)), ml, [t_cc])
                t_ml = Tok(ml, ml.n)
                tb_ = None
                for i8 in range(8):
                    for l in range(NLP):
                        dst = modA[:, l, i8 * 6:(i8 + 1) * 6]
                        tb_ = P.c("dve", lambda e, i8=i8, l=l, dst=dst: e.tensor_scalar(
                            out=dst, in0=mod_all[:, i8, l, :, 0], scalar1=bsel[:, 0:1], scalar2=None, op0=ALU.mult),
                            [t_ml, t_ld])
                        for b in range(1, B):
                            tb_ = P.c("dve", lambda e, i8=i8, l=l, dst=dst, b=b: e.scalar_tensor_tensor(
                                out=dst, in0=mod_all[:, i8, l, :, b], scalar=bsel[:, b:b + 1], in1=dst,
                                op0=ALU.mult, op1=ALU.add), [tb_])
                tb_ = P.c("dve", lambda e: e.tensor_tensor(out=modA[:], in0=modA[:], in1=bada[:], op=ALU.add), [tb_])
                tb_ = P.c("dve", lambda e: e.scalar_tensor_tensor(
                    out=A_all[:], in0=modA[:, :, 16:32], scalar=1.0, in1=gpre_sb[:], op0=ALU.add, op1=ALU.mult), [tb_])
                tb_ = P.c("dve", lambda e: e.scalar_tensor_tensor(
                    out=G2_all[:], in0=modA[:, :, 32:48], scalar=1.0, in1=gpost_sb[:], op0=ALU.add, op1=ALU.mult), [tb_])
                caster0.finish()
                fin = [tb_, tn, t_tabs]
                if dbg:
                    dsm = P.dsem("dbg")
                    P.dma("sp", lambda e: e.dma_start(out=dbg_mod, in_=modA[:].rearrange("p l j -> p (l j)")), dsm, [tb_])
                    P.dma("sp", lambda e: e.dma_start(out=dbg_tab[:, 0:TO], in_=cosT[:]), dsm, [t_tabs])
                    P.dma("sp", lambda e: e.dma_start(out=dbg_tab[:, TO:2 * TO], in_=sinT[:]), dsm, [t_tabs])
                P.op("dve", None, fin)
                P.run()

    def phase_A(i, l, x_src):
        with ExitStack() as les:
            P = Phase(nc, les, f"A{i}")

            def lsb(name, shape, dt):
                return les.enter_context(nc.sbuf_tensor(f"{P.name}_{name}", shape, dt))

            def lps(name, shape, dt):
                return les.enter_context(nc.psum_tensor(f"{P.name}_{name}", shape, dt))

            xt = lsb("xt", [128, KC, TT], F32)
            sqr = [lsb(f"sqr{j}", [128, TT], BF16) for j in range(2)]
            hT = [lsb(f"hT{j}", [128, KC, TT], BF16) for j in range(2)]
            wb = [lsb(f"wb{j}", [128, KC, 512], BF16) for j in range(3)]
            rpre = lsb("rpre", [128, TT], F32)
            rstd = lsb("rstd", [128, TT], F32)
            xr = [lsb(f"xr{j}", [128, TT], F32) for j in range(2)]
            NR = 4
            tf = [lsb(f"tf{j}", [128, TT], F32) for j in range(NR)]
            tb = [lsb(f"tb{j}", [128, TT], BF16) for j in range(NR)]
            qb = [lsb(f"qb{j}", [128, TT], BF16) for j in range(2)]
            r1 = [lsb(f"r1{j}", [128, TT], F32) for j in range(2)]
            r2 = [lsb(f"r2{j}", [128, TT], F32) for j in range(2)]
            acc = [lps(f"acc{j}", [128, 512], F32) for j in range(4)]
            ssb = lps("ssb", [128, 512], F32)
            pq = [lps(f"pq{j}", [128, 512], F32) for j in range(2)]

            s_xt = P.dsem("xt")
            s_wb = [P.dsem(f"wb{j}") for j in range(3)]
            s_tf = [P.dsem(f"tfo{j}") for j in range(NR)]
            s_th = [P.dsem(f"tho{j}") for j in range(NR)]
            s_tb = [P.dsem(f"tbo{j}") for j in range(NR)]

            st = dict(th_rd=[None] * NR,
                xt_rd=[], sqr_rd=[None, None], ssb_rd=None, rstd_rd=None, hT_rd=[None, None],
                xr_rd=[None, None], wb_rd=[None, None, None], acc_rd=[None] * 4,
                tf_rd=[None] * NR, tb_rd=[None] * NR, qb_rd=[None, None], pq_rd=[None, None],
                r1_rd=[None, None], r2_rd=[None, None], tfi=0, tbi=0, qi=0, ai=0,
            )
            win_v = win_bf[i].rearrange("(kc p) n -> p kc n", p=128)
            x_v = x_src.rearrange("(kc p) t -> p kc t", p=128)
            wloads = {}
            t_hT = {}
            pending_pe = []

            def issue_wload(gi):
                if gi >= NS * 12 or gi in wloads:
                    return
                g = gi % 12
                j = gi % 3
                wloads[gi] = P.dma("sp", lambda e, g=g, j=j: e.dma_start(
                    out=wb[j][:], in_=win_v[:, :, g * 512:(g + 1) * 512]), s_wb[j],
                    [st["wb_rd"][j], wcast_tok[i]])

            def emit_norm(s):
                c0 = s * TT
                hb = s % 2
                t_x = P.dma("sp", lambda e, c0=c0: e.dma_start(out=xt[:], in_=x_v[:, :, c0:c0 + TT]), s_xt, st["xt_rd"])
                st["xt_rd"] = []
                t_ss = None
                for kc in range(KC):
                    j = kc % 2
                    t_sq = P.c("act", lambda e, kc=kc, j=j: e.activation(out=sqr[j][:], in_=xt[:, kc, :], func=AF.Square),
                               [t_x, st["sqr_rd"][j]])
                    t_ss = P.op("pe", lambda e, kc=kc, j=j: e.matmul(ssb[:], ones_bf[:], sqr[j][:], start=(kc == 0), stop=(kc == KC - 1)),
                                [t_sq, st["ssb_rd"] if kc == 0 else None], sig=P.own["pe"])
                    st["sqr_rd"][j] = t_ss
                t_rp = P.c("dve", lambda e: e.tensor_scalar(out=rpre[:], in0=ssb[:], scalar1=1.0 / D, scalar2=EPS,
                                                            op0=ALU.mult, op1=ALU.add), [t_ss, st["rstd_rd"]])
                st["ssb_rd"] = t_rp
                t_ln = P.c("act", lambda e: e.activation(out=rpre[:], in_=rpre[:], func=AF.Ln), [t_rp])
                t_rs = P.c("act", lambda e: e.activation(out=rstd[:], in_=rpre[:], func=AF.Exp, scale=-0.5),
                           [t_ln, st["rstd_rd"]])
                st["ssb_rd"] = t_rs
                t_h = None
                for kc in range(KC):
                    j = kc % 2
                    t_xr = P.c("dve", lambda e, kc=kc, j=j: e.tensor_tensor(out=xr[j][:], in0=xt[:, kc, :], in1=rstd[:], op=ALU.mult),
                               [t_rs, t_x, st["xr_rd"][j]])
                    t_h = P.c("act", lambda e, kc=kc, j=j, hb=hb: e.activation(
                        out=hT[hb][:, kc, :], in_=xr[j][:], func=AF.Identity,
                        scale=A_all[:, l, kc:kc + 1], bias=modA[:, l, kc:kc + 1]),
                        [t_xr, st["hT_rd"][hb]])
                    st["xr_rd"][j] = t_h
                    if kc == KC - 1:
                        st["xt_rd"] = [t_xr, t_ss]
                        st["rstd_rd"] = t_xr
                t_hT[s] = t_h

            issue_wload(0)
            issue_wload(1)
            emit_norm(0)
            for s in range(NS):
                c0 = s * TT
                hb = s % 2
                t_h = t_hT[s]
                t_last_mm = None
                for g in range(12):
                    gi = s * 12 + g
                    issue_wload(gi)
                    issue_wload(gi + 1)
                    issue_wload(gi + 2)
                    j3 = gi % 3
                    t_w = wloads[gi]
                    t_mm_last_group = None
                    for sub in range(4):
                        a = st["ai"] % 4
                        st["ai"] += 1
                        t_mm = None
                        for kc in range(KC):
                            if g < 8:
                                fn = lambda e, a=a, j3=j3, kc=kc, sub=sub, hb=hb: e.matmul(
                                    acc[a][:], wb[j3][:, kc, sub * 128:(sub + 1) * 128], hT[hb][:, kc, :],
                                    start=(kc == 0), stop=(kc == KC - 1))
                            else:
                                fn = lambda e, a=a, j3=j3, kc=kc, sub=sub, hb=hb: e.matmul(
                                    acc[a][:], hT[hb][:, kc, sub * 128:(sub + 1) * 128], wb[j3][:, kc, :],
                                    start=(kc == 0), stop=(kc == KC - 1))
                            t_mm = P.op("pe", fn, [t_w, t_h, st["acc_rd"][a]],
                                        sig=(P.own["pe"] if kc == KC - 1 else None))
                        t_mm_last_group = t_mm
                        t_last_mm = t_mm
                        for fnp in pending_pe:
                            fnp()
                        pending_pe.clear()
                        if g < 2:
                            c = g * 4 + sub
                            k = st["tfi"] % NR
                            st["tfi"] += 1
                            t_e = P.c("act", lambda e, a=a, k=k: e.activation(out=tf[k][:], in_=acc[a][:], func=AF.Copy),
                                      [t_mm, st["tf_rd"][k], st["th_rd"][k]])
                            st["acc_rd"][a] = t_e
                            t_d1 = P.dma("pool", lambda e, k=k, c=c, c0=c0: e.dma_start(
                                out=uT[c * 128:(c + 1) * 128, c0:c0 + TT], in_=tf[k][:]), s_tf[k], [t_e])
                            t_d2 = P.dma("sp", lambda e, k=k, c=c, s=s: e.dma_start(
                                out=uh_loc[c * 128:(c + 1) * 128, s * 16:(s + 1) * 16], in_=tf[k][:, TT - 16:TT]), s_th[k], [t_e])
                            st["tf_rd"][k] = t_d1
                            st["th_rd"][k] = t_d2
                        elif g < 4:
                            c = (g - 2) * 4 + sub
                            k = st["tbi"] % NR
                            st["tbi"] += 1
                            t_e = P.c("act", lambda e, a=a, k=k: e.activation(out=tb[k][:], in_=acc[a][:], func=AF.Silu),
                                      [t_mm, st["tb_rd"][k]])
                            st["acc_rd"][a] = t_e
                            st["tb_rd"][k] = P.dma("pool", lambda e, k=k, c=c, c0=c0: e.dma_start(
                                out=sgpT[c * 128:(c + 1) * 128, c0:c0 + TT], in_=tb[k][:]), s_tb[k], [t_e])
                        elif g < 8:
                            c = (g - 4) * 4 + sub
                            jq = st["qi"] % 2
                            st["qi"] += 1
                            t_e = P.c("act", lambda e, a=a, jq=jq: e.activation(out=qb[jq][:], in_=acc[a][:], func=AF.Copy),
                                      [t_mm, st["qb_rd"][jq]])
                            st["acc_rd"][a] = t_e
                            k = st["tbi"] % NR
                            st["tbi"] += 1

                            def rope_tail(jq=jq, k=k, c=c, c0=c0, t_e=t_e):
                                t_p = P.op("pe", lambda e: e.matmul(pq[jq][:], perm_bf[:], qb[jq][:], start=True, stop=True),
                                           [t_e, st["pq_rd"][jq]], sig=P.own["pe"])
                                t_1 = P.c("pool", lambda e: e.tensor_tensor(
                                    out=r1[jq][:], in0=qb[jq][:], in1=cosT[:, c0:c0 + TT], op=ALU.mult), [t_e, st["r1_rd"][jq]])
                                t_2 = P.c("dve", lambda e: e.tensor_tensor(
                                    out=r2[jq][:], in0=pq[jq][:], in1=sinT[:, c0:c0 + TT], op=ALU.mult), [t_p, st["r2_rd"][jq]])
                                st["pq_rd"][jq] = t_2
                                t_3 = P.c("pool", lambda e: e.tensor_tensor(
                                    out=tb[k][:], in0=r1[jq][:], in1=r2[jq][:], op=ALU.add), [t_1, t_2, st["tb_rd"][k]])
                                st["qb_rd"][jq] = t_3
                                st["r1_rd"][jq] = t_3
                                st["r2_rd"][jq] = t_3
                                cc = c % 8
                                if c < 8:
                                    dst, rr = qT, cc * 128
                                else:
                                    dst, rr = kT_loc[cc // 4], (cc % 4) * 128
                                st["tb_rd"][k] = P.dma("pool", lambda e: e.dma_start(
                                    out=dst[rr:rr + 128, c0:c0 + TT], in_=tb[k][:]), s_tb[k], [t_3])
                            pending_pe.append(rope_tail)
                        elif g < 10:
                            k = st["tbi"] % NR
                            st["tbi"] += 1
                            t_e = P.c("act", lambda e, a=a, k=k: e.activation(out=tb[k][:], in_=acc[a][:], func=AF.Copy),
                                      [t_mm, st["tb_rd"][k]])
                            st["acc_rd"][a] = t_e
                            f0 = (g - 8) * 512
                            vr = (s % 2) * TT + sub * 128
                            st["tb_rd"][k] = P.dma("pool", lambda e, k=k, vr=vr, s=s, f0=f0: e.dma_start(
                                out=v_loc[s // 2][vr:vr + 128, f0:f0 + 512], in_=tb[k][:]), s_tb[k], [t_e])
                        else:
                            kf = st["tfi"] % NR
                            st["tfi"] += 1
                            t_e = P.c("act", lambda e, a=a, kf=kf: e.activation(out=tf[kf][:], in_=acc[a][:], func=AF.Silu),
                                      [t_mm, st["tf_rd"][kf], st["th_rd"][kf]])
                            st["acc_rd"][a] = t_e
                            k = st["tbi"] % NR
                            st["tbi"] += 1
                            t_m = P.c("dve", lambda e, kf=kf, k=k: e.tensor_tensor(
                                out=tb[k][:], in0=tf[kf][:], in1=sublnG[:, l, :], op=ALU.mult), [t_e, st["tb_rd"][k]])
                            st["tf_rd"][kf] = t_m
                            f0 = (g - 10) * 512
                            st["tb_rd"][k] = P.dma("pool", lambda e, k=k, c0=c0, sub=sub, f0=f0: e.dma_start(
                                out=gd[c0 + sub * 128:c0 + (sub + 1) * 128, f0:f0 + 512], in_=tb[k][:]), s_tb[k], [t_m])
                    st["wb_rd"][j3] = t_mm_last_group
                    if g == 5 and s + 1 < NS:
                        emit_norm(s + 1)
                st["hT_rd"][hb] = t_last_mm
            for fnp in pending_pe:
                fnp()
            pending_pe.clear()
            P.run()

    def phase_A2(i, l):
        with ExitStack() as les:
            P = Phase(nc, les, f"P{i}")

            def lsb(name, shape, dt):
                return les.enter_context(nc.sbuf_tensor(f"{P.name}_{name}", shape, dt))

            W = TT + 16
            ub = lsb("ub", [128, 8, W], F32)
            uhs = lsb("uhs", [128, 2, 8, NS * 16], F32)
            Ta = lsb("Ta", [128, W], F32)
            Tb = lsb("Tb", [128, W], F32)
            t16 = lsb("t16", [128, 16], F32)
            pooled = lsb("pooled", [128, 8, TT], BF16)
            sgp = lsb("sgp", [128, 8, TT], BF16)
            wp = lsb("wp", [128, 8, 256], BF16)
            mo = [lsb(f"mo{j}", [128, TT], BF16) for j in range(2)]
            pacc = [les.enter_context(nc.psum_tensor(f"{P.name}_pacc{j}", [128, 512], F32)) for j in range(2)]
            s_ld = P.dsem("ld")
            s_u = P.dsem("u")
            s_g = P.dsem("g")
            s_mo = [P.dsem(f"mo{j}") for j in range(2)]
            ccs = P.sem("cc")
            groups = [[0, 1], [2, 3], [4, 5], [6, 7]]
            t_cc = []
            for src, dst in ((uh_loc, uh_g), (kT_loc[0], kT_g[0]), (kT_loc[1], kT_g[1]), (v_loc[0], v_g[0]), (v_loc[1], v_g[1])):
                t_cc.append(P.op("pool", lambda e, src=src, dst=dst: e.collective_compute(
                    "AllGather", ALU.bypass, replica_groups=groups, ins=[src.opt()], outs=[dst.opt()]),
                    [t_cc[-1]] if t_cc else [], sig=ccs, inc=1))
                P.op("pool", None, [t_cc[-1]])
            t_w = P.dma("sp", lambda e: e.dma_start(out=wp[:], in_=wpool_bf[i].rearrange("(a p) d -> p a d", p=128)), s_ld, [wcast_tok[i]])
            t_h = P.dma("sp", lambda e: e.dma_start(
                out=uhs[:].rearrange("p r c x -> p (r c) x"),
                in_=uh_g.rearrange("(rc p) x -> p rc x", p=128)), s_ld, [t_cc[0]])
            t_ldc = Tok(s_ld, s_ld.n)
            ub_rd = []
            sgp_rd = None
            pooled_rd = None
            pacc_rd = [None, None]
            mo_rd = [None, None]
            tprev = None
            mi = 0
            for s in range(NS):
                c0 = s * TT
                t_u = P.dma("sp", lambda e, c0=c0: e.dma_start(
                    out=ub[:, :, 16:W], in_=uT.rearrange("(c p) t -> p c t", p=128)[:, :, c0:c0 + TT]), s_u, ub_rd)
                t_sg = P.dma("sp", lambda e, c0=c0: e.dma_start(
                    out=sgp[:], in_=sgpT.rearrange("(c p) t -> p c t", p=128)[:, :, c0:c0 + TT]), s_g, [sgp_rd])
                th = None
                first_ = True
                for r_ in range(2):
                    for s_ in range(NS):
                        idx = s * 8 + r_ * 4 + s_
                        src = uhs[:, r_, :, s_ * 16:(s_ + 1) * 16]
                        if first_:
                            th = P.c("dve", lambda e, src=src, idx=idx: e.tensor_scalar(
                                out=ub[:, :, 0:16], in0=src, scalar1=hw_sb[:, idx:idx + 1], scalar2=None, op0=ALU.mult),
                                [t_ldc] + ub_rd)
                            first_ = False
                        else:
                            th = P.c("dve", lambda e, src=src, idx=idx: e.scalar_tensor_tensor(
                                out=ub[:, :, 0:16], in0=src, scalar=hw_sb[:, idx:idx + 1], in1=ub[:, :, 0:16],
                                op0=ALU.mult, op1=ALU.add), [th])
                ub_rd = []
                tp = None
                for c in range(8):
                    g = c // 2
                    w = WINS[g]
                    u = ub[:, c, :]
                    cur, off = u, 0
                    bufs = [Ta, Tb]
                    bi = 0
                    sh = 1
                    tt_ = None
                    while sh < w:
                        o = bufs[bi]
                        lo = 2 * sh - 1
                        tt_ = P.c("dve", lambda e, o=o, cur=cur, lo=lo, sh=sh: e.tensor_tensor(
                            out=o[:, lo:W], in0=cur[:, lo:W], in1=cur[:, lo - sh:W - sh], op=ALU.add),
                            [t_u, th, tt_, tp, pooled_rd if c == 0 else None])
                        cur = o
                        bi ^= 1
                        sh *= 2
                    tp = P.c("dve", lambda e, c=c, cur=cur, w=w: e.scalar_tensor_tensor(
                        out=pooled[:, c, :], in0=cur[:, 16:W], scalar=1.0 / w, in1=ub[:, c, 16:W],
                        op0=ALU.mult, op1=ALU.subtract), [tt_])
                    tq = P.c("dve", lambda e, cur=cur, s=s, g=g: e.tensor_tensor(
                        out=t16[:], in0=cur[:, 16:32], in1=invc_sb[:, s, g, :], op=ALU.mult), [tp])
                    tp = P.c("dve", lambda e, c=c: e.tensor_tensor(
                        out=pooled[:, c, 0:16], in0=t16[:], in1=ub[:, c, 16:32], op=ALU.subtract), [tq])
                ub_rd = [tp]
                t_last_pm = None
                for c in range(8):
                    g, dc = c // 2, c % 2
                    a = mi % 2
                    tm = None
                    for cc in range(2):
                        tm = P.op("pe", lambda e, a=a, g=g, dc=dc, cc=cc: e.matmul(
                            pacc[a][:], wp[:, g * 2 + cc, dc * 128:(dc + 1) * 128], pooled[:, g * 2 + cc, :],
                            start=(cc == 0), stop=(cc == 1)), [tp, t_ldc, pacc_rd[a]],
                            sig=(P.own["pe"] if cc == 1 else None))
                    t_last_pm = tm
                    te = P.c("dve", lambda e, a=a, c=c: e.scalar_tensor_tensor(
                        out=mo[a][:], in0=pacc[a][:], scalar=pscale_sb[:, l, c:c + 1], in1=sgp[:, c, :],
                        op0=ALU.mult, op1=ALU.mult), [tm, t_sg, mo_rd[a]])
                    pacc_rd[a] = te
                    mo_rd[a] = P.dma("sp", lambda e, a=a, c=c, c0=c0: e.dma_start(
                        out=mixT[c * 128:(c + 1) * 128, c0:c0 + TT], in_=mo[a][:]), s_mo[a], [te])
                    sgp_rd = te
                    mi += 1
                pooled_rd = t_last_pm
            if dbg:
                ds_ = P.dsem("dbgk")
                for j in range(2):
                    P.dma("sp", lambda e, j=j: e.dma_start(out=dbg_k[j], in_=kT_g[j]), ds_, [t_cc[-1]])
            P.run(extra_final_waits=[t_cc[-1]])

    def phase_B(i, l):
        with ExitStack() as les:
            P = Phase(nc, les, f"B{i}_{nc.next_id()}")

            def lsb(name, shape, dt):
                return les.enter_context(nc.sbuf_tensor(f"{P.name}_{name}", shape, dt))

            def lps(name, shape, dt):
                return les.enter_context(nc.psum_tensor(f"{P.name}_{name}", shape, dt))

            kA = [[lsb(f"kA{b_}{m}", [128, S], BF16) for m in range(2)] for b_ in range(2)]
            vA = [lsb(f"vA{b_}", [128, 32, 129], BF16) for b_ in range(2)]
            qA = [[lsb(f"qA{s}{m}", [128, TT], BF16) for m in range(2)] for s in range(NS)]
            gA = [lsb(f"gA{j}", [128, 4, 128], BF16) for j in range(2)]
            pT = [lsb(f"pT{j}", [128, 2, TT], BF16) for j in range(2)]
            accs = lsb("accs", [128, 8, 129], F32)
            rinv = lsb("rinv", [128, 8], F32)
            o_sb = lsb("o_sb", [128, 4, 128], F32)
            t_sb = lsb("t_sb", [128, 128], F32)
            junk = lsb("junk", [128, 128], F32)
            ssq = lsb("ssq", [128, 4], F32)
            rs1 = lsb("rs1", [128, 4], F32)
            rs2 = lsb("rs2", [128, 4], F32)
            dt_ = lsb("dt_", [128, 4, 128], BF16)
            mixd = [lsb(f"mixd{j}", [128, TT], BF16) for j in range(2)]
            sc = [lps(f"sc{j}", [128, 2, 512], F32) for j in range(2)]
            accp = [lps(f"accp{j}", [128, 512], F32) for j in range(3)]
            trp = lps("trp", [128, 512], BF16)

            def acc_ap(m, ts):
                idx = m * 4 + ts
                return accp[idx // 3][:, (idx % 3) * 129:(idx % 3) * 129 + 129]

            s_c = P.dsem("const")
            s_k = [P.dsem(f"k{b_}") for b_ in range(2)]
            s_q = [P.dsem(f"q{s}") for s in range(NS)]
            s_g = [P.dsem(f"g{j}") for j in range(2)]
            s_o = [P.dsem(f"o{j}") for j in range(2)]
            for b_ in range(2):
                for m in range(2):
                    P.dma("sp", lambda e, b_=b_, m=m: e.dma_start(out=kA[b_][m][64:128, :], in_=khot_in), s_c)
            for s in range(NS):
                for m in range(2):
                    P.dma("sp", lambda e, s=s, m=m: e.dma_start(out=qA[s][m][64:128, :], in_=qmask_in[:, s * TT:(s + 1) * TT]), s_c)
            t_const = Tok(s_c, s_c.n)
            t_ones = None
            for b_ in range(2):
                t_ones = P.c("pool", lambda e, b_=b_: e.memset(vA[b_][:, :, 128:129], 1.0))

            kv_rd = [None, None]
            q_rd = [None] * NS
            g_rd = [None, None]
            sc_rd = [None, None]
            pT_rd = [None, None]
            stB = dict(acc_rd=None, accs_rd=None, trp_rd=None, dt_rd=None)
            mixd_rd = [None, None]
            kg_v = [kT_g[j].rearrange("(r n) c -> n r c", r=2) for j in range(2)]
            vg_v = [v_g[j].rearrange("(r b p) f -> r p b f", r=2, p=128) for j in range(2)]

            def load_kv(h):
                b_ = h % 2
                for m in range(2):
                    r0 = ((h % 4) * 2 + m) * 64
                    P.dma("sp", lambda e, b_=b_, m=m, r0=r0, h=h: e.dma_start(
                        out=kA[b_][m][0:64, :].rearrange("p (r c) -> p r c", r=2), in_=kg_v[h // 4][r0:r0 + 64, :, :]),
                        s_k[b_], [kv_rd[b_]])
                for r_ in range(2):
                    for hf in range(2):
                        b0 = r_ * 16 + hf * 8
                        P.dma("sp", lambda e, b_=b_, h=h, r_=r_, hf=hf, b0=b0: e.dma_start(
                            out=vA[b_][:, b0:b0 + 8, 0:128], in_=vg_v[hf][r_, :, :, h * 128:(h + 1) * 128]),
                            s_k[b_], [kv_rd[b_]])
                return Tok(s_k[b_], s_k[b_].n)

            iters = []
            for h in range(8):
                for s in range(NS):
                    nblk = 4 * (s + 1)
                    blocks = [r_ * 16 + k_ for r_ in range(2) for k_ in range(nblk)]
                    for bi_, kb in enumerate(blocks):
                        iters.append((h, s, kb, bi_ == 0, bi_ == len(blocks) - 1))
            N = len(iters)
            t_kv = {}
            t_qg = {}
            t_s_tok = {}
            pending = []
            gi_ = [0]

            def emit_loads(h, s):
                if s == 0:
                    if h == 0:
                        t_kv[0] = load_kv(0)
                    if h + 1 < 8:
                        t_kv[h + 1] = None
                c0 = s * TT
                for m in range(2):
                    r0 = (h * 2 + m) * 64
                    P.dma("sp", lambda e, s=s, m=m, r0=r0, c0=c0: e.dma_start(
                        out=qA[s][m][0:64, :], in_=qT[r0:r0 + 64, c0:c0 + TT]), s_q[s], [q_rd[s]])
                t_q = Tok(s_q[s], s_q[s].n)
                gj = gi_[0] % 2
                gi_[0] += 1
                t_g = P.dma("sp", lambda e, gj=gj, c0=c0, h=h: e.dma_start(
                    out=gA[gj][:], in_=gd[c0:c0 + TT, h * 128:(h + 1) * 128].rearrange("(t p) f -> p t f", p=128)),
                    s_g[gj], [g_rd[gj]])
                t_qg[(h, s)] = (t_q, t_g, gj)
                if s == 1 and h + 1 < 8:
                    t_kv[h + 1] = load_kv(h + 1)

            def emit_qk(n):
                h, s, kb, first_kb, last_kb = iters[n]
                if first_kb:
                    emit_loads(h, s)
                b_ = h % 2
                j = n % 2
                t_q = t_qg[(h, s)][0]
                t_s = None
                for m in range(2):
                    t_s = P.op("pe", lambda e, j=j, m=m, b_=b_, kb=kb, s=s: e.matmul(
                        sc[j][:, m, :], kA[b_][m][:, kb * 128:(kb + 1) * 128], qA[s][m][:, :], start=True, stop=True),
                        [t_kv[h], t_q, t_const, sc_rd[j]], sig=(P.own["pe"] if m == 1 else None))
                t_s_tok[n] = t_s
                if last_kb:
                    q_rd[s] = t_s

            def emit_rest(n):
                h, s, kb, first_kb, last_kb = iters[n]
                b_ = h % 2
                j = n % 2
                c0 = s * TT
                t_e = P.c("act", lambda e, j=j: e.activation(out=pT[j][:], in_=sc[j][:], func=AF.Exp, scale=0.125),
                          [t_s_tok[n], pT_rd[j]])
                sc_rd[j] = t_e
                t_pv = None
                for m in range(2):
                    for ts in range(4):
                        idx = m * 4 + ts
                        st_flag = first_kb and (idx % 3 == 0)
                        t_pv = P.op("pe", lambda e, j=j, m=m, ts=ts, b_=b_, kb=kb, st_flag=st_flag, last_kb=last_kb: e.matmul(
                            acc_ap(m, ts), pT[j][:, m, ts * 128:(ts + 1) * 128], vA[b_][:, kb, :],
                            start=st_flag, stop=last_kb, skip_group_check=True),
                            [t_e, t_ones, stB["acc_rd"] if first_kb else None],
                            sig=(P.own["pe"] if idx == 7 else None))
                pT_rd[j] = t_pv
                if not last_kb:
                    return
                if s == NS - 1:
                    kv_rd[b_] = t_pv
                t_q, t_g, gj = t_qg[(h, s)]
                tcp = None
                for bk in range(3):
                    nn = 3 if bk < 2 else 2
                    tcp = P.c("dve", lambda e, bk=bk, nn=nn: e.tensor_copy(
                        out=accs[:, bk * 3:bk * 3 + nn, :].rearrange("p a x -> p (a x)"), in_=accp[bk][:, 0:nn * 129]),
                        [t_pv, stB["accs_rd"]])
                stB["acc_rd"] = tcp
                t1 = P.c("dve", lambda e: e.reciprocal(out=rinv[:], in_=accs[:, :, 128]), [tcp])
                t1 = P.c("dve", lambda e: e.tensor_scalar(out=rinv[:, 4:8], in0=rinv[:, 4:8], scalar1=neglam[:, l:l + 1],
                                                          scalar2=None, op0=ALU.mult), [t1])
                tl_ = t1
                for ts in range(4):
                    ta_ = P.c("dve", lambda e, ts=ts: e.tensor_scalar(out=t_sb[:], in0=accs[:, 4 + ts, 0:128],
                                                                      scalar1=rinv[:, 4 + ts:5 + ts], scalar2=None, op0=ALU.mult),
                              [tl_, stB["dt_rd"] if ts == 0 else None])
                    tb2 = P.c("dve", lambda e, ts=ts: e.scalar_tensor_tensor(
                        out=o_sb[:, ts, :], in0=accs[:, ts, 0:128], scalar=rinv[:, ts:ts + 1], in1=t_sb[:],
                        op0=ALU.mult, op1=ALU.add), [ta_])
                    tl_ = P.c("dve", lambda e, ts=ts: e.scalar_tensor_tensor(
                        out=junk[:], in0=o_sb[:, ts, :], scalar=1.0, in1=o_sb[:, ts, :],
                        op0=ALU.mult, op1=ALU.mult, accum_out=ssq[:, ts:ts + 1]), [tb2])
                stB["accs_rd"] = tl_
                t2 = P.c("dve", lambda e: e.tensor_scalar(out=rs1[:], in0=ssq[:], scalar1=1.0 / 128.0, scalar2=EPS,
                                                          op0=ALU.mult, op1=ALU.add), [tl_])
                t3 = P.c("pool", lambda e: e.tensor_tensor(out=rs2[:], in0=rs1[:], in1=neghalf[:, 0:4], op=ALU.pow), [t2])
                td = None
                for ts in range(4):
                    td = P.c("dve", lambda e, ts=ts, gj=gj: e.scalar_tensor_tensor(
                        out=dt_[:, ts, :], in0=o_sb[:, ts, :], scalar=rs2[:, ts:ts + 1], in1=gA[gj][:, ts, :],
                        op0=ALU.mult, op1=ALU.mult), [t3, t_g, stB["dt_rd"]])
                g_rd[gj] = td
                mj = (h * NS + s) % 2

                def tail(td=td, mj=mj, h=h, c0=c0):
                    ttr = None
                    for ts in range(4):
                        ttr = P.op("pe", lambda e, ts=ts: e.transpose(trp[:, ts * 128:(ts + 1) * 128], dt_[:, ts, :], ident_bf[:]),
                                   [td, stB["trp_rd"]], sig=(P.own["pe"] if ts == 3 else None))
                    stB["dt_rd"] = ttr
                    tev = P.c("dve", lambda e: e.tensor_copy(out=mixd[mj][:], in_=trp[:]), [ttr, mixd_rd[mj]])
                    stB["trp_rd"] = tev
                    mixd_rd[mj] = P.dma("pool", lambda e: e.dma_start(
                        out=mixT[1024 + h * 128:1024 + (h + 1) * 128, c0:c0 + TT], in_=mixd[mj][:]), s_o[mj], [tev])
                pending.append((n + 4, tail))

            caster = Caster(P, lsb, cast_chunks(i, "out") + cast_chunks(i + 1, "in"), ("pool", "dve"))
            emit_qk(0)
            for n in range(N):
                if n + 1 < N:
                    emit_qk(n + 1)
                emit_rest(n)
                if n % 9 == 4:
                    caster.step()
                while pending and pending[0][0] <= n:
                    pending.pop(0)[1]()
            while pending:
                pending.pop(0)[1]()
            caster.finish()
            P.run()

    def phase_C(i, l, x_src, x_dst):
        with ExitStack() as les:
            P = Phase(nc, les, f"C{i}")

            def lsb(name, shape, dt):
                return les.enter_context(nc.sbuf_tensor(f"{P.name}_{name}", shape, dt))

            def lps(name, shape, dt):
                return les.enter_context(nc.psum_tensor(f"{P.name}_{name}", shape, dt))

            wo = lsb("wo", [128, KC, D], BF16)
            mx = [lsb(f"mx{j}", [128, KC, TT], BF16) for j in range(2)]
            yT = lsb("yT", [128, KC, TT], F32)
            ysq = [lsb(f"ysq{j}", [128, TT], BF16) for j in range(2)]
            xt = lsb("xt", [128, KC, TT], F32)
            rpre = lsb("rpre", [128, TT], F32)
            rstd = lsb("rstd", [128, TT], F32)
            tmp = [lsb(f"tmp{j}", [128, TT], F32) for j in range(2)]
            NXO = 4
            xo = [lsb(f"xo{j}", [128, TT], F32) for j in range(NXO)]
            acc = [lps(f"acc{j}", [128, 512], F32) for j in range(3)]
            ssb = lps("ssb", [128, 512], F32)
            s_m = [P.dsem(f"m{j}") for j in range(2)]
            s_x = P.dsem("x")
            s_o = [P.dsem(f"o{j}") for j in range(NXO)]
            wo_v = wout_bf[i].rearrange("(kc p) n -> p kc n", p=128)
            s_w4 = [P.dsem(f"w{j}") for j in range(4)]
            t_w4 = [P.dma("sp", lambda e, j=j: e.dma_start(out=wo[:, :, j * 512:(j + 1) * 512], in_=wo_v[:, :, j * 512:(j + 1) * 512]),
                          s_w4[j]) for j in range(4)]
            mix_v = mixT.rearrange("(kc p) t -> p kc t", p=128)
            x_v = x_src.rearrange("(kc p) t -> p kc t", p=128)
            xd_v = x_dst.rearrange("(kc p) t -> p kc t", p=128)
            mx_rd = [None, None]
            yT_rd = None
            ysq_rd = [None, None]
            acc_rd = [None] * 3
            ssb_rd = None
            xt_rd = None
            rstd_rd = None
            tmp_rd = [None, None]
            xo_rd = [None] * NXO
            ai = 0
            oi = 0
            t_mload = {}
            pend_c = []
            t_ss_box = [None]
            ssb_rd_box = [None]

            def load_m(s):
                if s >= NS or s in t_mload:
                    return
                j = s % 2
                t_mload[s] = P.dma("sp", lambda e, j=j, s=s: e.dma_start(out=mx[j][:], in_=mix_v[:, :, s * TT:(s + 1) * TT]),
                                   s_m[j], [mx_rd[j]])

            load_m(0)
            for s in range(NS):
                c0 = s * TT
                j = s % 2
                load_m(s + 1)
                t_m = t_mload[s]
                t_x = P.dma("sp", lambda e, c0=c0: e.dma_start(out=xt[:], in_=x_v[:, :, c0:c0 + TT]), s_x, [xt_rd])
                t_ss = None
                t_mm = None
                for oc in range(KC):
                    a = ai % 3
                    ai += 1
                    for kc in range(KC):
                        t_mm = P.op("pe", lambda e, a=a, kc=kc, oc=oc, j=j: e.matmul(
                            acc[a][:], wo[:, kc, oc * 128:(oc + 1) * 128], mx[j][:, kc, :], start=(kc == 0), stop=(kc == KC - 1)),
                            [t_w4[oc // 4], t_m, acc_rd[a]], sig=(P.own["pe"] if kc == KC - 1 else None))
                    t_e = P.c("act", lambda e, a=a, oc=oc: e.activation(out=yT[:, oc, :], in_=acc[a][:], func=AF.Copy),
                              [t_mm, yT_rd if oc == 0 else None])
                    jj = oc % 2
                    t_q = P.c("act", lambda e, a=a, jj=jj: e.activation(out=ysq[jj][:], in_=acc[a][:], func=AF.Square),
                              [t_mm, ysq_rd[jj]])
                    acc_rd[a] = t_q
                    def ss_tail(jj=jj, oc=oc, t_q=t_q, s=s):
                        t = P.op("pe", lambda e: e.matmul(ssb[:], ones_bf[:], ysq[jj][:], start=(oc == 0), stop=(oc == KC - 1)),
                                 [t_q, ssb_rd_box[0] if oc == 0 else None], sig=P.own["pe"])
                        ysq_rd[jj] = t
                        t_ss_box[0] = t
                    if pend_c:
                        pend_c.pop(0)()
                    pend_c.append(ss_tail)
                while pend_c:
                    pend_c.pop(0)()
                t_ss = t_ss_box[0]
                mx_rd[j] = t_mm
                t_rp = P.c("dve", lambda e: e.tensor_scalar(out=rpre[:], in0=ssb[:], scalar1=1.0 / D, scalar2=EPS,
                                                            op0=ALU.mult, op1=ALU.add), [t_ss, rstd_rd])
                ssb_rd_box[0] = t_rp
                t_ln = P.c("act", lambda e: e.activation(out=rpre[:], in_=rpre[:], func=AF.Ln), [t_rp])
                t_rs = P.c("act", lambda e: e.activation(out=rstd[:], in_=rpre[:], func=AF.Exp, scale=-0.5), [t_ln, rstd_rd])
                t_o = None
                for oc in range(KC):
                    jj = oi % 2
                    jx = oi % NXO
                    oi += 1
                    t_a = P.c("dve", lambda e, oc=oc, jj=jj: e.tensor_tensor(out=tmp[jj][:], in0=yT[:, oc, :], in1=rstd[:], op=ALU.mult),
                              [t_rs, t_e, tmp_rd[jj]])
                    t_b = P.c("dve", lambda e, oc=oc, jj=jj, jx=jx: e.scalar_tensor_tensor(
                        out=xo[jx][:], in0=tmp[jj][:], scalar=G2_all[:, l, oc:oc + 1], in1=xt[:, oc, :],
                        op0=ALU.mult, op1=ALU.add), [t_a, t_x, xo_rd[jx]])
                    tmp_rd[jj] = t_b
                    xo_rd[jx] = P.dma("pool" if oi % 2 else "sp", lambda e, oc=oc, jx=jx, c0=c0: e.dma_start(
                        out=xd_v[:, oc, c0:c0 + TT], in_=xo[jx][:]), s_o[jx], [t_b])
                    t_o = t_b
                yT_rd = t_o
                xt_rd = t_o
                rstd_rd = t_o
            P.run()

    prologue()
    for i, l in enumerate(layers):
        if stop == "pro":
            break
        x_src = xT_in if i == 0 else oT
        x_dst = oT
        phase_A(i, l, x_src)
        if stop == "A":
            break
        phase_A2(i, l)
        if stop == "P":
            break
        phase_B(i, l)
        for _ in range(dup_B):
            phase_B(i, l)
        if stop == "B":
            break
        phase_C(i, l, x_src, x_dst)
    es.close()
    return nc


def _bf(a):
    return np.asarray(a, dtype=np.float32).astype(ml_dtypes.bfloat16)


def _tok_idx(r):
    return np.concatenate([np.arange(t * TT, (t + 1) * TT) for t in TILES[r]])


def _static_tables():
    half = 32
    i = np.arange(128) % 64
    jf = (i % 32).astype(np.float32)
    inv = (np.float32(1.0) / (np.float32(10000.0) ** (np.arange(0, 64, 2, dtype=np.float32) / np.float32(64)))).astype(np.float32)
    invf = inv[(i % 32)].reshape(128, 1).astype(np.float32)
    sgn = np.where(i < half, -1.0, 1.0).astype(np.float32).reshape(128, 1)
    perm = np.zeros((128, 128), np.float32)
    for m in range(128):
        base = (m // 64) * 64
        im = m % 64
        partner = base + ((im + 32) % 64)
        perm[partner, m] = 1.0
    kchunk = np.concatenate([_tok_idx(0), _tok_idx(1)]) // 64
    khot = (np.arange(64)[:, None] == kchunk[None, :]).astype(np.float32)
    return invf, sgn, perm, khot


def _per_rank_tables(r):
    tok = _tok_idx(r)
    qchunk = tok // 64
    qmask = np.where(np.arange(64)[:, None] > qchunk[None, :], NEG, 0.0).astype(np.float32)
    hw = np.zeros((NS, 2, NS), np.float32)
    for s in range(NS):
        t = TILES[r][s]
        if t == 0:
            continue
        for r_ in range(2):
            if (t - 1) in TILES[r_]:
                hw[s, r_, TILES[r_].index(t - 1)] = 1.0
    invc = np.zeros((NS, 4, 16), np.float32)
    for s in range(NS):
        t0 = TILES[r][s] * TT
        for g, w in enumerate(WINS):
            cnt = np.minimum(t0 + np.arange(16) + 1, w).astype(np.float32)
            invc[s, g] = np.float32(1.0) / cnt
    return qmask, hw.reshape(-1), invc.reshape(-1)


def _rep(v):
    return np.ascontiguousarray(np.broadcast_to(np.asarray(v, np.float32).reshape(1, -1), (128, np.asarray(v).size)))


def _colT(a, nchunk):
    a = np.asarray(a, np.float32)
    L = a.shape[0]
    return np.ascontiguousarray(a.reshape(L, nchunk, 128).transpose(2, 0, 1).reshape(128, L * nchunk))


def _prepare(x, c, positions, w_ada, b_ada, g_pre, w_in, w_pool, pool_scale,
             lambda_q1, lambda_k1, lambda_q2, lambda_k2, subln_g, w_out, g_post):
    invf, sgn, perm, khot = _static_tables()
    c = np.asarray(c, np.float32)
    shared = dict(
        cT=np.ascontiguousarray(c.reshape(B, KC, 128).transpose(2, 1, 0).reshape(128, KC * B)),
        invf=invf, sgn=sgn, perm=_bf(perm), ones=_bf(np.ones((128, 128))), ident=_bf(np.eye(128)),
        khot=_bf(khot),
    )
    per_layer = dict(
        b_ada=np.asarray(b_ada, np.float32), g_pre=np.asarray(g_pre, np.float32),
        g_post=np.asarray(g_post, np.float32), pool_scale=np.asarray(pool_scale, np.float32),
        w_in=np.asarray(w_in, np.float32), w_out=np.asarray(w_out, np.float32),
        w_pool=np.asarray(w_pool, np.float32).reshape(DEPTH, 1024, 256),
        lq1=np.asarray(lambda_q1, np.float32), lk1=np.asarray(lambda_k1, np.float32),
        lq2=np.asarray(lambda_q2, np.float32), lk2=np.asarray(lambda_k2, np.float32),
        subln=np.asarray(subln_g, np.float32), w_ada=np.asarray(w_ada, np.float32),
    )
    per_core = []
    xT0 = []
    x = np.asarray(x, np.float32)
    positions = np.asarray(positions)
    for core in range(8):
        b, r = core // 2, core % 2
        tok = _tok_idx(r)
        qmask, hw, invc = _per_rank_tables(r)
        bsel = np.zeros((128, B), np.float32)
        bsel[:, b] = 1.0
        per_core.append(dict(
            bsel=bsel,
            posr=np.ascontiguousarray(np.broadcast_to(positions[b, tok].astype(np.int32)[None, :], (128, TO))),
            qmask=_bf(qmask), hw=_rep(hw), invc16=_rep(invc),
        ))
        xT0.append(np.ascontiguousarray(x[b, tok, :].T))
    return shared, per_layer, per_core, xT0


def _layer_maps(shared, per_layer, per_core, xT, layers):
    L = list(layers)
    pl = per_layer
    lamc = np.array([[-(0.8 - 0.6 * math.exp(-0.3 * l)), 1.0 - (0.8 - 0.6 * math.exp(-0.3 * l))] for l in L], np.float32)
    lay = dict(
        b_adaT=_colT(pl["b_ada"][L], 48), g_preT=_colT(pl["g_pre"][L], KC), g_postT=_colT(pl["g_post"][L], KC),
        pool_scaleT=_colT(pl["pool_scale"][L], 8),
        w_in=np.ascontiguousarray(pl["w_in"][L]), w_out=np.ascontiguousarray(pl["w_out"][L]),
        w_pool=np.ascontiguousarray(pl["w_pool"][L]),
        lq1r=_rep(pl["lq1"][L]), lk1r=_rep(pl["lk1"][L]), lq2r=_rep(pl["lq2"][L]), lk2r=_rep(pl["lk2"][L]),
        sublnr=_rep(pl["subln"][L]), lamc=_rep(lamc),
    )
    maps = []
    for core in range(8):
        m = dict(shared)
        m.update(lay)
        m.update(per_core[core])
        m["w_ada_sl"] = np.ascontiguousarray(pl["w_ada"][L][:, :, core * 768:(core + 1) * 768])
        m["xT"] = xT[core]
        maps.append(m)
    return maps


_PROG_CACHE = {}


def _get_prog(nl, dbg=False):
    key = (nl, dbg)
    if key not in _PROG_CACHE:
        _PROG_CACHE[key] = build_program(list(range(nl)), True, True, dbg=dbg)
    return _PROG_CACHE[key]


LAYERS_PER_LAUNCH = 4


def kernel(x, c, positions, w_ada, b_ada, g_pre, w_in, w_pool, pool_scale,
           lambda_q1, lambda_k1, lambda_q2, lambda_k2, subln_g, w_out, g_post):
    shared, per_layer, per_core, xT = _prepare(x, c, positions, w_ada, b_ada, g_pre, w_in, w_pool, pool_scale,
                                               lambda_q1, lambda_k1, lambda_q2, lambda_k2, subln_g, w_out, g_post)
    nl = LAYERS_PER_LAUNCH
    nc = _get_prog(nl)
    for l0 in range(0, DEPTH, nl):
        maps = _layer_maps(shared, per_layer, per_core, xT, range(l0, l0 + nl))
        res = run_bass_kernel_spmd(nc, maps, core_ids=list(range(8)))
        xT = [np.ascontiguousarray(res.results[core]["oT"]) for core in range(8)]
    out = np.empty((B, S, D), np.float32)
    for core in range(8):
        b, r = core // 2, core % 2
        out[b, _tok_idx(r), :] = xT[core].T
    return out
```
